# Optimizing a Trainium2 kernel written in Bass

```python
import jax, jax.numpy as jnp
from jax import lax
import numpy as np

D_MODEL = 2048
BATCH = 4
SEQ = 4096
DEPTH = 2

N_A_LAYERS = DEPTH // 2
N_B_LAYERS = DEPTH - N_A_LAYERS
CONV_WIDTH = 31
N_HEADS = 16
HEAD_DIM = D_MODEL // N_HEADS
D_FF = -(-(8 * D_MODEL) // (3 * 256)) * 256
BRANCHES = ((128, 1), (512, 4), (2048, 16))
BLOCK = 128
RMS_EPS = 1e-6
LN_EPS = 1e-5

kernel_name = "yoco_conformer_dilated_alibi"


def _rmsnorm(x, g):
    xf = x.astype(jnp.float32)
    y = xf * lax.rsqrt(jnp.mean(xf * xf, axis=-1, keepdims=True) + RMS_EPS)
    return (y * g.astype(jnp.float32)).astype(x.dtype)


def _layernorm(x, g, b):
    xf = x.astype(jnp.float32)
    mu = jnp.mean(xf, axis=-1, keepdims=True)
    var = jnp.mean(jnp.square(xf - mu), axis=-1, keepdims=True)
    y = (xf - mu) * lax.rsqrt(var + LN_EPS) * g.astype(jnp.float32) + b.astype(jnp.float32)
    return y.astype(x.dtype)


def _swiglu(h, w_gate, w_up, w_down):
    return (jax.nn.silu(h @ w_gate) * (h @ w_up)) @ w_down


def _conformer_conv(h, w1, b1, dw, dw_b, ln_g, ln_b, w2, b2):
    u = h @ w1 + b1
    a, gate = jnp.split(u, 2, axis=-1)
    u = a * jax.nn.sigmoid(gate)
    u = lax.conv_general_dilated(
        u, dw[:, None, :].astype(u.dtype), window_strides=(1,),
        padding=[(CONV_WIDTH - 1, 0)],
        dimension_numbers=("NWC", "WIO", "NWC"),
        feature_group_count=u.shape[-1]) + dw_b
    u = jax.nn.silu(_layernorm(u, ln_g, ln_b))
    return u @ w2 + b2


def _alibi_slopes():
    h = jnp.arange(1, N_HEADS + 1, dtype=jnp.float32)
    return jnp.exp2(-8.0 * h / N_HEADS)


def _residue_major(t, d):
    b, s, h, e = t.shape
    L = s // d
    return t.reshape(b, L, d, h, e).transpose(0, 2, 1, 3, 4), L


def _shared_branch_kv(k, v):
    out = []
    for window, d in BRANCHES:
        n_off = window // d
        kr, L = _residue_major(k, d)
        vr, _ = _residue_major(v, d)
        nb = -(-L // BLOCK)
        pad = ((0, 0), (0, 0), (n_off, nb * BLOCK - L), (0, 0), (0, 0))
        idx = jnp.arange(nb)[:, None] * BLOCK + jnp.arange(BLOCK + n_off)[None, :]
        out.append((jnp.pad(kr, pad)[:, :, idx], jnp.pad(vr, pad)[:, :, idx]))
    return out


def _dilated_attention(q, branch_kv):
    b, s, h, e = q.shape
    slopes = _alibi_slopes()
    scale = HEAD_DIM ** -0.5
    outs, lses = [], []
    for (window, d), (k_win, v_win) in zip(BRANCHES, branch_kv):
        n_off = window // d
        qr, L = _residue_major(q, d)
        nb = -(-L // BLOCK)
        qr = jnp.pad(qr, ((0, 0), (0, 0), (0, nb * BLOCK - L), (0, 0), (0, 0)))
        qr = qr.reshape(b, d, nb, BLOCK, h, e)
        scores = jnp.einsum("brnqhe,brnkhe->brnhqk", qr, k_win,
                            preferred_element_type=jnp.float32) * scale
        qi = jnp.arange(BLOCK)[:, None]
        kk = jnp.arange(BLOCK + n_off)[None, :]
        j = qi - kk + n_off
        key_idx = jnp.arange(nb)[:, None, None] * BLOCK + kk[None] - n_off
        valid = (j >= 0) & (j <= n_off) & (key_idx >= 0)
        bias = -slopes[:, None, None] * (d * j).astype(jnp.float32)
        logits = jnp.where(valid[None, None, :, None], scores + bias[None, None, None], -jnp.inf)
        m = jnp.max(logits, axis=-1, keepdims=True)
        p = jnp.exp(logits - m)
        den = jnp.sum(p, axis=-1, keepdims=True)
        o = jnp.einsum("brnhqk,brnkhe->brnqhe", p / den, v_win.astype(jnp.float32))
        lse = (m + jnp.log(den))[..., 0].transpose(0, 1, 2, 4, 3)
        o = o.reshape(b, d, nb * BLOCK, h, e)[:, :, :L].transpose(0, 2, 1, 3, 4).reshape(b, s, h, e)
        lse = lse.reshape(b, d, nb * BLOCK, h)[:, :, :L].transpose(0, 2, 1, 3).reshape(b, s, h)
        outs.append(o)
        lses.append(lse)
    w = jax.nn.softmax(jnp.stack(lses, axis=0), axis=0)
    return jnp.sum(w[..., None] * jnp.stack(outs, axis=0), axis=0)


def setup_inputs(seed: int = 0) -> dict:
    key = jax.random.key(seed)
    ks = jax.random.split(key, 24)
    D, F, W = D_MODEL, D_FF, CONV_WIDTH

    def dense(k, shape, fan_in):
        return jax.random.normal(k, shape, jnp.float32) * (fan_in ** -0.5)

    def gain(k, shape):
        return 1.0 + 0.02 * jax.random.normal(k, shape, jnp.float32)

    def bias(k, shape):
        return 0.02 * jax.random.normal(k, shape, jnp.float32)

    return {
        "x": jax.random.normal(ks[0], (BATCH, SEQ, D), jnp.float32),
        "a_norm_g": gain(ks[1], (N_A_LAYERS, D)),
        "conv_w1": dense(ks[2], (N_A_LAYERS, D, 2 * D), D),
        "conv_b1": bias(ks[3], (N_A_LAYERS, 2 * D)),
        "conv_dw": dense(ks[4], (N_A_LAYERS, W, D), W),
        "conv_dw_b": bias(ks[5], (N_A_LAYERS, D)),
        "conv_ln_g": gain(ks[6], (N_A_LAYERS, D)),
        "conv_ln_b": bias(ks[7], (N_A_LAYERS, D)),
        "conv_w2": dense(ks[8], (N_A_LAYERS, D, D), D),
        "conv_b2": bias(ks[9], (N_A_LAYERS, D)),
        "kv_norm_g": gain(ks[10], (D,)),
        "w_k": dense(ks[11], (D, D), D),
        "w_v": dense(ks[12], (D, D), D),
        "b_norm_g": gain(ks[13], (N_B_LAYERS, D)),
        "w_q": dense(ks[14], (N_B_LAYERS, D, D), D),
        "w_o": dense(ks[15], (N_B_LAYERS, D, D), D),
        "ffn_norm_g": gain(ks[16], (DEPTH, D)),
        "ffn_w_gate": dense(ks[17], (DEPTH, D, F), D),
        "ffn_w_up": dense(ks[18], (DEPTH, D, F), D),
        "ffn_w_down": dense(ks[19], (DEPTH, F, D), F),
        "final_norm_g": gain(ks[20], (D,)),
    }


def reference(x, a_norm_g, conv_w1, conv_b1, conv_dw, conv_dw_b, conv_ln_g, conv_ln_b,
              conv_w2, conv_b2, kv_norm_g, w_k, w_v, b_norm_g, w_q, w_o,
              ffn_norm_g, ffn_w_gate, ffn_w_up, ffn_w_down, final_norm_g):
    b, s, _ = x.shape
    h = x
    shared_kv = None
    for layer in range(DEPTH):
        if layer < N_A_LAYERS:
            a = layer
            h = h + _conformer_conv(_rmsnorm(h, a_norm_g[a]), conv_w1[a], conv_b1[a],
                                    conv_dw[a], conv_dw_b[a], conv_ln_g[a], conv_ln_b[a],
                                    conv_w2[a], conv_b2[a])
        else:
            if layer == N_A_LAYERS:
                kv_in = _rmsnorm(h, kv_norm_g)
                k = (kv_in @ w_k).reshape(b, s, N_HEADS, HEAD_DIM)
                v = (kv_in @ w_v).reshape(b, s, N_HEADS, HEAD_DIM)
                shared_kv = _shared_branch_kv(k, v)
            i = layer - N_A_LAYERS
            q = (_rmsnorm(h, b_norm_g[i]) @ w_q[i]).reshape(b, s, N_HEADS, HEAD_DIM)
            att = _dilated_attention(q, shared_kv).astype(h.dtype).reshape(b, s, D_MODEL)
            h = h + att @ w_o[i]
        h = h + _swiglu(_rmsnorm(h, ffn_norm_g[layer]), ffn_w_gate[layer],
                        ffn_w_up[layer], ffn_w_down[layer])
    return _rmsnorm(h, final_norm_g)
```

```python
import contextlib
import numpy as np
import ml_dtypes
import concourse.bass as bass
import concourse.mybir as mybir
from concourse.bass_utils import run_bass_kernel_spmd

F32 = mybir.dt.float32
BF16 = mybir.dt.bfloat16
AF = mybir.ActivationFunctionType
ALU = mybir.AluOpType

D = 2048
FF = 5632
NF = FF // 128
SEQ = 4096
NTOK = 2048
TT = 512
NTILE = NTOK // TT
H = 16
CW = 31
HALO = 32
RMS_EPS = 1e-6
LN_EPS = 1e-5
BIG = 30000.0
QSCALE = 128.0 ** -0.5

ENGS = ("pe", "act", "dve", "pool", "sp")
import os
DEBUG_NOCC = bool(os.environ.get("KDEBUG_NOCC"))
NO_PRECONV = False


class Op:
    __slots__ = ("eng", "idx", "fn", "deps", "signal", "dma_key", "dma_cnt", "cum", "inc")

    def __init__(self, eng, idx, fn):
        self.eng = eng
        self.idx = idx
        self.fn = fn
        self.deps = []
        self.signal = False
        self.dma_key = None
        self.dma_cnt = 0
        self.cum = 0
        self.inc = 16


class Prog:
    def __init__(self, nc):
        self.nc = nc
        self.ops = {e: [] for e in ENGS}
        self.last_w = {}
        self.readers = {}
        self.dma_cnt = {}
        self.dma_inc = {}

    def op(self, eng, fn, reads=(), writes=(), dma=None, inc=16):
        o = Op(eng, len(self.ops[eng]), fn)
        if dma is not None:
            o.dma_key = dma
            o.inc = inc
            self.dma_inc[dma] = inc
            self.dma_cnt[dma] = self.dma_cnt.get(dma, 0) + 1
            o.dma_cnt = self.dma_cnt[dma]
        deps = []
        for k in reads:
            w = self.last_w.get(k)
            if w is not None:
                deps.append(w)
        for k in writes:
            w = self.last_w.get(k)
            if w is not None:
                deps.append(w)
            for r in self.readers.get(k, {}).values():
                deps.append(r)
        seen = set()
        for d in deps:
            if d is o or id(d) in seen:
                continue
            seen.add(id(d))
            if d.dma_key is None and d.eng == eng:
                if eng in ("pe", "sp"):
                    continue
                if o.dma_key is None and o.idx - d.idx > 3:
                    continue
            o.deps.append(d)
            if d.dma_key is None:
                d.signal = True
        for k in reads:
            self.readers.setdefault(k, {})[(eng, o.dma_key)] = o
        for k in writes:
            self.last_w[k] = o
            self.readers[k] = {}
        self.ops[eng].append(o)
        return o

    def fence(self):
        lasts = []
        for e in ENGS:
            for o in reversed(self.ops[e]):
                if o.dma_key is None and o.fn is not None:
                    lasts.append(o)
                    break
        dmas = []
        for k, c in self.dma_cnt.items():
            p = Op("sp", -1, None)
            p.dma_key = k
            p.dma_cnt = c
            p.inc = self.dma_inc[k]
            dmas.append(p)
        for e in ENGS:
            o = Op(e, len(self.ops[e]), None)
            for d in lasts:
                if d.eng != e:
                    o.deps.append(d)
                    d.signal = True
            o.deps.extend(dmas)
            self.ops[e].append(o)
        self.last_w = {}
        self.readers = {}

    def emit(self):
        nc = self.nc
        with contextlib.ExitStack() as st:
            esem = {e: st.enter_context(nc.semaphore("s_" + e)) for e in ENGS}
            dsem = {}
            for n, k in enumerate(self.dma_cnt):
                dsem[k] = st.enter_context(nc.semaphore("d%d" % n))
            block = st.enter_context(nc.Block())
            for e in ENGS:
                c = 0
                for o in self.ops[e]:
                    if o.signal:
                        c += 1
                    o.cum = c

            def run(e, engine):
                waited = {}
                for o in self.ops[e]:
                    for d in o.deps:
                        if d.dma_key is not None:
                            s, v, key = dsem[d.dma_key], d.inc * d.dma_cnt, ("d", d.dma_key)
                        else:
                            s, v, key = esem[d.eng], d.cum, ("e", d.eng)
                        if waited.get(key, 0) >= v:
                            continue
                        waited[key] = v
                        engine.wait_ge(s, v)
                    if o.fn is None:
                        continue
                    ins = o.fn(engine)
                    if o.dma_key is not None:
                        ins.then_inc(dsem[o.dma_key], o.inc)
                    elif o.signal:
                        ins.then_inc(esem[e], 1)
                if e == "sp":
                    for k, c in self.dma_cnt.items():
                        if waited.get(("d", k), 0) < self.dma_inc[k] * c:
                            engine.wait_ge(dsem[k], self.dma_inc[k] * c)

            @block.tensor
            def _(eng):
                run("pe", eng)

            @block.scalar
            def _(eng):
                run("act", eng)

            @block.vector
            def _(eng):
                run("dve", eng)

            @block.gpsimd
            def _(eng):
                run("pool", eng)

            @block.sync
            def _(eng):
                run("sp", eng)


VEC_COLS = {}
_c = 0
for _n, _w in (("a_g", 16), ("b1", 32), ("dwb", 16), ("lng", 16), ("lnb", 16), ("b2", 16),
               ("kv_g", 16), ("b_g", 16), ("f_g0", 16), ("f_g1", 16), ("fin_g", 16),
               ("dw", 16 * CW), ("hm", 1), ("pm", 1)):
    VEC_COLS[_n] = (_c, _w)
    _c += _w
NV = _c


def _pack(v):
    return np.ascontiguousarray(np.asarray(v, np.float32).reshape(-1, 128).T)


def make_vecs(inp, half):
    vecs = np.zeros((128, NV), np.float32)

    def put(name, arr):
        s, w = VEC_COLS[name]
        vecs[:, s:s + w] = arr

    put("a_g", _pack(inp["a_norm_g"][0]))
    put("b1", _pack(inp["conv_b1"][0]))
    put("dwb", _pack(inp["conv_dw_b"][0]))
    put("lng", _pack(inp["conv_ln_g"][0]))
    put("lnb", _pack(inp["conv_ln_b"][0]))
    put("b2", _pack(inp["conv_b2"][0]))
    put("kv_g", _pack(inp["kv_norm_g"]))
    put("b_g", _pack(inp["b_norm_g"][0]))
    put("f_g0", _pack(inp["ffn_norm_g"][0]))
    put("f_g1", _pack(inp["ffn_norm_g"][1]))
    put("fin_g", _pack(inp["final_norm_g"]))
    dw = np.asarray(inp["conv_dw"][0], np.float32)
    dwp = dw.T.reshape(16, 128, CW).transpose(1, 0, 2).reshape(128, 16 * CW)
    put("dw", dwp)
    put("hm", np.full((128, 1), 1.0 if half == 1 else 0.0, np.float32))
    put("pm", np.full((128, 1), 0.0 if half == 1 else BIG, np.float32))
    return vecs


def make_consts():
    k = np.arange(128)[:, None]
    q = np.arange(128)[None, :]
    t0 = np.where(k >= q, q - k + 128, BIG).astype(np.float32)
    t1 = np.where(k <= q, q - k, BIG).astype(np.float32)
    tc = np.zeros((128, 2, 512), np.float32)
    tc[:, 0, :] = np.tile(t0, (1, 4))
    tc[:, 1, :] = np.tile(t1, (1, 4))
    return np.eye(128, dtype=np.float32), tc


class Builder:
    def __init__(self, phase):
        self.phase = phase
        nc = self.nc = bass.Bass("TRN2", target_bir_lowering=False)
        self.P = Prog(nc)
        self.nbank = 0
        self.nw = 0
        din = lambda n, s, dt=F32: nc.dram_tensor(n, s, dt, kind="ExternalInput").ap()
        dout = lambda n, s, dt=F32: nc.dram_tensor(n, s, dt, kind="ExternalOutput").ap()
        dint = lambda n, s, dt=F32: nc.dram_tensor(n, s, dt, kind="Internal").ap()
        self.vecs_d = din("vecs", [128, NV])
        self.ident_d = din("ident", [128, 128])
        hasA = "A" in phase
        hasB = "B" in phase
        if hasA:
            self.x_d = din("x", [NTOK, D])
            self.xh_d = din("xh", [HALO, D])
            self.w1_d = din("conv_w1", [D, 2 * D])
            self.w2_d = din("conv_w2", [D, D])
            self.wk_d = din("w_k", [D, D])
            self.wv_d = din("w_v", [D, D])
            self.g0_d = din("gate0", [D, FF])
            self.u0_d = din("up0", [D, FF])
            self.d0_d = din("down0", [FF, D])
        if hasB:
            self.tc_d = din("tconst", [128, 2, 512])
            self.wq_d = din("w_q", [D, D])
            self.wo_d = din("w_o", [D, D])
            self.g1_d = din("gate1", [D, FF])
            self.u1_d = din("up1", [D, FF])
            self.d1_d = din("down1", [FF, D])
            self.out_d = dout("out", [NTOK, D])
            self.qT_d = dint("qT", [H, 128, NTOK], BF16)
            self.attT_d = dint("attT", [H, 128, NTOK], BF16)
        KSH = [NTILE, H * 128, TT]
        assert phase == "AB", "only the fused program is supported"
        self.h1T_d = dint("h1T", [16, 128, NTOK])
        self.wscr = dint("wscr", [96, 128, 16 * 512], BF16)
        self.wmap = {}
        self.preconv = False
        self.preq = []
        self.wstored = set()
        self.defer_store = False
        self.k_own = dint("k_own", KSH, BF16)
        self.v_own = dint("v_own", KSH, BF16)
        self.k_all = dint("k_all", [NTILE, 2 * H * 128, TT], BF16)
        self.v_all = dint("v_all", [NTILE, 2 * H * 128, TT], BF16)
        self.k_prev = self.k_all[:, 0:H * 128, :]
        self.v_prev = self.v_all[:, 0:H * 128, :]
        self.alloc()

    def alloc(self):
        nc = self.nc
        self.ps = [nc.alloc_psum_tensor("ps%d" % i, [128, 512], F32) for i in range(8)]
        off = [0]
        NB = 101 * 1024
        arena = self.arena = nc.alloc_sbuf_tensor("arena", [128, NB], BF16)

        def carve(nbytes, dt=BF16):
            n = nbytes // 2
            a = arena[:, off[0]:off[0] + n]
            off[0] += n
            assert off[0] <= NB, off[0]
            return a.bitcast(F32) if dt == F32 else a

        self.carve = carve
        self.vecs = nc.alloc_sbuf_tensor("vecs_sb", [128, NV], F32)
        self.ident = nc.alloc_sbuf_tensor("ident_sb", [128, 128], F32)
        self.identb = nc.alloc_sbuf_tensor("identb", [128, 128], BF16)
        self.onesb = nc.alloc_sbuf_tensor("onesb", [128, 128], BF16)
        self.onesf = nc.alloc_sbuf_tensor("onesf", [128, 128], F32)
        self.eps_rms = nc.alloc_sbuf_tensor("eps_rms", [128, 1], F32)
        self.eps_ln = nc.alloc_sbuf_tensor("eps_ln", [128, 1], F32)
        self.m0 = off[0]
        self.hT = carve(16 * 512 * 4, F32).rearrange("p (c t) -> p c t", c=16)
        self.xn = carve(16 * 512 * 2).rearrange("p (c t) -> p c t", c=16)
        u0 = off[0]
        self.u = carve(16 * 544 * 2).rearrange("p (c t) -> p c t", c=16)
        self.v = carve(16 * 512 * 4, F32).rearrange("p (c t) -> p c t", c=16)
        u1 = off[0]
        self.act = arena[:, u0:u0 + NF * 512].rearrange("p (c t) -> p c t", c=NF)
        assert u0 + NF * 512 <= u1
        s0 = off[0]
        stage = carve(16 * 512 * 2)
        self.stage = stage
        self.st16 = stage.rearrange("p (c t) -> p c t", c=16)
        self.vst = stage.rearrange("p (b f) -> p b f", b=4)
        self.xs = [arena[:, s0 + i * 4096:s0 + (i + 1) * 4096].bitcast(F32) for i in range(2)]
        self.wbuf = [carve(16 * 512 * 2).rearrange("p (k n) -> p k n", k=16) for _ in range(3)]
        self.dg = [carve(CW * 128 * 2).rearrange("p (j m) -> p j m", j=CW) for _ in range(2)]
        self.sq = [carve(512 * 2) for _ in range(4)]
        self.tmp = [carve(512 * 4, F32) for _ in range(3)]
        self.rstd = carve(512 * 4, F32)
        self.lnA = carve(512 * 4, F32)
        self.lnB = carve(512 * 4, F32)
        self.mean = carve(512 * 4, F32)
        self.hTh = carve(16 * HALO * 4, F32).rearrange("p (c t) -> p c t", c=16)
        self.xnh = carve(16 * HALO * 2).rearrange("p (c t) -> p c t", c=16)
        self.uhs = carve(16 * HALO * 2).rearrange("p (c t) -> p c t", c=16)
        self.m1 = off[0]
        if "B" in self.phase:
            off[0] = self.m0
            self.kTs = [carve(2 * NTOK * 2) for _ in range(2)]
            self.qTs = [carve(NTOK * 2) for _ in range(2)]
            self.V1 = [carve(17 * 128 * 2).rearrange("p (t e) -> p t e", e=128) for _ in range(2)]
            self.V4 = [carve(20 * 128 * 2).rearrange("p (r t e) -> p r t e", r=4, e=128) for _ in range(2)]
            self.V16 = [carve(32 * 128 * 2).rearrange("p (r t e) -> p r t e", r=16, e=128) for _ in range(2)]
            self.vTs = [carve(2 * NTOK * 2) for _ in range(2)]
            self.accn = [carve(NTOK * 4, F32) for _ in range(3)]
            self.accd = [carve(NTOK * 4, F32) for _ in range(3)]
            self.nsum = carve(NTOK * 4, F32)
            self.dsum = carve(NTOK * 4, F32)
            self.rden = carve(NTOK * 4, F32)
            self.atto = [carve(NTOK * 2) for _ in range(2)]
            self.sc = [[carve(512 * 4, F32) for _ in range(2)] for _ in range(2)]
            self.pT = [[carve(512 * 2) for _ in range(2)] for _ in range(2)]
            self.TA = carve(512 * 4, F32)
            self.TB = carve(512 * 4, F32)
            self.TD = carve(512 * 4, F32)
            self.TC = carve(512 * 4, F32)
            assert off[0] <= self.m1, (off[0], self.m1)
            off[0] = self.m1

    def vcol(self, name, c=0, w=1):
        s, _ = VEC_COLS[name]
        return self.vecs[:, s + c:s + c + w]

    def bank(self):
        b = self.nbank % 6
        self.nbank += 1
        return b

    def mm(self, out, lhsT, rhs, start, stop, reads, writes):
        self.P.op("pe", lambda e: e.matmul(out, lhsT=lhsT, rhs=rhs, start=start, stop=stop),
                  reads=reads, writes=writes)

    def mm_items(self, items, src, skey, n, dc_outer):
        if dc_outer:
            for dc in range(16):
                for p, b, cs in items:
                    self.mm(self.ps[p][:, 0:n], self.wbuf[b][:, dc, cs], src(dc), dc == 0, dc == 15,
                            [("w", b), skey(dc)], [("ps", p)])
        else:
            for p, b, cs in items:
                for dc in range(16):
                    self.mm(self.ps[p][:, 0:n], self.wbuf[b][:, dc, cs], src(dc), dc == 0, dc == 15,
                            [("w", b), skey(dc)], [("ps", p)])

    def tr(self, out, in_, ident, reads, writes):
        self.P.op("pe", lambda e: e.transpose(out=out, in_=in_, identity=ident), reads=reads, writes=writes)

    def actv(self, out, in_, func, reads, writes, bias=0.0, scale=1.0):
        self.P.op("act", lambda e: e.activation(out=out, in_=in_, func=func, bias=bias, scale=scale),
                  reads=reads, writes=writes)

    def stt(self, out, in0, scalar, in1, op0, op1, reads, writes, eng="dve"):
        self.P.op(eng, lambda e: e.scalar_tensor_tensor(out=out, in0=in0, scalar=scalar, in1=in1, op0=op0, op1=op1),
                  reads=reads, writes=writes)

    def tt(self, out, in0, in1, op, reads, writes, eng="dve"):
        self.P.op(eng, lambda e: e.tensor_tensor(out=out, in0=in0, in1=in1, op=op), reads=reads, writes=writes)

    def ts(self, out, in0, s1, s2, op0, op1, reads, writes, eng="dve"):
        if s2 is None:
            self.P.op(eng, lambda e: e.tensor_scalar(out=out, in0=in0, scalar1=s1, scalar2=None, op0=op0),
                      reads=reads, writes=writes)
        else:
            self.P.op(eng, lambda e: e.tensor_scalar(out=out, in0=in0, scalar1=s1, scalar2=s2, op0=op0, op1=op1),
                      reads=reads, writes=writes)

    def cp(self, out, in_, reads, writes, eng="dve"):
        if eng == "act":
            self.P.op("act", lambda e: e.copy(out=out, in_=in_), reads=reads, writes=writes)
        else:
            self.P.op(eng, lambda e: e.tensor_copy(out=out, in_=in_), reads=reads, writes=writes)

    def dma(self, out, in_, reads, writes, key, eng="sp"):
        return self.P.op(eng, lambda e: e.dma_start(out=out, in_=in_), reads=reads, writes=writes, dma=key)

    def wload(self, src, nk=16, wid=None):
        b = self.nw % 3
        self.nw += 1
        dst = self.wbuf[b][:, 0:nk, :]
        if wid not in self.wmap:
            self.wmap[wid] = len(self.wmap)
        idx = self.wmap[wid]
        if idx in self.wstored:
            self.dma(dst, self.wscr[idx][:, 0:nk * 512].rearrange("p (k n) -> p k n", k=nk),
                     reads=[("dram", "wscr", idx)], writes=[("w", b)], key=("w", b), eng="pool")
        else:
            self.dma(dst, src.rearrange("(k p) n -> p k n", p=128), reads=[], writes=[("w", b)],
                     key=("w", b), eng="pool")
            if not self.defer_store:
                self.wstored.add(idx)
                self.dma(self.wscr[idx][:, 0:nk * 512].rearrange("p (k n) -> p k n", k=nk), dst,
                         reads=[("w", b)], writes=[("dram", "wscr", idx)], key="wst")
        return b

    def plan_preconvert(self):
        q = []
        for j in range(4):
            q.append((self.wo_d[:, 512 * j:512 * j + 512], 16, (self.wo_d.name, j)))
        for j in range(NF // 4):
            q.append((self.g1_d[:, 512 * j:512 * j + 512], 16, (self.g1_d.name, j)))
            q.append((self.u1_d[:, 512 * j:512 * j + 512], 16, (self.u1_d.name, j)))
        for qq in range(4):
            for g in range(3):
                nk = 16 if g < 2 else NF - 32
                q.append((self.d1_d[g * 2048:g * 2048 + nk * 128, 512 * qq:512 * qq + 512], nk,
                          (self.d1_d.name, g, qq)))
        self.preq = q

    def preconvert_one(self):
        if not self.preq:
            return
        src, nk, wid = self.preq.pop(0)
        idx = self.wmap[wid] = len(self.wmap)
        self.wstored.add(idx)
        self.dma(self.wscr[idx][:, 0:nk * 512].rearrange("p (k n) -> p k n", k=nk),
                 src.rearrange("(k p) n -> p k n", p=128), reads=[], writes=[("dram", "wscr", idx)],
                 key="wpc", eng="pool")

    def setup(self):
        self.dma(self.vecs[:], self.vecs_d, [], ["vecs"], "c0")
        self.dma(self.ident[:], self.ident_d, [], ["ident"], "c1")
        self.cp(self.identb[:], self.ident[:], ["ident"], ["identb"])
        self.P.op("dve", lambda e: e.memset(self.onesb[:], 1.0), writes=["onesb"])
        self.P.op("dve", lambda e: e.memset(self.onesf[:], 1.0), writes=["onesf"])
        self.P.op("dve", lambda e: e.memset(self.eps_rms[:], RMS_EPS), writes=["eps"])
        self.P.op("dve", lambda e: e.memset(self.eps_ln[:], LN_EPS), writes=["eps"])

    def rmsnorm(self, src, skey, dst, dkey, gname, n):
        S = self.ps[6]
        for c in range(16):
            sq = self.sq[c % 4]
            self.actv(sq[:, 0:n], src(c), AF.Square, [skey(c)], [("sq", c % 4)])
            self.mm(S[:, 0:n], self.onesb[:], sq[:, 0:n], c == 0, c == 15,
                    [("sq", c % 4), "onesb"], [("ps", 6)])
        r = self.rstd[:, 0:n]
        self.actv(r, S[:, 0:n], AF.Sqrt, [("ps", 6)], ["rstd"], bias=self.eps_rms[:], scale=1.0 / D)
        self.P.op("dve", lambda e: e.reciprocal(out=r, in_=r), reads=["rstd"], writes=["rstd"])
        for c in range(16):
            self.stt(dst(c), src(c), self.vcol(gname, c), r, ALU.mult, ALU.mult,
                     [skey(c), "rstd", "vecs"], [dkey(c)])

    def load_x_tile(self, i):
        for tb in range(4):
            xs = self.xs[tb % 2]
            r0 = i * TT + tb * 128
            self.dma(xs, self.x_d[r0:r0 + 128, :], [], [("stage", tb % 2)], ("xs", tb % 2))
            for cg in range(4):
                b = self.bank()
                for cc in range(4):
                    c = cg * 4 + cc
                    self.tr(self.ps[b][:, cc * 128:(cc + 1) * 128], xs[:, c * 128:(c + 1) * 128], self.ident[:],
                            [("stage", tb % 2), "ident"], [("ps", b)])
                self.cp(self.hT[:, cg * 4:(cg + 1) * 4, tb * 128:(tb + 1) * 128],
                        self.ps[b][:].rearrange("p (c t) -> p c t", c=4), [("ps", b)],
                        [("hT", cg * 4 + k) for k in range(4)], eng=("act" if cg % 2 else "dve"))

    def load_halo(self):
        self.xsh = self.xs[1]
        self.dma(self.xsh[0:HALO, :], self.xh_d, [], [("stage", 1)], "xsh")
        for cg in range(4):
            b = self.bank()
            for cc in range(4):
                c = cg * 4 + cc
                self.tr(self.ps[b][:, cc * HALO:(cc + 1) * HALO], self.xsh[0:HALO, c * 128:(c + 1) * 128],
                        self.ident[0:HALO, 0:HALO], [("stage", 1), "ident"], [("ps", b)])
            self.cp(self.hTh[:, cg * 4:(cg + 1) * 4, :],
                    self.ps[b][:, 0:4 * HALO].rearrange("p (c t) -> p c t", c=4), [("ps", b)],
                    [("hTh", cg * 4 + k) for k in range(4)])

    def w1_glu(self, src, skey, n, dst, dkey, halo):
        for j in range(4):
            ba = self.wload(self.w1_d[:, 512 * j:512 * j + 512], wid=("w1a", j))
            bg = self.wload(self.w1_d[:, D + 512 * j:D + 512 * j + 512], wid=("w1g", j))
            for half in range(2):
                ccs = (2 * half, 2 * half + 1)
                bk = {cc: (self.bank(), self.bank()) for cc in ccs}
                items = []
                for cc in ccs:
                    cs = slice(cc * 128, (cc + 1) * 128)
                    items += [(bk[cc][0], ba, cs), (bk[cc][1], bg, cs)]
                self.mm_items(items, src, skey, n, dc_outer=(j == 0 and half == 0))
                for cc in ccs:
                    c = 4 * j + cc
                    pa, pg = bk[cc]
                    t = self.tmp[c % 3]
                    self.actv(t[:, 0:n], self.ps[pg][:, 0:n], AF.Sigmoid, [("ps", pg), "vecs"],
                              [("tmp", c % 3)], bias=self.vcol("b1", 16 + c))
                    self.stt(dst(c), self.ps[pa][:, 0:n], self.vcol("b1", c), t[:, 0:n], ALU.add, ALU.mult,
                             [("ps", pa), ("tmp", c % 3), "vecs"], [dkey(c)])
                    if halo:
                        self.ts(dst(c), dst(c), self.vcol("hm"), None, ALU.mult, None, [dkey(c), "vecs"],
                                [dkey(c)])

    def conv_ln(self, i):
        S1, S2 = self.ps[6], self.ps[7]
        if i > 0:
            self.exchange(i - 1)
            self.cp(self.u[:, :, 0:HALO], self.uhs[:, :, :], ["uhs"], [("uh", c) for c in range(16)], eng="dve")
        for c in range(16):
            dg = self.dg[c % 2]
            s, _ = VEC_COLS["dw"]
            dwc = self.vecs[:, s + c * CW:s + (c + 1) * CW]
            self.tt(dg[:], self.identb[:].unsqueeze(1).broadcast_to([128, CW, 128]),
                    dwc.unsqueeze(2).broadcast_to([128, CW, 128]), ALU.mult,
                    ["identb", "vecs"], [("dg", c % 2)])
            b = self.bank()
            for j in range(CW):
                self.mm(self.ps[b][:], dg[:, j, :], self.u[:, c, HALO - (CW - 1 - j):HALO - (CW - 1 - j) + TT], j == 0, j == CW - 1,
                        [("dg", c % 2), ("u", c), ("uh", c)], [("ps", b)])
            self.actv(self.v[:, c, :], self.ps[b][:], AF.Identity, [("ps", b), "vecs"], [("v", c)],
                      bias=self.vcol("dwb", c))
            sq = self.sq[c % 4]
            self.actv(sq[:], self.ps[b][:], AF.Square, [("ps", b), "vecs"], [("sq", c % 4)],
                      bias=self.vcol("dwb", c))
            self.mm(S1[:], self.onesf[:], self.v[:, c, :], c == 0, c == 15, [("v", c), "onesf"], [("ps", 6)])
            self.mm(S2[:], self.onesb[:], sq[:], c == 0, c == 15, [("sq", c % 4), "onesb"], [("ps", 7)])
        if i < NTILE - 1:
            self.cp(self.uhs[:, :, :], self.u[:, :, TT:TT + HALO], [("u", c) for c in range(16)],
                    ["uhs"], eng="dve")
        mean, A, B = self.mean, self.lnA, self.lnB
        self.ts(mean[:], S1[:], 1.0 / D, None, ALU.mult, None, [("ps", 6)], ["mean"])
        self.tt(B[:], mean[:], mean[:], ALU.mult, ["mean"], ["lnB"])
        self.stt(A[:], S2[:], 1.0 / D, B[:], ALU.mult, ALU.subtract, [("ps", 7), "lnB"], ["lnA"])
        self.actv(A[:], A[:], AF.Sqrt, ["lnA"], ["lnA"], bias=self.eps_ln[:], scale=1.0)
        self.P.op("dve", lambda e: e.reciprocal(out=A[:], in_=A[:]), reads=["lnA"], writes=["lnA"])
        self.stt(B[:], mean[:], -1.0, A[:], ALU.mult, ALU.mult, ["mean", "lnA"], ["lnB"])
        for c in range(16):
            self.tt(self.v[:, c, :], self.v[:, c, :], A[:], ALU.mult, [("v", c), "lnA"], [("v", c)])
        for c in range(16):
            self.tt(self.v[:, c, :], self.v[:, c, :], B[:], ALU.add, [("v", c), "lnB"], [("v", c)])
            self.actv(self.xn[:, c, :], self.v[:, c, :], AF.Silu, [("v", c), "vecs"], [("xn", c)],
                      bias=self.vcol("lnb", c), scale=self.vcol("lng", c))

    def proj_resid(self, w_d, src, skey, bias_name):
        for j in range(4):
            b = self.wload(w_d[:, 512 * j:512 * j + 512], wid=(w_d.name, j))
            pbk = [self.bank() for _ in range(4)] if j == 0 else None
            if j == 0:
                self.mm_items([(pbk[cc], b, slice(cc * 128, (cc + 1) * 128)) for cc in range(4)], src, skey, TT, True)
            for cc in range(4):
                c = 4 * j + cc
                if j == 0:
                    p = pbk[cc]
                else:
                    p = self.bank()
                    self.mm_items([(p, b, slice(cc * 128, (cc + 1) * 128))], src, skey, TT, False)
                if bias_name is not None:
                    self.stt(self.hT[:, c, :], self.ps[p][:], self.vcol(bias_name, c), self.hT[:, c, :],
                             ALU.add, ALU.add, [("ps", p), ("hT", c), "vecs"], [("hT", c)])
                else:
                    self.tt(self.hT[:, c, :], self.ps[p][:], self.hT[:, c, :], ALU.add,
                            [("ps", p), ("hT", c)], [("hT", c)])

    def proj_out(self, w_d, dst, dkey, scale, eng):
        for j in range(4):
            b = self.wload(w_d[:, 512 * j:512 * j + 512], wid=(w_d.name, j))
            xsrc, xkey = (lambda dc: self.xn[:, dc, :]), (lambda dc: ("xn", dc))
            pbk = [self.bank() for _ in range(4)] if j == 0 else None
            if j == 0:
                self.mm_items([(pbk[cc], b, slice(cc * 128, (cc + 1) * 128)) for cc in range(4)], xsrc, xkey, TT, True)
            for cc in range(4):
                c = 4 * j + cc
                if j == 0:
                    p = pbk[cc]
                else:
                    p = self.bank()
                    self.mm_items([(p, b, slice(cc * 128, (cc + 1) * 128))], xsrc, xkey, TT, False)
                if scale is not None:
                    self.actv(dst(c), self.ps[p][:], AF.Identity, [("ps", p)], dkey(c), scale=scale)
                else:
                    self.cp(dst(c), self.ps[p][:], [("ps", p)], dkey(c), eng=eng)

    def ffn(self, g_d, u_d, d_d, gname):
        self.rmsnorm(lambda c: self.hT[:, c, :], lambda c: ("hT", c),
                     lambda c: self.xn[:, c, :], lambda c: ("xn", c), gname, TT)
        for j in range(NF // 4):
            bg = self.wload(g_d[:, 512 * j:512 * j + 512], wid=(g_d.name, j))
            bu = self.wload(u_d[:, 512 * j:512 * j + 512], wid=(u_d.name, j))
            if self.preconv:
                self.preconvert_one()
            xsrc, xkey = (lambda dc: self.xn[:, dc, :]), (lambda dc: ("xn", dc))
            for half in range(2):
                ccs = (2 * half, 2 * half + 1)
                bk = {cc: (self.bank(), self.bank()) for cc in ccs}
                items = []
                for cc in ccs:
                    cs = slice(cc * 128, (cc + 1) * 128)
                    items += [(bk[cc][0], bg, cs), (bk[cc][1], bu, cs)]
                self.mm_items(items, xsrc, xkey, TT, dc_outer=(j == 0 and half == 0))
                for cc in ccs:
                    f = 4 * j + cc
                    pg, pu = bk[cc]
                    t = self.tmp[f % 3]
                    self.actv(t[:], self.ps[pg][:], AF.Silu, [("ps", pg)], [("tmp", f % 3)])
                    self.tt(self.act[:, f, :], t[:], self.ps[pu][:], ALU.mult, [("tmp", f % 3), ("ps", pu)],
                            [("act", f)])
        for q in range(4):
            banks = [self.bank() for _ in range(4)]
            for g in range(3):
                nk = 16 if g < 2 else NF - 32
                b = self.wload(d_d[g * 2048:g * 2048 + nk * 128, 512 * q:512 * q + 512], nk, wid=(d_d.name, g, q))
                if self.preconv:
                    self.preconvert_one()
                for cc in range(4):
                    for fk in range(nk):
                        f = g * 16 + fk
                        self.mm(self.ps[banks[cc]][:], self.wbuf[b][:, fk, cc * 128:(cc + 1) * 128],
                                self.act[:, f, :], f == 0, f == NF - 1, [("w", b), ("act", f)],
                                [("ps", banks[cc])])
            for cc in range(4):
                c = 4 * q + cc
                self.tt(self.hT[:, c, :], self.ps[banks[cc]][:], self.hT[:, c, :], ALU.add,
                        [("ps", banks[cc]), ("hT", c)], [("hT", c)])

    def phaseA(self):
        self.plan_preconvert()
        for i in range(NTILE):
            cols = slice(i * TT, (i + 1) * TT)
            self.load_x_tile(i)
            self.rmsnorm(lambda c: self.hT[:, c, :], lambda c: ("hT", c),
                         lambda c: self.xn[:, c, :], lambda c: ("xn", c), "a_g", TT)
            self.w1_glu(lambda c: self.xn[:, c, :], lambda c: ("xn", c), TT,
                        lambda c: self.u[:, c, HALO:HALO + TT], lambda c: ("u", c), False)
            if i == 0:
                self.load_halo()
                self.rmsnorm(lambda c: self.hTh[:, c, :], lambda c: ("hTh", c),
                             lambda c: self.xnh[:, c, :], lambda c: ("xnh", c), "a_g", HALO)
                self.w1_glu(lambda c: self.xnh[:, c, :], lambda c: ("xnh", c), HALO,
                            lambda c: self.u[:, c, 0:HALO], lambda c: ("uh", c), True)
            self.conv_ln(i)
            self.proj_resid(self.w2_d, lambda c: self.xn[:, c, :], lambda c: ("xn", c), "b2")
            self.preconv = (i >= 2 and not NO_PRECONV)
            self.defer_store = (i == 0)
            self.ffn(self.g0_d, self.u0_d, self.d0_d, "f_g0")
            self.preconv = False
            self.defer_store = False
            self.dma(self.h1T_d[:, :, cols].rearrange("c p t -> p c t"), self.hT[:, :, :],
                     [("hT", c) for c in range(16)], [("dram", "h1", i)], "h1o")
            self.rmsnorm(lambda c: self.hT[:, c, :], lambda c: ("hT", c),
                         lambda c: self.xn[:, c, :], lambda c: ("xn", c), "kv_g", TT)
            skh = lambda c: [("stage", c // 8)]
            for w_d, dst_d, nm, key, eng in ((self.wk_d, self.k_own[i], "k", "kto", "act"),
                                             (self.wv_d, self.v_own[i], "v", "vo", "dve")):
                self.proj_out(w_d, lambda c: self.st16[:, c, :], skh, None, eng)
                dv = dst_d.rearrange("(h e) t -> e h t", e=128)
                for hh in range(2):
                    self.dma(dv[:, 8 * hh:8 * hh + 8, :], self.st16[:, 8 * hh:8 * hh + 8, :], [("stage", hh)],
                             [("dram", nm, i)], key)
            if "B" in self.phase:
                self.rmsnorm(lambda c: self.hT[:, c, :], lambda c: ("hT", c),
                             lambda c: self.xn[:, c, :], lambda c: ("xn", c), "b_g", TT)
                self.proj_out(self.wq_d, lambda c: self.st16[:, c, :], skh, QSCALE, "act")
                qv = self.qT_d[:, :, cols].rearrange("h e t -> e h t")
                for hh in range(2):
                    self.dma(qv[:, 8 * hh:8 * hh + 8, :], self.st16[:, 8 * hh:8 * hh + 8, :], [("stage", hh)], [],
                             "qo")
        self.exchange(NTILE - 1)

    def exchange(self, i):
        if self.phase != "AB" or DEBUG_NOCC:
            return
        rg = [[0, 1], [2, 3], [4, 5], [6, 7]]
        self.P.op("pool", lambda e: e.collective_compute(
            "AllGather", ALU.bypass, replica_groups=rg,
            ins=[self.k_own[i].rearrange("(a b) t -> a (b t)", b=4)],
            outs=[self.k_all[i].rearrange("(a b) t -> a (b t)", b=4)]),
            reads=[("dram", "k", i)], writes=[("dram", "kall")], dma="cc", inc=1)
        self.P.op("pool", lambda e: e.collective_compute(
            "AllGather", ALU.bypass, replica_groups=rg,
            ins=[self.v_own[i].rearrange("(a b) t -> a (b t)", b=4)],
            outs=[self.v_all[i].rearrange("(a b) t -> a (b t)", b=4)]),
            reads=[("dram", "v", i)], writes=[("dram", "vall")], dma="cc", inc=1)

    def phaseB1(self):
        for i in range(NTILE):
            cols = slice(i * TT, (i + 1) * TT)
            self.dma(self.hT[:, :, :], self.h1T_d[:, :, cols].rearrange("c p t -> p c t"), [("dram", "h1", i)],
                     [("hT", c) for c in range(16)], "h1i")
            self.rmsnorm(lambda c: self.hT[:, c, :], lambda c: ("hT", c),
                         lambda c: self.xn[:, c, :], lambda c: ("xn", c), "b_g", TT)
            sk = [("stage", 0), ("stage", 1)]
            self.proj_out(self.wq_d, lambda c: self.st16[:, c, :], lambda c: sk, QSCALE, "act")
            self.dma(self.qT_d[:, :, cols].rearrange("h e t -> e h t"), self.st16[:, :, :], sk, [], "qo")

    def att_loads(self, h):
        s = h % 2
        hs = slice(h * 128, (h + 1) * 128)
        for dst, prev, own, key, dk in ((self.kTs[s], self.k_prev, self.k_own, ("kTs", s), "kall"),
                                        (self.vTs[s], self.v_prev, self.v_own, ("vTs", s), "vall")):
            self.dma(dst[:, 0:NTOK].rearrange("p (i t) -> p i t", i=NTILE),
                     prev[:, hs, :].rearrange("i e t -> e i t"), [("dram", dk)], [key], ("kl", s))
            self.dma(dst[:, NTOK:2 * NTOK].rearrange("p (i t) -> p i t", i=NTILE),
                     own[:, hs, :].rearrange("i e t -> e i t"), [], [key], ("kl", s))
        self.dma(self.qTs[s][:], self.qT_d[h], [], [("qTs", s)], ("ql", s))

    def att_vtiles(self, h):
        s = h % 2
        vT = self.vTs[s]
        jobs = []
        cols = [slice(NTOK + 128 * kb, NTOK + 128 * (kb + 1)) for kb in range(-1, 16)]
        jobs.append((self.V1[s].rearrange("p t e -> p (t e)"), cols))
        cols = []
        for r in range(4):
            for kb in range(-1, 4):
                st = NTOK + r + 512 * kb
                cols.append(slice(st, st + 509, 4))
        jobs.append((self.V4[s].rearrange("p r t e -> p (r t e)"), cols))
        cols = []
        for r in range(16):
            for kb in range(-1, 1):
                st = NTOK * (kb + 1) + r
                cols.append(slice(st, st + 2033, 16))
        jobs.append((self.V16[s].rearrange("p r t e -> p (r t e)"), cols))
        n = 0
        for dest, cols in jobs:
            for c0 in range(0, len(cols), 8):
                grp = cols[c0:c0 + 8]
                b = n % 4
                n += 1
                pb = self.ps[b][:].bitcast(BF16)
                for k, cs in enumerate(grp):
                    self.tr(pb[:, 128 * k:128 * (k + 1)], vT[:, cs], self.identb[:],
                            [("vTs", s), "identb"], [("ps", b)])
                self.cp(dest[:, 128 * c0:128 * (c0 + len(grp))], pb[:, 0:128 * len(grp)], [("ps", b)],
                        [("V", s)], eng=("act" if n % 2 else "dve"))

    def att_groups(self, s):
        kT, qT = self.kTs[s], self.qTs[s]
        groups = []
        for g in range(4):
            blocks = []
            for bb in range(4):
                nb = 4 * g + bb
                blocks.append((kT[:, NTOK + 128 * (nb - 1):NTOK + 128 * nb],
                               kT[:, NTOK + 128 * nb:NTOK + 128 * (nb + 1)],
                               qT[:, 128 * nb:128 * (nb + 1)],
                               self.V1[s][:, nb, :], self.V1[s][:, nb + 1, :]))
            groups.append((self.TA if g == 0 else self.TB, blocks,
                           (lambda a, g=g: a[:, 512 * g:512 * (g + 1)]), 1))
        for r in range(4):
            blocks = []
            for nb in range(4):
                kb0 = NTOK + r + 512 * (nb - 1)
                kb1 = NTOK + r + 512 * nb
                blocks.append((kT[:, kb0:kb0 + 509:4], kT[:, kb1:kb1 + 509:4],
                               qT[:, r + 512 * nb:r + 512 * nb + 509:4],
                               self.V4[s][:, r, nb, :], self.V4[s][:, r, nb + 1, :]))
            groups.append((self.TA, blocks, (lambda a, r=r: a[:, 512 * r:512 * (r + 1)]), 4))
        for g in range(4):
            blocks = []
            for bb in range(4):
                r = 4 * g + bb
                blocks.append((kT[:, r:NTOK:16], kT[:, NTOK + r:2 * NTOK:16], qT[:, r:NTOK:16],
                               self.V16[s][:, r, 0, :], self.V16[s][:, r, 1, :]))
            groups.append((self.TD, blocks, (lambda a, g=g: a[:, 512 * g:512 * (g + 1)]), 16))
        return groups

    def att_scores(self, h, gi, grp):
        s = h % 2
        Tx, blocks, accf, d = grp
        par = gi % 2
        X, Y = self.ps[par * 2], self.ps[par * 2 + 1]
        for bb, (kp, kc, q, vp, vc) in enumerate(blocks):
            self.mm(X[:, 128 * bb:128 * (bb + 1)], kp, q, True, True, [("kTs", s), ("qTs", s)], [("ps", par * 2)])
        for bb, (kp, kc, q, vp, vc) in enumerate(blocks):
            self.mm(Y[:, 128 * bb:128 * (bb + 1)], kc, q, True, True, [("kTs", s), ("qTs", s)],
                    [("ps", par * 2 + 1)])
        slope = 2.0 ** (-8.0 * (h + 1) / H)
        for w, (T, Pb) in enumerate(((Tx, X), (self.TC, Y))):
            scb, ptb = self.sc[par][w], self.pT[par][w]
            self.stt(scb[:], T[:], -slope * d, Pb[:], ALU.mult, ALU.add, ["T", ("ps", par * 2 + w)],
                     [("sc", par, w)])
            self.actv(ptb[:], scb[:], AF.Exp, [("sc", par, w)], [("pT", par, w)])

    def att_pv(self, h, gi, grp):
        s = h % 2
        Tx, blocks, accf, d = grp
        par = gi % 2
        N_, Dn = self.ps[4 + par * 2], self.ps[5 + par * 2]
        pX, pY = self.pT[par][0], self.pT[par][1]
        for bb, (kp, kc, q, vp, vc) in enumerate(blocks):
            cs = slice(128 * bb, 128 * (bb + 1))
            self.mm(N_[:, cs], vp, pX[:, cs], True, False, [("V", s), ("pT", par, 0)], [("ps", 4 + par * 2)])
            self.mm(N_[:, cs], vc, pY[:, cs], False, True, [("V", s), ("pT", par, 1)], [("ps", 4 + par * 2)])
        for bb in range(4):
            cs = slice(128 * bb, 128 * (bb + 1))
            self.mm(Dn[:, cs], self.onesb[:], pX[:, cs], True, False, ["onesb", ("pT", par, 0)],
                    [("ps", 5 + par * 2)])
            self.mm(Dn[:, cs], self.onesb[:], pY[:, cs], False, True, ["onesb", ("pT", par, 1)],
                    [("ps", 5 + par * 2)])
        br = gi // 4
        an, ad = accf(self.accn[br]), accf(self.accd[br])
        pn, pd = N_[:], Dn[:]
        self.cp(an, pn, [("ps", 4 + par * 2)], [("accn", br, gi % 4)], eng="act")
        self.cp(ad, pd, [("ps", 5 + par * 2)], [("accd", br, gi % 4)], eng=("act" if gi % 2 else "dve"))

    def att_finalize(self, h):
        s = h % 2
        nk = lambda br: [("accn", br, g) for g in range(4)]
        dk = lambda br: [("accd", br, g) for g in range(4)]
        v4 = lambda a: a[:].rearrange("p (r i) -> p i r", r=4)
        n4 = lambda a: a[:].rearrange("p (i r) -> p i r", r=4)
        v16 = lambda a: a[:].rearrange("p (r q) -> p q r", r=16)
        n16 = lambda a: a[:].rearrange("p (q r) -> p q r", r=16)
        for g in range(4):
            qs = slice(128 * g, 128 * (g + 1))
            self.tt(n4(self.nsum)[:, qs, :], n4(self.accn[0])[:, qs, :], v4(self.accn[1])[:, qs, :], ALU.add,
                    [("accn", 0, g)] + nk(1), [("nsum", g)], eng="pool")
            self.tt(n4(self.dsum)[:, qs, :], n4(self.accd[0])[:, qs, :], v4(self.accd[1])[:, qs, :], ALU.add,
                    [("accd", 0, g)] + dk(1), [("dsum", g)], eng="pool")
        sg = lambda nm: [(nm, g) for g in range(4)]
        self.tt(n16(self.dsum), n16(self.dsum), v16(self.accd[2]), ALU.add, sg("dsum") + dk(2), sg("dsum"),
                eng="pool")
        self.tt(n16(self.nsum), n16(self.nsum), v16(self.accn[2]), ALU.add, sg("nsum") + nk(2), sg("nsum"),
                eng="pool")

    def att_finalize_b(self, h):
        s = h % 2
        sg = lambda nm: [(nm, g) for g in range(4)]
        self.actv(self.rden[:], self.dsum[:], AF.Ln, sg("dsum"), ["rden"])
        self.actv(self.rden[:], self.rden[:], AF.Exp, ["rden"], ["rden"], scale=-1.0)
        ao = self.atto[s]
        self.tt(ao[:], self.nsum[:], self.rden[:], ALU.mult, sg("nsum") + ["rden"], [("atto", s)], eng="pool")
        self.dma(self.attT_d[h], ao[:], [("atto", s)], [], ("ao", s))

    def phaseB2(self):
        P = self.P
        P.fence()
        self.dma(self.TB[:], self.tc_d[:, 0, :], [], ["T"], "c2")
        self.dma(self.TC[:], self.tc_d[:, 1, :], [], ["T"], "c2")
        self.cp(self.TA[:], self.TB[:], ["T"], ["TA"])
        self.ts(self.TD[:], self.TB[:], self.vcol("pm"), None, ALU.add, None, ["T", "vecs"], ["T"])
        self.ts(self.TA[:, 0:128], self.TB[:, 0:128], self.vcol("pm"), None, ALU.add, None,
                ["T", "TA", "vecs"], ["T"])
        self.att_loads(0)
        for h in range(H):
            s = h % 2
            if h + 1 < H:
                self.att_loads(h + 1)
            groups = self.att_groups(s)
            self.att_vtiles(h)
            self.att_scores(h, 0, groups[0])
            for gi in range(len(groups)):
                if gi + 1 < len(groups):
                    self.att_scores(h, gi + 1, groups[gi + 1])
                self.att_pv(h, gi, groups[gi])
                if gi == 5 and h > 0:
                    self.att_finalize_b(h - 1)
            self.att_finalize(h)
        self.att_finalize_b(H - 1)
        P.fence()

    def phaseB3(self):
        for i in range(NTILE):
            cols = slice(i * TT, (i + 1) * TT)
            sk = [("stage", 0), ("stage", 1)]
            self.dma(self.hT[:, :, :], self.h1T_d[:, :, cols].rearrange("c p t -> p c t"), [("dram", "h1", i)],
                     [("hT", c) for c in range(16)], "h1i")
            self.dma(self.st16[:, :, :], self.attT_d[:, :, cols].rearrange("h e t -> e h t"), [], sk, "ati")
            self.proj_resid(self.wo_d, lambda c: self.st16[:, c, :], lambda c: ("stage", c // 8), None)
            self.ffn(self.g1_d, self.u1_d, self.d1_d, "f_g1")
            self.rmsnorm(lambda c: self.hT[:, c, :], lambda c: ("hT", c),
                         lambda c: self.v[:, c, :], lambda c: ("v", c), "fin_g", TT)
            for tb in range(4):
                xs = self.xs[tb % 2]
                for cg in range(4):
                    b = self.bank()
                    for cc in range(4):
                        c = cg * 4 + cc
                        self.tr(self.ps[b][:, cc * 128:(cc + 1) * 128], self.v[:, c, tb * 128:(tb + 1) * 128],
                                self.ident[:], [("v", c), "ident"], [("ps", b)])
                    self.cp(xs[:, cg * 512:(cg + 1) * 512], self.ps[b][:], [("ps", b)], [("stage", tb % 2)],
                            eng=("act" if cg % 2 else "dve"))
                r0 = i * TT + tb * 128
                self.dma(self.out_d[r0:r0 + 128, :], xs, [("stage", tb % 2)], [], ("oo", tb % 2))

    def build(self):
        self.setup()
        if "A" in self.phase:
            self.phaseA()
        if "B" in self.phase:
            if "A" not in self.phase:
                self.phaseB1()
            self.phaseB2()
            self.phaseB3()
        self.P.emit()
        return self.nc


_NC_CACHE = {}


def get_nc(phase):
    if phase not in _NC_CACHE:
        _NC_CACHE[phase] = Builder(phase).build()
    return _NC_CACHE[phase]


def kernel(**inp):
    inp = {k: np.asarray(v) for k, v in inp.items()}
    x = inp["x"]
    ident, tconst = make_consts()
    ncores = 8
    c32 = lambda a: np.ascontiguousarray(a, dtype=np.float32)
    w = {"conv_w1": c32(inp["conv_w1"][0]), "conv_w2": c32(inp["conv_w2"][0]), "w_k": c32(inp["w_k"]),
         "w_v": c32(inp["w_v"]), "gate0": c32(inp["ffn_w_gate"][0]), "up0": c32(inp["ffn_w_up"][0]),
         "down0": c32(inp["ffn_w_down"][0]),
         "w_q": c32(inp["w_q"][0]), "w_o": c32(inp["w_o"][0]), "gate1": c32(inp["ffn_w_gate"][1]),
         "up1": c32(inp["ffn_w_up"][1]), "down1": c32(inp["ffn_w_down"][1]), "tconst": tconst,
         "ident": ident}
    maps = []
    for core in range(ncores):
        b, half = core // 2, core % 2
        t0 = half * NTOK
        xo = c32(x[b, t0:t0 + NTOK])
        xh = c32(x[b, t0 - HALO:t0]) if half == 1 else np.zeros((HALO, D), np.float32)
        m = {"x": xo, "xh": xh, "vecs": make_vecs(inp, half)}
        m.update(w)
        maps.append(m)
    res = run_bass_kernel_spmd(get_nc("AB"), maps, core_ids=list(range(ncores)))
    out = np.empty((4, SEQ, D), np.float32)
    for core in range(ncores):
        b, half = core // 2, core % 2
        out[b, half * NTOK:(half + 1) * NTOK] = res.results[core]["out"]
    return out
```

```python
import contextlib
import numpy as np
import ml_dtypes
import concourse.bass as bass
import concourse.mybir as mybir
from concourse.bass_utils import run_bass_kernel_spmd

F32 = mybir.dt.float32
BF16 = mybir.dt.bfloat16
AF = mybir.ActivationFunctionType
ALU = mybir.AluOpType

D = 2048
FF = 5632
NF = FF // 128
SEQ = 4096
NTOK = 2048
TT = 512
NTILE = NTOK // TT
H = 16
CW = 31
HALO = 32
RMS_EPS = 1e-6
LN_EPS = 1e-5
BIG = 30000.0
QSCALE = 128.0 ** -0.5

ENGS = ("pe", "act", "dve", "pool", "sp")
import os
DEBUG_NOCC = bool(os.environ.get("KDEBUG_NOCC"))
NO_PRECONV = False


class Op:
    __slots__ = ("eng", "idx", "fn", "deps", "signal", "dma_key", "dma_cnt", "cum", "inc")

    def __init__(self, eng, idx, fn):
        self.eng = eng
        self.idx = idx
        self.fn = fn
        self.deps = []
        self.signal = False
        self.dma_key = None
        self.dma_cnt = 0
        self.cum = 0
        self.inc = 16


class Prog:
    def __init__(self, nc):
        self.nc = nc
        self.ops = {e: [] for e in ENGS}
        self.last_w = {}
        self.readers = {}
        self.dma_cnt = {}
        self.dma_inc = {}

    def op(self, eng, fn, reads=(), writes=(), dma=None, inc=16):
        o = Op(eng, len(self.ops[eng]), fn)
        if dma is not None:
            o.dma_key = dma
            o.inc = inc
            self.dma_inc[dma] = inc
            self.dma_cnt[dma] = self.dma_cnt.get(dma, 0) + 1
            o.dma_cnt = self.dma_cnt[dma]
        deps = []
        for k in reads:
            w = self.last_w.get(k)
            if w is not None:
                deps.append(w)
        for k in writes:
            w = self.last_w.get(k)
            if w is not None:
                deps.append(w)
            for r in self.readers.get(k, {}).values():
                deps.append(r)
        seen = set()
        for d in deps:
            if d is o or id(d) in seen:
                continue
            seen.add(id(d))
            if d.dma_key is None and d.eng == eng:
                if eng in ("pe", "sp"):
                    continue
                if o.dma_key is None and o.idx - d.idx > 3:
                    continue
            o.deps.append(d)
            if d.dma_key is None:
                d.signal = True
        for k in reads:
            self.readers.setdefault(k, {})[(eng, o.dma_key)] = o
        for k in writes:
            self.last_w[k] = o
            self.readers[k] = {}
        self.ops[eng].append(o)
        return o

    def fence(self):
        lasts = []
        for e in ENGS:
            for o in reversed(self.ops[e]):
                if o.dma_key is None and o.fn is not None:
                    lasts.append(o)
                    break
        dmas = []
        for k, c in self.dma_cnt.items():
            p = Op("sp", -1, None)
            p.dma_key = k
            p.dma_cnt = c
            p.inc = self.dma_inc[k]
            dmas.append(p)
        for e in ENGS:
            o = Op(e, len(self.ops[e]), None)
            for d in lasts:
                if d.eng != e:
                    o.deps.append(d)
                    d.signal = True
            o.deps.extend(dmas)
            self.ops[e].append(o)
        self.last_w = {}
        self.readers = {}

    def emit(self):
        nc = self.nc
        with contextlib.ExitStack() as st:
            esem = {e: st.enter_context(nc.semaphore("s_" + e)) for e in ENGS}
            dsem = {}
            for n, k in enumerate(self.dma_cnt):
                dsem[k] = st.enter_context(nc.semaphore("d%d" % n))
            block = st.enter_context(nc.Block())
            for e in ENGS:
                c = 0
                for o in self.ops[e]:
                    if o.signal:
                        c += 1
                    o.cum = c

            def run(e, engine):
                waited = {}
                for o in self.ops[e]:
                    for d in o.deps:
                        if d.dma_key is not None:
                            s, v, key = dsem[d.dma_key], d.inc * d.dma_cnt, ("d", d.dma_key)
                        else:
                            s, v, key = esem[d.eng], d.cum, ("e", d.eng)
                        if waited.get(key, 0) >= v:
                            continue
                        waited[key] = v
                        engine.wait_ge(s, v)
                    if o.fn is None:
                        continue
                    ins = o.fn(engine)
                    if o.dma_key is not None:
                        ins.then_inc(dsem[o.dma_key], o.inc)
                    elif o.signal:
                        ins.then_inc(esem[e], 1)
                if e == "sp":
                    for k, c in self.dma_cnt.items():
                        if waited.get(("d", k), 0) < self.dma_inc[k] * c:
                            engine.wait_ge(dsem[k], self.dma_inc[k] * c)

            @block.tensor
            def _(eng):
                run("pe", eng)

            @block.scalar
            def _(eng):
                run("act", eng)

            @block.vector
            def _(eng):
                run("dve", eng)

            @block.gpsimd
            def _(eng):
                run("pool", eng)

            @block.sync
            def _(eng):
                run("sp", eng)


VEC_COLS = {}
_c = 0
for _n, _w in (("a_g", 16), ("b1", 32), ("dwb", 16), ("lng", 16), ("lnb", 16), ("b2", 16),
               ("kv_g", 16), ("b_g", 16), ("f_g0", 16), ("f_g1", 16), ("fin_g", 16),
               ("dw", 16 * CW), ("hm", 1), ("pm", 1)):
    VEC_COLS[_n] = (_c, _w)
    _c += _w
NV = _c


def _pack(v):
    return np.ascontiguousarray(np.asarray(v, np.float32).reshape(-1, 128).T)


def make_vecs(inp, half):
    vecs = np.zeros((128, NV), np.float32)

    def put(name, arr):
        s, w = VEC_COLS[name]
        vecs[:, s:s + w] = arr

    put("a_g", _pack(inp["a_norm_g"][0]))
    put("b1", _pack(inp["conv_b1"][0]))
    put("dwb", _pack(inp["conv_dw_b"][0]))
    put("lng", _pack(inp["conv_ln_g"][0]))
    put("lnb", _pack(inp["conv_ln_b"][0]))
    put("b2", _pack(inp["conv_b2"][0]))
    put("kv_g", _pack(inp["kv_norm_g"]))
    put("b_g", _pack(inp["b_norm_g"][0]))
    put("f_g0", _pack(inp["ffn_norm_g"][0]))
    put("f_g1", _pack(inp["ffn_norm_g"][1]))
    put("fin_g", _pack(inp["final_norm_g"]))
    dw = np.asarray(inp["conv_dw"][0], np.float32)
    dwp = dw.T.reshape(16, 128, CW).transpose(1, 0, 2).reshape(128, 16 * CW)
    put("dw", dwp)
    put("hm", np.full((128, 1), 1.0 if half == 1 else 0.0, np.float32))
    put("pm", np.full((128, 1), 0.0 if half == 1 else BIG, np.float32))
    return vecs


def make_consts():
    k = np.arange(128)[:, None]
    q = np.arange(128)[None, :]
    t0 = np.where(k >= q, q - k + 128, BIG).astype(np.float32)
    t1 = np.where(k <= q, q - k, BIG).astype(np.float32)
    tc = np.zeros((128, 2, 512), np.float32)
    tc[:, 0, :] = np.tile(t0, (1, 4))
    tc[:, 1, :] = np.tile(t1, (1, 4))
    return np.eye(128, dtype=np.float32), tc


class Builder:
    def __init__(self, phase):
        self.phase = phase
        nc = self.nc = bass.Bass("TRN2", target_bir_lowering=False)
        self.P = Prog(nc)
        self.nbank = 0
        self.nw = 0
        din = lambda n, s, dt=F32: nc.dram_tensor(n, s, dt, kind="ExternalInput").ap()
        dout = lambda n, s, dt=F32: nc.dram_tensor(n, s, dt, kind="ExternalOutput").ap()
        dint = lambda n, s, dt=F32: nc.dram_tensor(n, s, dt, kind="Internal").ap()
        self.vecs_d = din("vecs", [128, NV])
        self.ident_d = din("ident", [128, 128])
        hasA = "A" in phase
        hasB = "B" in phase
        if hasA:
            self.x_d = din("x", [NTOK, D])
            self.xh_d = din("xh", [HALO, D])
            self.w1_d = din("conv_w1", [D, 2 * D])
            self.w2_d = din("conv_w2", [D, D])
            self.wk_d = din("w_k", [D, D])
            self.wv_d = din("w_v", [D, D])
            self.g0_d = din("gate0", [D, FF])
            self.u0_d = din("up0", [D, FF])
            self.d0_d = din("down0", [FF, D])
        if hasB:
            self.tc_d = din("tconst", [128, 2, 512])
            self.wq_d = din("w_q", [D, D])
            self.wo_d = din("w_o", [D, D])
            self.g1_d = din("gate1", [D, FF])
            self.u1_d = din("up1", [D, FF])
            self.d1_d = din("down1", [FF, D])
            self.out_d = dout("out", [NTOK, D])
            self.qT_d = dint("qT", [H, 128, NTOK], BF16)
            self.attT_d = dint("attT", [H, 128, NTOK], BF16)
        KSH = [NTILE, H * 128, TT]
        assert phase == "AB", "only the fused program is supported"
        self.h1T_d = dint("h1T", [16, 128, NTOK])
        self.wscr = dint("wscr", [96, 128, 16 * 512], BF16)
        self.wmap = {}
        self.preconv = False
        self.preq = []
        self.k_own = dint("k_own", KSH, BF16)
        self.v_own = dint("v_own", KSH, BF16)
        self.k_all = dint("k_all", [NTILE, 2 * H * 128, TT], BF16)
        self.v_all = dint("v_all", [NTILE, 2 * H * 128, TT], BF16)
        self.k_prev = self.k_all[:, 0:H * 128, :]
        self.v_prev = self.v_all[:, 0:H * 128, :]
        self.alloc()

    def alloc(self):
        nc = self.nc
        self.ps = [nc.alloc_psum_tensor("ps%d" % i, [128, 512], F32) for i in range(8)]
        off = [0]
        NB = 101 * 1024
        arena = self.arena = nc.alloc_sbuf_tensor("arena", [128, NB], BF16)

        def carve(nbytes, dt=BF16):
            n = nbytes // 2
            a = arena[:, off[0]:off[0] + n]
            off[0] += n
            assert off[0] <= NB, off[0]
            return a.bitcast(F32) if dt == F32 else a

        self.carve = carve
        self.vecs = nc.alloc_sbuf_tensor("vecs_sb", [128, NV], F32)
        self.ident = nc.alloc_sbuf_tensor("ident_sb", [128, 128], F32)
        self.identb = nc.alloc_sbuf_tensor("identb", [128, 128], BF16)
        self.onesb = nc.alloc_sbuf_tensor("onesb", [128, 128], BF16)
        self.onesf = nc.alloc_sbuf_tensor("onesf", [128, 128], F32)
        self.eps_rms = nc.alloc_sbuf_tensor("eps_rms", [128, 1], F32)
        self.eps_ln = nc.alloc_sbuf_tensor("eps_ln", [128, 1], F32)
        self.m0 = off[0]
        self.hT = carve(16 * 512 * 4, F32).rearrange("p (c t) -> p c t", c=16)
        self.xn = carve(16 * 512 * 2).rearrange("p (c t) -> p c t", c=16)
        u0 = off[0]
        self.u = carve(16 * 544 * 2).rearrange("p (c t) -> p c t", c=16)
        self.v = carve(16 * 512 * 4, F32).rearrange("p (c t) -> p c t", c=16)
        u1 = off[0]
        self.act = arena[:, u0:u0 + NF * 512].rearrange("p (c t) -> p c t", c=NF)
        assert u0 + NF * 512 <= u1
        s0 = off[0]
        stage = carve(16 * 512 * 2)
        self.stage = stage
        self.st16 = stage.rearrange("p (c t) -> p c t", c=16)
        self.vst = stage.rearrange("p (b f) -> p b f", b=4)
        self.xs = [arena[:, s0 + i * 4096:s0 + (i + 1) * 4096].bitcast(F32) for i in range(2)]
        self.wbuf = [carve(16 * 512 * 2).rearrange("p (k n) -> p k n", k=16) for _ in range(3)]
        self.dg = [carve(CW * 128 * 2).rearrange("p (j m) -> p j m", j=CW) for _ in range(2)]
        self.sq = [carve(512 * 2) for _ in range(4)]
        self.tmp = [carve(512 * 4, F32) for _ in range(3)]
        self.rstd = carve(512 * 4, F32)
        self.lnA = carve(512 * 4, F32)
        self.lnB = carve(512 * 4, F32)
        self.mean = carve(512 * 4, F32)
        self.hTh = carve(16 * HALO * 4, F32).rearrange("p (c t) -> p c t", c=16)
        self.xnh = carve(16 * HALO * 2).rearrange("p (c t) -> p c t", c=16)
        self.uhs = carve(16 * HALO * 2).rearrange("p (c t) -> p c t", c=16)
        self.m1 = off[0]
        if "B" in self.phase:
            off[0] = self.m0
            self.kTs = [carve(2 * NTOK * 2) for _ in range(2)]
            self.qTs = [carve(NTOK * 2) for _ in range(2)]
            self.V1 = [carve(17 * 128 * 2).rearrange("p (t e) -> p t e", e=128) for _ in range(2)]
            self.V4 = [carve(20 * 128 * 2).rearrange("p (r t e) -> p r t e", r=4, e=128) for _ in range(2)]
            self.V16 = [carve(32 * 128 * 2).rearrange("p (r t e) -> p r t e", r=16, e=128) for _ in range(2)]
            self.vTs = [carve(2 * NTOK * 2) for _ in range(2)]
            self.accn = [carve(NTOK * 4, F32) for _ in range(3)]
            self.accd = [carve(NTOK * 4, F32) for _ in range(3)]
            self.nsum = carve(NTOK * 4, F32)
            self.dsum = carve(NTOK * 4, F32)
            self.rden = carve(NTOK * 4, F32)
            self.atto = [carve(NTOK * 2) for _ in range(2)]
            self.sc = [[carve(512 * 4, F32) for _ in range(2)] for _ in range(2)]
            self.pT = [[carve(512 * 2) for _ in range(2)] for _ in range(2)]
            self.TA = carve(512 * 4, F32)
            self.TB = carve(512 * 4, F32)
            self.TD = carve(512 * 4, F32)
            self.TC = carve(512 * 4, F32)
            assert off[0] <= self.m1, (off[0], self.m1)
            off[0] = self.m1

    def vcol(self, name, c=0, w=1):
        s, _ = VEC_COLS[name]
        return self.vecs[:, s + c:s + c + w]

    def bank(self):
        b = self.nbank % 6
        self.nbank += 1
        return b

    def mm(self, out, lhsT, rhs, start, stop, reads, writes):
        self.P.op("pe", lambda e: e.matmul(out, lhsT=lhsT, rhs=rhs, start=start, stop=stop),
                  reads=reads, writes=writes)

    def mm_items(self, items, src, skey, n, dc_outer):
        if dc_outer:
            for dc in range(16):
                for p, b, cs in items:
                    self.mm(self.ps[p][:, 0:n], self.wbuf[b][:, dc, cs], src(dc), dc == 0, dc == 15,
                            [("w", b), skey(dc)], [("ps", p)])
        else:
            for p, b, cs in items:
                for dc in range(16):
                    self.mm(self.ps[p][:, 0:n], self.wbuf[b][:, dc, cs], src(dc), dc == 0, dc == 15,
                            [("w", b), skey(dc)], [("ps", p)])

    def tr(self, out, in_, ident, reads, writes):
        self.P.op("pe", lambda e: e.transpose(out=out, in_=in_, identity=ident), reads=reads, writes=writes)

    def actv(self, out, in_, func, reads, writes, bias=0.0, scale=1.0):
        self.P.op("act", lambda e: e.activation(out=out, in_=in_, func=func, bias=bias, scale=scale),
                  reads=reads, writes=writes)

    def stt(self, out, in0, scalar, in1, op0, op1, reads, writes, eng="dve"):
        self.P.op(eng, lambda e: e.scalar_tensor_tensor(out=out, in0=in0, scalar=scalar, in1=in1, op0=op0, op1=op1),
                  reads=reads, writes=writes)

    def tt(self, out, in0, in1, op, reads, writes, eng="dve"):
        self.P.op(eng, lambda e: e.tensor_tensor(out=out, in0=in0, in1=in1, op=op), reads=reads, writes=writes)

    def ts(self, out, in0, s1, s2, op0, op1, reads, writes, eng="dve"):
        if s2 is None:
            self.P.op(eng, lambda e: e.tensor_scalar(out=out, in0=in0, scalar1=s1, scalar2=None, op0=op0),
                      reads=reads, writes=writes)
        else:
            self.P.op(eng, lambda e: e.tensor_scalar(out=out, in0=in0, scalar1=s1, scalar2=s2, op0=op0, op1=op1),
                      reads=reads, writes=writes)

    def cp(self, out, in_, reads, writes, eng="dve"):
        if eng == "act":
            self.P.op("act", lambda e: e.copy(out=out, in_=in_), reads=reads, writes=writes)
        else:
            self.P.op(eng, lambda e: e.tensor_copy(out=out, in_=in_), reads=reads, writes=writes)

    def dma(self, out, in_, reads, writes, key, eng="sp"):
        return self.P.op(eng, lambda e: e.dma_start(out=out, in_=in_), reads=reads, writes=writes, dma=key)

    def wload(self, src, nk=16, wid=None):
        b = self.nw % 3
        self.nw += 1
        dst = self.wbuf[b][:, 0:nk, :]
        if wid not in self.wmap:
            idx = self.wmap[wid] = len(self.wmap)
            self.dma(dst, src.rearrange("(k p) n -> p k n", p=128), reads=[], writes=[("w", b)],
                     key=("w", b), eng="pool")
            self.dma(self.wscr[idx][:, 0:nk * 512].rearrange("p (k n) -> p k n", k=nk), dst,
                     reads=[("w", b)], writes=[("dram", "wscr", idx)], key="wst")
        else:
            idx = self.wmap[wid]
            self.dma(dst, self.wscr[idx][:, 0:nk * 512].rearrange("p (k n) -> p k n", k=nk),
                     reads=[("dram", "wscr", idx)], writes=[("w", b)], key=("w", b), eng="pool")
        return b

    def plan_preconvert(self):
        q = []
        for j in range(4):
            q.append((self.wo_d[:, 512 * j:512 * j + 512], 16, (self.wo_d.name, j)))
        for j in range(NF // 4):
            q.append((self.g1_d[:, 512 * j:512 * j + 512], 16, (self.g1_d.name, j)))
            q.append((self.u1_d[:, 512 * j:512 * j + 512], 16, (self.u1_d.name, j)))
        for qq in range(4):
            for g in range(3):
                nk = 16 if g < 2 else NF - 32
                q.append((self.d1_d[g * 2048:g * 2048 + nk * 128, 512 * qq:512 * qq + 512], nk,
                          (self.d1_d.name, g, qq)))
        self.preq = q

    def preconvert_one(self):
        if not self.preq:
            return
        src, nk, wid = self.preq.pop(0)
        idx = self.wmap[wid] = len(self.wmap)
        self.dma(self.wscr[idx][:, 0:nk * 512].rearrange("p (k n) -> p k n", k=nk),
                 src.rearrange("(k p) n -> p k n", p=128), reads=[], writes=[("dram", "wscr", idx)],
                 key="wpc", eng="pool")

    def setup(self):
        self.dma(self.vecs[:], self.vecs_d, [], ["vecs"], "c0")
        self.dma(self.ident[:], self.ident_d, [], ["ident"], "c1")
        self.cp(self.identb[:], self.ident[:], ["ident"], ["identb"])
        self.P.op("dve", lambda e: e.memset(self.onesb[:], 1.0), writes=["onesb"])
        self.P.op("dve", lambda e: e.memset(self.onesf[:], 1.0), writes=["onesf"])
        self.P.op("dve", lambda e: e.memset(self.eps_rms[:], RMS_EPS), writes=["eps"])
        self.P.op("dve", lambda e: e.memset(self.eps_ln[:], LN_EPS), writes=["eps"])

    def rmsnorm(self, src, skey, dst, dkey, gname, n):
        S = self.ps[6]
        for c in range(16):
            sq = self.sq[c % 4]
            self.actv(sq[:, 0:n], src(c), AF.Square, [skey(c)], [("sq", c % 4)])
            self.mm(S[:, 0:n], self.onesb[:], sq[:, 0:n], c == 0, c == 15,
                    [("sq", c % 4), "onesb"], [("ps", 6)])
        r = self.rstd[:, 0:n]
        self.actv(r, S[:, 0:n], AF.Sqrt, [("ps", 6)], ["rstd"], bias=self.eps_rms[:], scale=1.0 / D)
        self.P.op("dve", lambda e: e.reciprocal(out=r, in_=r), reads=["rstd"], writes=["rstd"])
        for c in range(16):
            self.stt(dst(c), src(c), self.vcol(gname, c), r, ALU.mult, ALU.mult,
                     [skey(c), "rstd", "vecs"], [dkey(c)])

    def load_x_tile(self, i):
        for tb in range(4):
            xs = self.xs[tb % 2]
            r0 = i * TT + tb * 128
            self.dma(xs, self.x_d[r0:r0 + 128, :], [], [("stage", tb % 2)], ("xs", tb % 2))
            for cg in range(4):
                b = self.bank()
                for cc in range(4):
                    c = cg * 4 + cc
                    self.tr(self.ps[b][:, cc * 128:(cc + 1) * 128], xs[:, c * 128:(c + 1) * 128], self.ident[:],
                            [("stage", tb % 2), "ident"], [("ps", b)])
                self.cp(self.hT[:, cg * 4:(cg + 1) * 4, tb * 128:(tb + 1) * 128],
                        self.ps[b][:].rearrange("p (c t) -> p c t", c=4), [("ps", b)],
                        [("hT", cg * 4 + k) for k in range(4)], eng=("act" if cg % 2 else "dve"))

    def load_halo(self):
        self.xsh = self.xs[1]
        self.dma(self.xsh[0:HALO, :], self.xh_d, [], [("stage", 1)], "xsh")
        for cg in range(4):
            b = self.bank()
            for cc in range(4):
                c = cg * 4 + cc
                self.tr(self.ps[b][:, cc * HALO:(cc + 1) * HALO], self.xsh[0:HALO, c * 128:(c + 1) * 128],
                        self.ident[0:HALO, 0:HALO], [("stage", 1), "ident"], [("ps", b)])
            self.cp(self.hTh[:, cg * 4:(cg + 1) * 4, :],
                    self.ps[b][:, 0:4 * HALO].rearrange("p (c t) -> p c t", c=4), [("ps", b)],
                    [("hTh", cg * 4 + k) for k in range(4)])

    def w1_glu(self, src, skey, n, dst, dkey, halo):
        for j in range(4):
            ba = self.wload(self.w1_d[:, 512 * j:512 * j + 512], wid=("w1a", j))
            bg = self.wload(self.w1_d[:, D + 512 * j:D + 512 * j + 512], wid=("w1g", j))
            for half in range(2):
                ccs = (2 * half, 2 * half + 1)
                bk = {cc: (self.bank(), self.bank()) for cc in ccs}
                items = []
                for cc in ccs:
                    cs = slice(cc * 128, (cc + 1) * 128)
                    items += [(bk[cc][0], ba, cs), (bk[cc][1], bg, cs)]
                self.mm_items(items, src, skey, n, dc_outer=(j == 0 and half == 0))
                for cc in ccs:
                    c = 4 * j + cc
                    pa, pg = bk[cc]
                    t = self.tmp[c % 3]
                    self.actv(t[:, 0:n], self.ps[pg][:, 0:n], AF.Sigmoid, [("ps", pg), "vecs"],
                              [("tmp", c % 3)], bias=self.vcol("b1", 16 + c))
                    self.stt(dst(c), self.ps[pa][:, 0:n], self.vcol("b1", c), t[:, 0:n], ALU.add, ALU.mult,
                             [("ps", pa), ("tmp", c % 3), "vecs"], [dkey(c)])
                    if halo:
                        self.ts(dst(c), dst(c), self.vcol("hm"), None, ALU.mult, None, [dkey(c), "vecs"],
                                [dkey(c)])

    def conv_ln(self, i):
        S1, S2 = self.ps[6], self.ps[7]
        if i > 0:
            self.exchange(i - 1)
            self.cp(self.u[:, :, 0:HALO], self.uhs[:, :, :], ["uhs"], [("uh", c) for c in range(16)], eng="dve")
        for c in range(16):
            dg = self.dg[c % 2]
            s, _ = VEC_COLS["dw"]
            dwc = self.vecs[:, s + c * CW:s + (c + 1) * CW]
            self.tt(dg[:], self.identb[:].unsqueeze(1).broadcast_to([128, CW, 128]),
                    dwc.unsqueeze(2).broadcast_to([128, CW, 128]), ALU.mult,
                    ["identb", "vecs"], [("dg", c % 2)])
            b = self.bank()
            for j in range(CW):
                self.mm(self.ps[b][:], dg[:, j, :], self.u[:, c, HALO - (CW - 1 - j):HALO - (CW - 1 - j) + TT], j == 0, j == CW - 1,
                        [("dg", c % 2), ("u", c), ("uh", c)], [("ps", b)])
            self.actv(self.v[:, c, :], self.ps[b][:], AF.Identity, [("ps", b), "vecs"], [("v", c)],
                      bias=self.vcol("dwb", c))
            sq = self.sq[c % 4]
            self.actv(sq[:], self.ps[b][:], AF.Square, [("ps", b), "vecs"], [("sq", c % 4)],
                      bias=self.vcol("dwb", c))
            self.mm(S1[:], self.onesf[:], self.v[:, c, :], c == 0, c == 15, [("v", c), "onesf"], [("ps", 6)])
            self.mm(S2[:], self.onesb[:], sq[:], c == 0, c == 15, [("sq", c % 4), "onesb"], [("ps", 7)])
        if i < NTILE - 1:
            self.cp(self.uhs[:, :, :], self.u[:, :, TT:TT + HALO], [("u", c) for c in range(16)],
                    ["uhs"], eng="dve")
        mean, A, B = self.mean, self.lnA, self.lnB
        self.ts(mean[:], S1[:], 1.0 / D, None, ALU.mult, None, [("ps", 6)], ["mean"])
        self.tt(B[:], mean[:], mean[:], ALU.mult, ["mean"], ["lnB"])
        self.stt(A[:], S2[:], 1.0 / D, B[:], ALU.mult, ALU.subtract, [("ps", 7), "lnB"], ["lnA"])
        self.actv(A[:], A[:], AF.Sqrt, ["lnA"], ["lnA"], bias=self.eps_ln[:], scale=1.0)
        self.P.op("dve", lambda e: e.reciprocal(out=A[:], in_=A[:]), reads=["lnA"], writes=["lnA"])
        self.stt(B[:], mean[:], -1.0, A[:], ALU.mult, ALU.mult, ["mean", "lnA"], ["lnB"])
        for c in range(16):
            self.tt(self.v[:, c, :], self.v[:, c, :], A[:], ALU.mult, [("v", c), "lnA"], [("v", c)])
        for c in range(16):
            self.tt(self.v[:, c, :], self.v[:, c, :], B[:], ALU.add, [("v", c), "lnB"], [("v", c)])
            self.actv(self.xn[:, c, :], self.v[:, c, :], AF.Silu, [("v", c), "vecs"], [("xn", c)],
                      bias=self.vcol("lnb", c), scale=self.vcol("lng", c))

    def proj_resid(self, w_d, src, skey, bias_name):
        for j in range(4):
            b = self.wload(w_d[:, 512 * j:512 * j + 512], wid=(w_d.name, j))
            pbk = [self.bank() for _ in range(4)] if j == 0 else None
            if j == 0:
                self.mm_items([(pbk[cc], b, slice(cc * 128, (cc + 1) * 128)) for cc in range(4)], src, skey, TT, True)
            for cc in range(4):
                c = 4 * j + cc
                if j == 0:
                    p = pbk[cc]
                else:
                    p = self.bank()
                    self.mm_items([(p, b, slice(cc * 128, (cc + 1) * 128))], src, skey, TT, False)
                if bias_name is not None:
                    self.stt(self.hT[:, c, :], self.ps[p][:], self.vcol(bias_name, c), self.hT[:, c, :],
                             ALU.add, ALU.add, [("ps", p), ("hT", c), "vecs"], [("hT", c)])
                else:
                    self.tt(self.hT[:, c, :], self.ps[p][:], self.hT[:, c, :], ALU.add,
                            [("ps", p), ("hT", c)], [("hT", c)])

    def proj_out(self, w_d, dst, dkey, scale, eng):
        for j in range(4):
            b = self.wload(w_d[:, 512 * j:512 * j + 512], wid=(w_d.name, j))
            xsrc, xkey = (lambda dc: self.xn[:, dc, :]), (lambda dc: ("xn", dc))
            pbk = [self.bank() for _ in range(4)] if j == 0 else None
            if j == 0:
                self.mm_items([(pbk[cc], b, slice(cc * 128, (cc + 1) * 128)) for cc in range(4)], xsrc, xkey, TT, True)
            for cc in range(4):
                c = 4 * j + cc
                if j == 0:
                    p = pbk[cc]
                else:
                    p = self.bank()
                    self.mm_items([(p, b, slice(cc * 128, (cc + 1) * 128))], xsrc, xkey, TT, False)
                if scale is not None:
                    self.actv(dst(c), self.ps[p][:], AF.Identity, [("ps", p)], dkey(c), scale=scale)
                else:
                    self.cp(dst(c), self.ps[p][:], [("ps", p)], dkey(c), eng=eng)

    def ffn(self, g_d, u_d, d_d, gname):
        self.rmsnorm(lambda c: self.hT[:, c, :], lambda c: ("hT", c),
                     lambda c: self.xn[:, c, :], lambda c: ("xn", c), gname, TT)
        for j in range(NF // 4):
            bg = self.wload(g_d[:, 512 * j:512 * j + 512], wid=(g_d.name, j))
            bu = self.wload(u_d[:, 512 * j:512 * j + 512], wid=(u_d.name, j))
            if self.preconv:
                self.preconvert_one()
            xsrc, xkey = (lambda dc: self.xn[:, dc, :]), (lambda dc: ("xn", dc))
            for half in range(2):
                ccs = (2 * half, 2 * half + 1)
                bk = {cc: (self.bank(), self.bank()) for cc in ccs}
                items = []
                for cc in ccs:
                    cs = slice(cc * 128, (cc + 1) * 128)
                    items += [(bk[cc][0], bg, cs), (bk[cc][1], bu, cs)]
                self.mm_items(items, xsrc, xkey, TT, dc_outer=(j == 0 and half == 0))
                for cc in ccs:
                    f = 4 * j + cc
                    pg, pu = bk[cc]
                    t = self.tmp[f % 3]
                    self.actv(t[:], self.ps[pg][:], AF.Silu, [("ps", pg)], [("tmp", f % 3)])
                    self.tt(self.act[:, f, :], t[:], self.ps[pu][:], ALU.mult, [("tmp", f % 3), ("ps", pu)],
                            [("act", f)])
        for q in range(4):
            banks = [self.bank() for _ in range(4)]
            for g in range(3):
                nk = 16 if g < 2 else NF - 32
                b = self.wload(d_d[g * 2048:g * 2048 + nk * 128, 512 * q:512 * q + 512], nk, wid=(d_d.name, g, q))
                if self.preconv and g == 0:
                    self.preconvert_one()
                for cc in range(4):
                    for fk in range(nk):
                        f = g * 16 + fk
                        self.mm(self.ps[banks[cc]][:], self.wbuf[b][:, fk, cc * 128:(cc + 1) * 128],
                                self.act[:, f, :], f == 0, f == NF - 1, [("w", b), ("act", f)],
                                [("ps", banks[cc])])
            for cc in range(4):
                c = 4 * q + cc
                self.tt(self.hT[:, c, :], self.ps[banks[cc]][:], self.hT[:, c, :], ALU.add,
                        [("ps", banks[cc]), ("hT", c)], [("hT", c)])

    def phaseA(self):
        self.plan_preconvert()
        for i in range(NTILE):
            cols = slice(i * TT, (i + 1) * TT)
            self.load_x_tile(i)
            self.rmsnorm(lambda c: self.hT[:, c, :], lambda c: ("hT", c),
                         lambda c: self.xn[:, c, :], lambda c: ("xn", c), "a_g", TT)
            self.w1_glu(lambda c: self.xn[:, c, :], lambda c: ("xn", c), TT,
                        lambda c: self.u[:, c, HALO:HALO + TT], lambda c: ("u", c), False)
            if i == 0:
                self.load_halo()
                self.rmsnorm(lambda c: self.hTh[:, c, :], lambda c: ("hTh", c),
                             lambda c: self.xnh[:, c, :], lambda c: ("xnh", c), "a_g", HALO)
                self.w1_glu(lambda c: self.xnh[:, c, :], lambda c: ("xnh", c), HALO,
                            lambda c: self.u[:, c, 0:HALO], lambda c: ("uh", c), True)
            self.conv_ln(i)
            self.proj_resid(self.w2_d, lambda c: self.xn[:, c, :], lambda c: ("xn", c), "b2")
            self.preconv = (i >= 1 and not NO_PRECONV)
            self.ffn(self.g0_d, self.u0_d, self.d0_d, "f_g0")
            self.preconv = False
            self.dma(self.h1T_d[:, :, cols].rearrange("c p t -> p c t"), self.hT[:, :, :],
                     [("hT", c) for c in range(16)], [("dram", "h1", i)], "h1o")
            self.rmsnorm(lambda c: self.hT[:, c, :], lambda c: ("hT", c),
                         lambda c: self.xn[:, c, :], lambda c: ("xn", c), "kv_g", TT)
            skh = lambda c: [("stage", c // 8)]
            for w_d, dst_d, nm, key, eng in ((self.wk_d, self.k_own[i], "k", "kto", "act"),
                                             (self.wv_d, self.v_own[i], "v", "vo", "dve")):
                self.proj_out(w_d, lambda c: self.st16[:, c, :], skh, None, eng)
                dv = dst_d.rearrange("(h e) t -> e h t", e=128)
                for hh in range(2):
                    self.dma(dv[:, 8 * hh:8 * hh + 8, :], self.st16[:, 8 * hh:8 * hh + 8, :], [("stage", hh)],
                             [("dram", nm, i)], key)
            if "B" in self.phase:
                self.rmsnorm(lambda c: self.hT[:, c, :], lambda c: ("hT", c),
                             lambda c: self.xn[:, c, :], lambda c: ("xn", c), "b_g", TT)
                self.proj_out(self.wq_d, lambda c: self.st16[:, c, :], skh, QSCALE, "act")
                qv = self.qT_d[:, :, cols].rearrange("h e t -> e h t")
                for hh in range(2):
                    self.dma(qv[:, 8 * hh:8 * hh + 8, :], self.st16[:, 8 * hh:8 * hh + 8, :], [("stage", hh)], [],
                             "qo")
        self.exchange(NTILE - 1)

    def exchange(self, i):
        if self.phase != "AB" or DEBUG_NOCC:
            return
        rg = [[0, 1], [2, 3], [4, 5], [6, 7]]
        self.P.op("pool", lambda e: e.collective_compute(
            "AllGather", ALU.bypass, replica_groups=rg,
            ins=[self.k_own[i].rearrange("(a b) t -> a (b t)", b=4)],
            outs=[self.k_all[i].rearrange("(a b) t -> a (b t)", b=4)]),
            reads=[("dram", "k", i)], writes=[("dram", "kall")], dma="cc", inc=1)
        self.P.op("pool", lambda e: e.collective_compute(
            "AllGather", ALU.bypass, replica_groups=rg,
            ins=[self.v_own[i].rearrange("(a b) t -> a (b t)", b=4)],
            outs=[self.v_all[i].rearrange("(a b) t -> a (b t)", b=4)]),
            reads=[("dram", "v", i)], writes=[("dram", "vall")], dma="cc", inc=1)

    def phaseB1(self):
        for i in range(NTILE):
            cols = slice(i * TT, (i + 1) * TT)
            self.dma(self.hT[:, :, :], self.h1T_d[:, :, cols].rearrange("c p t -> p c t"), [("dram", "h1", i)],
                     [("hT", c) for c in range(16)], "h1i")
            self.rmsnorm(lambda c: self.hT[:, c, :], lambda c: ("hT", c),
                         lambda c: self.xn[:, c, :], lambda c: ("xn", c), "b_g", TT)
            sk = [("stage", 0), ("stage", 1)]
            self.proj_out(self.wq_d, lambda c: self.st16[:, c, :], lambda c: sk, QSCALE, "act")
            self.dma(self.qT_d[:, :, cols].rearrange("h e t -> e h t"), self.st16[:, :, :], sk, [], "qo")

    def att_loads(self, h):
        s = h % 2
        hs = slice(h * 128, (h + 1) * 128)
        for dst, prev, own, key, dk in ((self.kTs[s], self.k_prev, self.k_own, ("kTs", s), "kall"),
                                        (self.vTs[s], self.v_prev, self.v_own, ("vTs", s), "vall")):
            self.dma(dst[:, 0:NTOK].rearrange("p (i t) -> p i t", i=NTILE),
                     prev[:, hs, :].rearrange("i e t -> e i t"), [("dram", dk)], [key], ("kl", s))
            self.dma(dst[:, NTOK:2 * NTOK].rearrange("p (i t) -> p i t", i=NTILE),
                     own[:, hs, :].rearrange("i e t -> e i t"), [], [key], ("kl", s))
        self.dma(self.qTs[s][:], self.qT_d[h], [], [("qTs", s)], ("ql", s))

    def att_vtiles(self, h):
        s = h % 2
        vT = self.vTs[s]
        jobs = []
        cols = [slice(NTOK + 128 * kb, NTOK + 128 * (kb + 1)) for kb in range(-1, 16)]
        jobs.append((self.V1[s].rearrange("p t e -> p (t e)"), cols))
        cols = []
        for r in range(4):
            for kb in range(-1, 4):
                st = NTOK + r + 512 * kb
                cols.append(slice(st, st + 509, 4))
        jobs.append((self.V4[s].rearrange("p r t e -> p (r t e)"), cols))
        cols = []
        for r in range(16):
            for kb in range(-1, 1):
                st = NTOK * (kb + 1) + r
                cols.append(slice(st, st + 2033, 16))
        jobs.append((self.V16[s].rearrange("p r t e -> p (r t e)"), cols))
        n = 0
        for dest, cols in jobs:
            for c0 in range(0, len(cols), 8):
                grp = cols[c0:c0 + 8]
                b = n % 4
                n += 1
                pb = self.ps[b][:].bitcast(BF16)
                for k, cs in enumerate(grp):
                    self.tr(pb[:, 128 * k:128 * (k + 1)], vT[:, cs], self.identb[:],
                            [("vTs", s), "identb"], [("ps", b)])
                self.cp(dest[:, 128 * c0:128 * (c0 + len(grp))], pb[:, 0:128 * len(grp)], [("ps", b)],
                        [("V", s)], eng=("act" if n % 2 else "dve"))

    def att_groups(self, s):
        kT, qT = self.kTs[s], self.qTs[s]
        groups = []
        for g in range(4):
            blocks = []
            for bb in range(4):
                nb = 4 * g + bb
                blocks.append((kT[:, NTOK + 128 * (nb - 1):NTOK + 128 * nb],
                               kT[:, NTOK + 128 * nb:NTOK + 128 * (nb + 1)],
                               qT[:, 128 * nb:128 * (nb + 1)],
                               self.V1[s][:, nb, :], self.V1[s][:, nb + 1, :]))
            groups.append((self.TA if g == 0 else self.TB, blocks,
                           (lambda a, g=g: a[:, 512 * g:512 * (g + 1)]), 1))
        for r in range(4):
            blocks = []
            for nb in range(4):
                kb0 = NTOK + r + 512 * (nb - 1)
                kb1 = NTOK + r + 512 * nb
                blocks.append((kT[:, kb0:kb0 + 509:4], kT[:, kb1:kb1 + 509:4],
                               qT[:, r + 512 * nb:r + 512 * nb + 509:4],
                               self.V4[s][:, r, nb, :], self.V4[s][:, r, nb + 1, :]))
            groups.append((self.TA, blocks, (lambda a, r=r: a[:, 512 * r:512 * (r + 1)]), 4))
        for g in range(4):
            blocks = []
            for bb in range(4):
                r = 4 * g + bb
                blocks.append((kT[:, r:NTOK:16], kT[:, NTOK + r:2 * NTOK:16], qT[:, r:NTOK:16],
                               self.V16[s][:, r, 0, :], self.V16[s][:, r, 1, :]))
            groups.append((self.TD, blocks, (lambda a, g=g: a[:, 512 * g:512 * (g + 1)]), 16))
        return groups

    def att_scores(self, h, gi, grp):
        s = h % 2
        Tx, blocks, accf, d = grp
        par = gi % 2
        X, Y = self.ps[par * 2], self.ps[par * 2 + 1]
        for bb, (kp, kc, q, vp, vc) in enumerate(blocks):
            self.mm(X[:, 128 * bb:128 * (bb + 1)], kp, q, True, True, [("kTs", s), ("qTs", s)], [("ps", par * 2)])
        for bb, (kp, kc, q, vp, vc) in enumerate(blocks):
            self.mm(Y[:, 128 * bb:128 * (bb + 1)], kc, q, True, True, [("kTs", s), ("qTs", s)],
                    [("ps", par * 2 + 1)])
        slope = 2.0 ** (-8.0 * (h + 1) / H)
        for w, (T, Pb) in enumerate(((Tx, X), (self.TC, Y))):
            scb, ptb = self.sc[par][w], self.pT[par][w]
            self.stt(scb[:], T[:], -slope * d, Pb[:], ALU.mult, ALU.add, ["T", ("ps", par * 2 + w)],
                     [("sc", par, w)])
            self.actv(ptb[:], scb[:], AF.Exp, [("sc", par, w)], [("pT", par, w)])

    def att_pv(self, h, gi, grp):
        s = h % 2
        Tx, blocks, accf, d = grp
        par = gi % 2
        N_, Dn = self.ps[4 + par * 2], self.ps[5 + par * 2]
        pX, pY = self.pT[par][0], self.pT[par][1]
        for bb, (kp, kc, q, vp, vc) in enumerate(blocks):
            cs = slice(128 * bb, 128 * (bb + 1))
            self.mm(N_[:, cs], vp, pX[:, cs], True, False, [("V", s), ("pT", par, 0)], [("ps", 4 + par * 2)])
            self.mm(N_[:, cs], vc, pY[:, cs], False, True, [("V", s), ("pT", par, 1)], [("ps", 4 + par * 2)])
        for bb in range(4):
            cs = slice(128 * bb, 128 * (bb + 1))
            self.mm(Dn[:, cs], self.onesb[:], pX[:, cs], True, False, ["onesb", ("pT", par, 0)],
                    [("ps", 5 + par * 2)])
            self.mm(Dn[:, cs], self.onesb[:], pY[:, cs], False, True, ["onesb", ("pT", par, 1)],
                    [("ps", 5 + par * 2)])
        br = gi // 4
        an, ad = accf(self.accn[br]), accf(self.accd[br])
        pn, pd = N_[:], Dn[:]
        self.cp(an, pn, [("ps", 4 + par * 2)], [("accn", br, gi % 4)], eng="act")
        self.cp(ad, pd, [("ps", 5 + par * 2)], [("accd", br, gi % 4)], eng=("act" if gi % 2 else "dve"))

    def att_finalize(self, h):
        s = h % 2
        nk = lambda br: [("accn", br, g) for g in range(4)]
        dk = lambda br: [("accd", br, g) for g in range(4)]
        v4 = lambda a: a[:].rearrange("p (r i) -> p i r", r=4)
        n4 = lambda a: a[:].rearrange("p (i r) -> p i r", r=4)
        v16 = lambda a: a[:].rearrange("p (r q) -> p q r", r=16)
        n16 = lambda a: a[:].rearrange("p (q r) -> p q r", r=16)
        for g in range(4):
            qs = slice(128 * g, 128 * (g + 1))
            self.tt(n4(self.nsum)[:, qs, :], n4(self.accn[0])[:, qs, :], v4(self.accn[1])[:, qs, :], ALU.add,
                    [("accn", 0, g)] + nk(1), [("nsum", g)], eng="pool")
            self.tt(n4(self.dsum)[:, qs, :], n4(self.accd[0])[:, qs, :], v4(self.accd[1])[:, qs, :], ALU.add,
                    [("accd", 0, g)] + dk(1), [("dsum", g)], eng="pool")
        sg = lambda nm: [(nm, g) for g in range(4)]
        self.tt(n16(self.dsum), n16(self.dsum), v16(self.accd[2]), ALU.add, sg("dsum") + dk(2), sg("dsum"),
                eng="pool")
        self.tt(n16(self.nsum), n16(self.nsum), v16(self.accn[2]), ALU.add, sg("nsum") + nk(2), sg("nsum"),
                eng="pool")

    def att_finalize_b(self, h, part):
        s = h % 2
        qs = slice(512 * part, 512 * (part + 1))
        self.actv(self.rden[:, qs], self.dsum[:, qs], AF.Ln, [("dsum", part)], [("rden", part)])
        self.actv(self.rden[:, qs], self.rden[:, qs], AF.Exp, [("rden", part)], [("rden", part)], scale=-1.0)
        if part == 3:
            sg = lambda nm: [(nm, g) for g in range(4)]
            ao = self.atto[s]
            self.tt(ao[:], self.nsum[:], self.rden[:], ALU.mult, sg("nsum") + sg("rden"), [("atto", s)],
                    eng="pool")
            self.dma(self.attT_d[h], ao[:], [("atto", s)], [], ("ao", s))

    def phaseB2(self):
        P = self.P
        P.fence()
        self.dma(self.TB[:], self.tc_d[:, 0, :], [], ["T"], "c2")
        self.dma(self.TC[:], self.tc_d[:, 1, :], [], ["T"], "c2")
        self.cp(self.TA[:], self.TB[:], ["T"], ["TA"])
        self.ts(self.TD[:], self.TB[:], self.vcol("pm"), None, ALU.add, None, ["T", "vecs"], ["T"])
        self.ts(self.TA[:, 0:128], self.TB[:, 0:128], self.vcol("pm"), None, ALU.add, None,
                ["T", "TA", "vecs"], ["T"])
        self.att_loads(0)
        for h in range(H):
            s = h % 2
            if h + 1 < H:
                self.att_loads(h + 1)
            groups = self.att_groups(s)
            self.att_vtiles(h)
            self.att_scores(h, 0, groups[0])
            for gi in range(len(groups)):
                if gi + 1 < len(groups):
                    self.att_scores(h, gi + 1, groups[gi + 1])
                self.att_pv(h, gi, groups[gi])
                if 4 <= gi < 8 and h > 0:
                    self.att_finalize_b(h - 1, gi - 4)
            self.att_finalize(h)
        for part in range(4):
            self.att_finalize_b(H - 1, part)
        P.fence()

    def phaseB3(self):
        for i in range(NTILE):
            cols = slice(i * TT, (i + 1) * TT)
            sk = [("stage", 0), ("stage", 1)]
            self.dma(self.hT[:, :, :], self.h1T_d[:, :, cols].rearrange("c p t -> p c t"), [("dram", "h1", i)],
                     [("hT", c) for c in range(16)], "h1i")
            self.dma(self.st16[:, :, :], self.attT_d[:, :, cols].rearrange("h e t -> e h t"), [], sk, "ati")
            self.proj_resid(self.wo_d, lambda c: self.st16[:, c, :], lambda c: ("stage", c // 8), None)
            self.ffn(self.g1_d, self.u1_d, self.d1_d, "f_g1")
            self.rmsnorm(lambda c: self.hT[:, c, :], lambda c: ("hT", c),
                         lambda c: self.v[:, c, :], lambda c: ("v", c), "fin_g", TT)
            for tb in range(4):
                xs = self.xs[tb % 2]
                for cg in range(4):
                    b = self.bank()
                    for cc in range(4):
                        c = cg * 4 + cc
                        self.tr(self.ps[b][:, cc * 128:(cc + 1) * 128], self.v[:, c, tb * 128:(tb + 1) * 128],
                                self.ident[:], [("v", c), "ident"], [("ps", b)])
                    self.cp(xs[:, cg * 512:(cg + 1) * 512], self.ps[b][:], [("ps", b)], [("stage", tb % 2)],
                            eng=("act" if cg % 2 else "dve"))
                r0 = i * TT + tb * 128
                self.dma(self.out_d[r0:r0 + 128, :], xs, [("stage", tb % 2)], [], ("oo", tb % 2))

    def build(self):
        self.setup()
        if "A" in self.phase:
            self.phaseA()
        if "B" in self.phase:
            if "A" not in self.phase:
                self.phaseB1()
            self.phaseB2()
            self.phaseB3()
        self.P.emit()
        return self.nc


_NC_CACHE = {}


def get_nc(phase):
    if phase not in _NC_CACHE:
        _NC_CACHE[phase] = Builder(phase).build()
    return _NC_CACHE[phase]


def kernel(**inp):
    inp = {k: np.asarray(v) for k, v in inp.items()}
    x = inp["x"]
    ident, tconst = make_consts()
    ncores = 8
    c32 = lambda a: np.ascontiguousarray(a, dtype=np.float32)
    w = {"conv_w1": c32(inp["conv_w1"][0]), "conv_w2": c32(inp["conv_w2"][0]), "w_k": c32(inp["w_k"]),
         "w_v": c32(inp["w_v"]), "gate0": c32(inp["ffn_w_gate"][0]), "up0": c32(inp["ffn_w_up"][0]),
         "down0": c32(inp["ffn_w_down"][0]),
         "w_q": c32(inp["w_q"][0]), "w_o": c32(inp["w_o"][0]), "gate1": c32(inp["ffn_w_gate"][1]),
         "up1": c32(inp["ffn_w_up"][1]), "down1": c32(inp["ffn_w_down"][1]), "tconst": tconst,
         "ident": ident}
    maps = []
    for core in range(ncores):
        b, half = core // 2, core % 2
        t0 = half * NTOK
        xo = c32(x[b, t0:t0 + NTOK])
        xh = c32(x[b, t0 - HALO:t0]) if half == 1 else np.zeros((HALO, D), np.float32)
        m = {"x": xo, "xh": xh, "vecs": make_vecs(inp, half)}
        m.update(w)
        maps.append(m)
    res = run_bass_kernel_spmd(get_nc("AB"), maps, core_ids=list(range(ncores)))
    out = np.empty((4, SEQ, D), np.float32)
    for core in range(ncores):
        b, half = core // 2, core % 2
        out[b, half * NTOK:(half + 1) * NTOK] = res.results[core]["out"]
    return out
```

```python
import contextlib
import numpy as np
import ml_dtypes
import concourse.bass as bass
import concourse.mybir as mybir
from concourse.bass_utils import run_bass_kernel_spmd

F32 = mybir.dt.float32
BF16 = mybir.dt.bfloat16
AF = mybir.ActivationFunctionType
ALU = mybir.AluOpType

D = 2048
FF = 5632
NF = FF // 128
SEQ = 4096
NTOK = 2048
TT = 512
NTILE = NTOK // TT
H = 16
CW = 31
HALO = 32
RMS_EPS = 1e-6
LN_EPS = 1e-5
BIG = 30000.0
QSCALE = 128.0 ** -0.5

ENGS = ("pe", "act", "dve", "pool", "sp")
import os
DEBUG_NOCC = bool(os.environ.get("KDEBUG_NOCC"))
NO_PRECONV = False


class Op:
    __slots__ = ("eng", "idx", "fn", "deps", "signal", "dma_key", "dma_cnt", "cum", "inc")

    def __init__(self, eng, idx, fn):
        self.eng = eng
        self.idx = idx
        self.fn = fn
        self.deps = []
        self.signal = False
        self.dma_key = None
        self.dma_cnt = 0
        self.cum = 0
        self.inc = 16


class Prog:
    def __init__(self, nc):
        self.nc = nc
        self.ops = {e: [] for e in ENGS}
        self.last_w = {}
        self.readers = {}
        self.dma_cnt = {}
        self.dma_inc = {}

    def op(self, eng, fn, reads=(), writes=(), dma=None, inc=16):
        o = Op(eng, len(self.ops[eng]), fn)
        if dma is not None:
            o.dma_key = dma
            o.inc = inc
            self.dma_inc[dma] = inc
            self.dma_cnt[dma] = self.dma_cnt.get(dma, 0) + 1
            o.dma_cnt = self.dma_cnt[dma]
        deps = []
        for k in reads:
            w = self.last_w.get(k)
            if w is not None:
                deps.append(w)
        for k in writes:
            w = self.last_w.get(k)
            if w is not None:
                deps.append(w)
            for r in self.readers.get(k, {}).values():
                deps.append(r)
        seen = set()
        for d in deps:
            if d is o or id(d) in seen:
                continue
            seen.add(id(d))
            if d.dma_key is None and d.eng == eng:
                if eng in ("pe", "sp"):
                    continue
                if o.dma_key is None and o.idx - d.idx > 3:
                    continue
            o.deps.append(d)
            if d.dma_key is None:
                d.signal = True
        for k in reads:
            self.readers.setdefault(k, {})[(eng, o.dma_key)] = o
        for k in writes:
            self.last_w[k] = o
            self.readers[k] = {}
        self.ops[eng].append(o)
        return o

    def fence(self):
        lasts = []
        for e in ENGS:
            for o in reversed(self.ops[e]):
                if o.dma_key is None and o.fn is not None:
                    lasts.append(o)
                    break
        dmas = []
        for k, c in self.dma_cnt.items():
            p = Op("sp", -1, None)
            p.dma_key = k
            p.dma_cnt = c
            p.inc = self.dma_inc[k]
            dmas.append(p)
        for e in ENGS:
            o = Op(e, len(self.ops[e]), None)
            for d in lasts:
                if d.eng != e:
                    o.deps.append(d)
                    d.signal = True
            o.deps.extend(dmas)
            self.ops[e].append(o)
        self.last_w = {}
        self.readers = {}

    def emit(self):
        nc = self.nc
        with contextlib.ExitStack() as st:
            esem = {e: st.enter_context(nc.semaphore("s_" + e)) for e in ENGS}
            dsem = {}
            for n, k in enumerate(self.dma_cnt):
                dsem[k] = st.enter_context(nc.semaphore("d%d" % n))
            block = st.enter_context(nc.Block())
            for e in ENGS:
                c = 0
                for o in self.ops[e]:
                    if o.signal:
                        c += 1
                    o.cum = c

            def run(e, engine):
                waited = {}
                for o in self.ops[e]:
                    for d in o.deps:
                        if d.dma_key is not None:
                            s, v, key = dsem[d.dma_key], d.inc * d.dma_cnt, ("d", d.dma_key)
                        else:
                            s, v, key = esem[d.eng], d.cum, ("e", d.eng)
                        if waited.get(key, 0) >= v:
                            continue
                        waited[key] = v
                        engine.wait_ge(s, v)
                    if o.fn is None:
                        continue
                    ins = o.fn(engine)
                    if o.dma_key is not None:
                        ins.then_inc(dsem[o.dma_key], o.inc)
                    elif o.signal:
                        ins.then_inc(esem[e], 1)
                if e == "sp":
                    for k, c in self.dma_cnt.items():
                        if waited.get(("d", k), 0) < self.dma_inc[k] * c:
                            engine.wait_ge(dsem[k], self.dma_inc[k] * c)

            @block.tensor
            def _(eng):
                run("pe", eng)

            @block.scalar
            def _(eng):
                run("act", eng)

            @block.vector
            def _(eng):
                run("dve", eng)

            @block.gpsimd
            def _(eng):
                run("pool", eng)

            @block.sync
            def _(eng):
                run("sp", eng)


VEC_COLS = {}
_c = 0
for _n, _w in (("a_g", 16), ("b1", 32), ("dwb", 16), ("lng", 16), ("lnb", 16), ("b2", 16),
               ("kv_g", 16), ("b_g", 16), ("f_g0", 16), ("f_g1", 16), ("fin_g", 16),
               ("dw", 16 * CW), ("hm", 1), ("pm", 1)):
    VEC_COLS[_n] = (_c, _w)
    _c += _w
NV = _c


def _pack(v):
    return np.ascontiguousarray(np.asarray(v, np.float32).reshape(-1, 128).T)


def make_vecs(inp, half):
    vecs = np.zeros((128, NV), np.float32)

    def put(name, arr):
        s, w = VEC_COLS[name]
        vecs[:, s:s + w] = arr

    put("a_g", _pack(inp["a_norm_g"][0]))
    put("b1", _pack(inp["conv_b1"][0]))
    put("dwb", _pack(inp["conv_dw_b"][0]))
    put("lng", _pack(inp["conv_ln_g"][0]))
    put("lnb", _pack(inp["conv_ln_b"][0]))
    put("b2", _pack(inp["conv_b2"][0]))
    put("kv_g", _pack(inp["kv_norm_g"]))
    put("b_g", _pack(inp["b_norm_g"][0]))
    put("f_g0", _pack(inp["ffn_norm_g"][0]))
    put("f_g1", _pack(inp["ffn_norm_g"][1]))
    put("fin_g", _pack(inp["final_norm_g"]))
    dw = np.asarray(inp["conv_dw"][0], np.float32)
    dwp = dw.T.reshape(16, 128, CW).transpose(1, 0, 2).reshape(128, 16 * CW)
    put("dw", dwp)
    put("hm", np.full((128, 1), 1.0 if half == 1 else 0.0, np.float32))
    put("pm", np.full((128, 1), 0.0 if half == 1 else BIG, np.float32))
    return vecs


def make_consts():
    k = np.arange(128)[:, None]
    q = np.arange(128)[None, :]
    t0 = np.where(k >= q, q - k + 128, BIG).astype(np.float32)
    t1 = np.where(k <= q, q - k, BIG).astype(np.float32)
    tc = np.zeros((128, 2, 512), np.float32)
    tc[:, 0, :] = np.tile(t0, (1, 4))
    tc[:, 1, :] = np.tile(t1, (1, 4))
    return np.eye(128, dtype=np.float32), tc


class Builder:
    def __init__(self, phase):
        self.phase = phase
        nc = self.nc = bass.Bass("TRN2", target_bir_lowering=False)
        self.P = Prog(nc)
        self.nbank = 0
        self.nw = 0
        din = lambda n, s, dt=F32: nc.dram_tensor(n, s, dt, kind="ExternalInput").ap()
        dout = lambda n, s, dt=F32: nc.dram_tensor(n, s, dt, kind="ExternalOutput").ap()
        dint = lambda n, s, dt=F32: nc.dram_tensor(n, s, dt, kind="Internal").ap()
        self.vecs_d = din("vecs", [128, NV])
        self.ident_d = din("ident", [128, 128])
        hasA = "A" in phase
        hasB = "B" in phase
        if hasA:
            self.x_d = din("x", [NTOK, D])
            self.xh_d = din("xh", [HALO, D])
            self.w1_d = din("conv_w1", [D, 2 * D])
            self.w2_d = din("conv_w2", [D, D])
            self.wk_d = din("w_k", [D, D])
            self.wv_d = din("w_v", [D, D])
            self.g0_d = din("gate0", [D, FF])
            self.u0_d = din("up0", [D, FF])
            self.d0_d = din("down0", [FF, D])
        if hasB:
            self.tc_d = din("tconst", [128, 2, 512])
            self.wq_d = din("w_q", [D, D])
            self.wo_d = din("w_o", [D, D])
            self.g1_d = din("gate1", [D, FF])
            self.u1_d = din("up1", [D, FF])
            self.d1_d = din("down1", [FF, D])
            self.out_d = dout("out", [NTOK, D])
            self.qT_d = dint("qT", [H, 128, NTOK], BF16)
            self.attT_d = dint("attT", [H, 128, NTOK], BF16)
        KSH = [NTILE, H * 128, TT]
        assert phase == "AB", "only the fused program is supported"
        self.h1T_d = dint("h1T", [16, 128, NTOK])
        self.wscr = dint("wscr", [96, 128, 16 * 512], BF16)
        self.wmap = {}
        self.preconv = False
        self.preq = []
        self.k_own = dint("k_own", KSH, BF16)
        self.v_own = dint("v_own", KSH, BF16)
        self.k_all = dint("k_all", [NTILE, 2 * H * 128, TT], BF16)
        self.v_all = dint("v_all", [NTILE, 2 * H * 128, TT], BF16)
        self.k_prev = self.k_all[:, 0:H * 128, :]
        self.v_prev = self.v_all[:, 0:H * 128, :]
        self.alloc()

    def alloc(self):
        nc = self.nc
        self.ps = [nc.alloc_psum_tensor("ps%d" % i, [128, 512], F32) for i in range(8)]
        off = [0]
        NB = 101 * 1024
        arena = self.arena = nc.alloc_sbuf_tensor("arena", [128, NB], BF16)

        def carve(nbytes, dt=BF16):
            n = nbytes // 2
            a = arena[:, off[0]:off[0] + n]
            off[0] += n
            assert off[0] <= NB, off[0]
            return a.bitcast(F32) if dt == F32 else a

        self.carve = carve
        self.vecs = nc.alloc_sbuf_tensor("vecs_sb", [128, NV], F32)
        self.ident = nc.alloc_sbuf_tensor("ident_sb", [128, 128], F32)
        self.identb = nc.alloc_sbuf_tensor("identb", [128, 128], BF16)
        self.onesb = nc.alloc_sbuf_tensor("onesb", [128, 128], BF16)
        self.onesf = nc.alloc_sbuf_tensor("onesf", [128, 128], F32)
        self.eps_rms = nc.alloc_sbuf_tensor("eps_rms", [128, 1], F32)
        self.eps_ln = nc.alloc_sbuf_tensor("eps_ln", [128, 1], F32)
        self.m0 = off[0]
        self.hT = carve(16 * 512 * 4, F32).rearrange("p (c t) -> p c t", c=16)
        self.xn = carve(16 * 512 * 2).rearrange("p (c t) -> p c t", c=16)
        u0 = off[0]
        self.u = carve(16 * 544 * 2).rearrange("p (c t) -> p c t", c=16)
        self.v = carve(16 * 512 * 4, F32).rearrange("p (c t) -> p c t", c=16)
        u1 = off[0]
        self.act = arena[:, u0:u0 + NF * 512].rearrange("p (c t) -> p c t", c=NF)
        assert u0 + NF * 512 <= u1
        s0 = off[0]
        stage = carve(16 * 512 * 2)
        self.stage = stage
        self.st16 = stage.rearrange("p (c t) -> p c t", c=16)
        self.vst = stage.rearrange("p (b f) -> p b f", b=4)
        self.xs = [arena[:, s0 + i * 4096:s0 + (i + 1) * 4096].bitcast(F32) for i in range(2)]
        self.wbuf = [carve(16 * 512 * 2).rearrange("p (k n) -> p k n", k=16) for _ in range(3)]
        self.dg = [carve(CW * 128 * 2).rearrange("p (j m) -> p j m", j=CW) for _ in range(2)]
        self.sq = [carve(512 * 2) for _ in range(4)]
        self.tmp = [carve(512 * 4, F32) for _ in range(3)]
        self.rstd = carve(512 * 4, F32)
        self.lnA = carve(512 * 4, F32)
        self.lnB = carve(512 * 4, F32)
        self.mean = carve(512 * 4, F32)
        self.hTh = carve(16 * HALO * 4, F32).rearrange("p (c t) -> p c t", c=16)
        self.xnh = carve(16 * HALO * 2).rearrange("p (c t) -> p c t", c=16)
        self.uhs = carve(16 * HALO * 2).rearrange("p (c t) -> p c t", c=16)
        self.m1 = off[0]
        if "B" in self.phase:
            off[0] = self.m0
            self.kTs = [carve(2 * NTOK * 2) for _ in range(2)]
            self.qTs = [carve(NTOK * 2) for _ in range(2)]
            self.V1 = [carve(17 * 128 * 2).rearrange("p (t e) -> p t e", e=128) for _ in range(2)]
            self.V4 = [carve(20 * 128 * 2).rearrange("p (r t e) -> p r t e", r=4, e=128) for _ in range(2)]
            self.V16 = [carve(32 * 128 * 2).rearrange("p (r t e) -> p r t e", r=16, e=128) for _ in range(2)]
            self.vTs = [carve(2 * NTOK * 2) for _ in range(2)]
            self.accn = [carve(NTOK * 4, F32) for _ in range(3)]
            self.accd = [carve(NTOK * 4, F32) for _ in range(3)]
            self.nsum = carve(NTOK * 4, F32)
            self.dsum = carve(NTOK * 4, F32)
            self.rden = carve(NTOK * 4, F32)
            self.atto = [carve(NTOK * 2) for _ in range(2)]
            self.sc = [[carve(512 * 4, F32) for _ in range(2)] for _ in range(2)]
            self.pT = [[carve(512 * 2) for _ in range(2)] for _ in range(2)]
            self.TA = carve(512 * 4, F32)
            self.TB = carve(512 * 4, F32)
            self.TD = carve(512 * 4, F32)
            self.TC = carve(512 * 4, F32)
            assert off[0] <= self.m1, (off[0], self.m1)
            off[0] = self.m1

    def vcol(self, name, c=0, w=1):
        s, _ = VEC_COLS[name]
        return self.vecs[:, s + c:s + c + w]

    def bank(self):
        b = self.nbank % 6
        self.nbank += 1
        return b

    def mm(self, out, lhsT, rhs, start, stop, reads, writes):
        self.P.op("pe", lambda e: e.matmul(out, lhsT=lhsT, rhs=rhs, start=start, stop=stop),
                  reads=reads, writes=writes)

    def mm_items(self, items, src, skey, n, dc_outer):
        if dc_outer:
            for dc in range(16):
                for p, b, cs in items:
                    self.mm(self.ps[p][:, 0:n], self.wbuf[b][:, dc, cs], src(dc), dc == 0, dc == 15,
                            [("w", b), skey(dc)], [("ps", p)])
        else:
            for p, b, cs in items:
                for dc in range(16):
                    self.mm(self.ps[p][:, 0:n], self.wbuf[b][:, dc, cs], src(dc), dc == 0, dc == 15,
                            [("w", b), skey(dc)], [("ps", p)])

    def tr(self, out, in_, ident, reads, writes):
        self.P.op("pe", lambda e: e.transpose(out=out, in_=in_, identity=ident), reads=reads, writes=writes)

    def actv(self, out, in_, func, reads, writes, bias=0.0, scale=1.0):
        self.P.op("act", lambda e: e.activation(out=out, in_=in_, func=func, bias=bias, scale=scale),
                  reads=reads, writes=writes)

    def stt(self, out, in0, scalar, in1, op0, op1, reads, writes, eng="dve"):
        self.P.op(eng, lambda e: e.scalar_tensor_tensor(out=out, in0=in0, scalar=scalar, in1=in1, op0=op0, op1=op1),
                  reads=reads, writes=writes)

    def tt(self, out, in0, in1, op, reads, writes, eng="dve"):
        self.P.op(eng, lambda e: e.tensor_tensor(out=out, in0=in0, in1=in1, op=op), reads=reads, writes=writes)

    def ts(self, out, in0, s1, s2, op0, op1, reads, writes, eng="dve"):
        if s2 is None:
            self.P.op(eng, lambda e: e.tensor_scalar(out=out, in0=in0, scalar1=s1, scalar2=None, op0=op0),
                      reads=reads, writes=writes)
        else:
            self.P.op(eng, lambda e: e.tensor_scalar(out=out, in0=in0, scalar1=s1, scalar2=s2, op0=op0, op1=op1),
                      reads=reads, writes=writes)

    def cp(self, out, in_, reads, writes, eng="dve"):
        if eng == "act":
            self.P.op("act", lambda e: e.copy(out=out, in_=in_), reads=reads, writes=writes)
        else:
            self.P.op(eng, lambda e: e.tensor_copy(out=out, in_=in_), reads=reads, writes=writes)

    def dma(self, out, in_, reads, writes, key, eng="sp"):
        return self.P.op(eng, lambda e: e.dma_start(out=out, in_=in_), reads=reads, writes=writes, dma=key)

    def wload(self, src, nk=16, wid=None):
        b = self.nw % 3
        self.nw += 1
        dst = self.wbuf[b][:, 0:nk, :]
        if wid not in self.wmap:
            idx = self.wmap[wid] = len(self.wmap)
            self.dma(dst, src.rearrange("(k p) n -> p k n", p=128), reads=[], writes=[("w", b)],
                     key=("w", b), eng="pool")
            self.dma(self.wscr[idx][:, 0:nk * 512].rearrange("p (k n) -> p k n", k=nk), dst,
                     reads=[("w", b)], writes=[("dram", "wscr", idx)], key="wst")
        else:
            idx = self.wmap[wid]
            self.dma(dst, self.wscr[idx][:, 0:nk * 512].rearrange("p (k n) -> p k n", k=nk),
                     reads=[("dram", "wscr", idx)], writes=[("w", b)], key=("w", b), eng="pool")
        return b

    def plan_preconvert(self):
        q = []
        for j in range(4):
            q.append((self.wo_d[:, 512 * j:512 * j + 512], 16, (self.wo_d.name, j)))
        for j in range(NF // 4):
            q.append((self.g1_d[:, 512 * j:512 * j + 512], 16, (self.g1_d.name, j)))
            q.append((self.u1_d[:, 512 * j:512 * j + 512], 16, (self.u1_d.name, j)))
        for qq in range(4):
            for g in range(3):
                nk = 16 if g < 2 else NF - 32
                q.append((self.d1_d[g * 2048:g * 2048 + nk * 128, 512 * qq:512 * qq + 512], nk,
                          (self.d1_d.name, g, qq)))
        self.preq = q
        q0 = []
        for j in range(4):
            q0.append((self.g0_d[:, 512 * j:512 * j + 512], 16, (self.g0_d.name, j)))
            q0.append((self.u0_d[:, 512 * j:512 * j + 512], 16, (self.u0_d.name, j)))
        self.preq0 = q0

    def preconvert_one(self, queue=None):
        queue = self.preq if queue is None else queue
        if not queue or NO_PRECONV:
            return
        src, nk, wid = queue.pop(0)
        idx = self.wmap[wid] = len(self.wmap)
        self.dma(self.wscr[idx][:, 0:nk * 512].rearrange("p (k n) -> p k n", k=nk),
                 src.rearrange("(k p) n -> p k n", p=128), reads=[], writes=[("dram", "wscr", idx)],
                 key="wpc", eng="pool")

    def setup(self):
        self.dma(self.vecs[:], self.vecs_d, [], ["vecs"], "c0")
        self.dma(self.ident[:], self.ident_d, [], ["ident"], "c1")
        self.cp(self.identb[:], self.ident[:], ["ident"], ["identb"])
        self.P.op("dve", lambda e: e.memset(self.onesb[:], 1.0), writes=["onesb"])
        self.P.op("dve", lambda e: e.memset(self.onesf[:], 1.0), writes=["onesf"])
        self.P.op("dve", lambda e: e.memset(self.eps_rms[:], RMS_EPS), writes=["eps"])
        self.P.op("dve", lambda e: e.memset(self.eps_ln[:], LN_EPS), writes=["eps"])

    def rms_stat(self, c, src, skey, n=TT):
        S = self.ps[6]
        sq = self.sq[c % 4]
        self.actv(sq[:, 0:n], src(c), AF.Square, [skey(c)], [("sq", c % 4)])
        self.mm(S[:, 0:n], self.onesb[:], sq[:, 0:n], c == 0, c == 15, [("sq", c % 4), "onesb"], [("ps", 6)])

    def rmsnorm(self, src, skey, dst, dkey, gname, n, stats="compute"):
        S = self.ps[6]
        r = self.rstd[:, 0:n]
        if stats == "compute":
            for c in range(16):
                self.rms_stat(c, src, skey, n)
        if stats != "reuse":
            self.actv(r, S[:, 0:n], AF.Sqrt, [("ps", 6)], ["rstd"], bias=self.eps_rms[:], scale=1.0 / D)
            self.P.op("dve", lambda e: e.reciprocal(out=r, in_=r), reads=["rstd"], writes=["rstd"])
        for c in range(16):
            self.stt(dst(c), src(c), self.vcol(gname, c), r, ALU.mult, ALU.mult,
                     [skey(c), "rstd", "vecs"], [dkey(c)])

    def load_x_tile(self, i):
        for tb in range(4):
            xs = self.xs[tb % 2]
            r0 = i * TT + tb * 128
            self.dma(xs, self.x_d[r0:r0 + 128, :], [], [("stage", tb % 2)], ("xs", tb % 2))
            for cg in range(4):
                b = self.bank()
                for cc in range(4):
                    c = cg * 4 + cc
                    self.tr(self.ps[b][:, cc * 128:(cc + 1) * 128], xs[:, c * 128:(c + 1) * 128], self.ident[:],
                            [("stage", tb % 2), "ident"], [("ps", b)])
                self.cp(self.hT[:, cg * 4:(cg + 1) * 4, tb * 128:(tb + 1) * 128],
                        self.ps[b][:].rearrange("p (c t) -> p c t", c=4), [("ps", b)],
                        [("hT", cg * 4 + k) for k in range(4)], eng=("act" if cg % 2 else "dve"))

    def load_halo(self):
        self.xsh = self.xs[1]
        self.dma(self.xsh[0:HALO, :], self.xh_d, [], [("stage", 1)], "xsh")
        for cg in range(4):
            b = self.bank()
            for cc in range(4):
                c = cg * 4 + cc
                self.tr(self.ps[b][:, cc * HALO:(cc + 1) * HALO], self.xsh[0:HALO, c * 128:(c + 1) * 128],
                        self.ident[0:HALO, 0:HALO], [("stage", 1), "ident"], [("ps", b)])
            self.cp(self.hTh[:, cg * 4:(cg + 1) * 4, :],
                    self.ps[b][:, 0:4 * HALO].rearrange("p (c t) -> p c t", c=4), [("ps", b)],
                    [("hTh", cg * 4 + k) for k in range(4)])

    def w1_glu(self, src, skey, n, dst, dkey, halo):
        for j in range(4):
            ba = self.wload(self.w1_d[:, 512 * j:512 * j + 512], wid=("w1a", j))
            bg = self.wload(self.w1_d[:, D + 512 * j:D + 512 * j + 512], wid=("w1g", j))
            for half in range(2):
                ccs = (2 * half, 2 * half + 1)
                bk = {cc: (self.bank(), self.bank()) for cc in ccs}
                items = []
                for cc in ccs:
                    cs = slice(cc * 128, (cc + 1) * 128)
                    items += [(bk[cc][0], ba, cs), (bk[cc][1], bg, cs)]
                self.mm_items(items, src, skey, n, dc_outer=(j == 0 and half == 0))
                for cc in ccs:
                    c = 4 * j + cc
                    pa, pg = bk[cc]
                    t = self.tmp[c % 3]
                    self.actv(t[:, 0:n], self.ps[pg][:, 0:n], AF.Sigmoid, [("ps", pg), "vecs"],
                              [("tmp", c % 3)], bias=self.vcol("b1", 16 + c))
                    self.stt(dst(c), self.ps[pa][:, 0:n], self.vcol("b1", c), t[:, 0:n], ALU.add, ALU.mult,
                             [("ps", pa), ("tmp", c % 3), "vecs"], [dkey(c)])
                    if halo:
                        self.ts(dst(c), dst(c), self.vcol("hm"), None, ALU.mult, None, [dkey(c), "vecs"],
                                [dkey(c)])

    def conv_ln(self, i):
        S1, S2 = self.ps[6], self.ps[7]
        if i > 0:
            self.exchange(i - 1)
            self.cp(self.u[:, :, 0:HALO], self.uhs[:, :, :], ["uhs"], [("uh", c) for c in range(16)], eng="dve")
        for c in range(16):
            if i == 0 and c % 2 == 1:
                self.preconvert_one(self.preq0)
            elif i >= 1 and c % 4 == 3:
                self.preconvert_one()
            dg = self.dg[c % 2]
            s, _ = VEC_COLS["dw"]
            dwc = self.vecs[:, s + c * CW:s + (c + 1) * CW]
            self.tt(dg[:], self.identb[:].unsqueeze(1).broadcast_to([128, CW, 128]),
                    dwc.unsqueeze(2).broadcast_to([128, CW, 128]), ALU.mult,
                    ["identb", "vecs"], [("dg", c % 2)])
            b = self.bank()
            for j in range(CW):
                self.mm(self.ps[b][:], dg[:, j, :], self.u[:, c, HALO - (CW - 1 - j):HALO - (CW - 1 - j) + TT], j == 0, j == CW - 1,
                        [("dg", c % 2), ("u", c), ("uh", c)], [("ps", b)])
            self.actv(self.v[:, c, :], self.ps[b][:], AF.Identity, [("ps", b), "vecs"], [("v", c)],
                      bias=self.vcol("dwb", c))
            sq = self.sq[c % 4]
            self.actv(sq[:], self.ps[b][:], AF.Square, [("ps", b), "vecs"], [("sq", c % 4)],
                      bias=self.vcol("dwb", c))
            self.mm(S1[:], self.onesf[:], self.v[:, c, :], c == 0, c == 15, [("v", c), "onesf"], [("ps", 6)])
            self.mm(S2[:], self.onesb[:], sq[:], c == 0, c == 15, [("sq", c % 4), "onesb"], [("ps", 7)])
        if i < NTILE - 1:
            self.cp(self.uhs[:, :, :], self.u[:, :, TT:TT + HALO], [("u", c) for c in range(16)],
                    ["uhs"], eng="dve")
        mean, A, B = self.mean, self.lnA, self.lnB
        self.ts(mean[:], S1[:], 1.0 / D, None, ALU.mult, None, [("ps", 6)], ["mean"])
        self.tt(B[:], mean[:], mean[:], ALU.mult, ["mean"], ["lnB"])
        self.stt(A[:], S2[:], 1.0 / D, B[:], ALU.mult, ALU.subtract, [("ps", 7), "lnB"], ["lnA"])
        self.actv(A[:], A[:], AF.Sqrt, ["lnA"], ["lnA"], bias=self.eps_ln[:], scale=1.0)
        self.P.op("dve", lambda e: e.reciprocal(out=A[:], in_=A[:]), reads=["lnA"], writes=["lnA"])
        self.stt(B[:], mean[:], -1.0, A[:], ALU.mult, ALU.mult, ["mean", "lnA"], ["lnB"])
        for c in range(16):
            self.tt(self.v[:, c, :], self.v[:, c, :], A[:], ALU.mult, [("v", c), "lnA"], [("v", c)])
            if c > 0:
                self.ln_tail(c - 1, B)
        self.ln_tail(15, B)

    def ln_tail(self, c, B):
        if True:
            self.tt(self.v[:, c, :], self.v[:, c, :], B[:], ALU.add, [("v", c), "lnB"], [("v", c)])
            self.actv(self.xn[:, c, :], self.v[:, c, :], AF.Silu, [("v", c), "vecs"], [("xn", c)],
                      bias=self.vcol("lnb", c), scale=self.vcol("lng", c))

    def proj_resid(self, w_d, src, skey, bias_name):
        for j in range(4):
            b = self.wload(w_d[:, 512 * j:512 * j + 512], wid=(w_d.name, j))
            pbk = [self.bank() for _ in range(4)] if j == 0 else None
            if j == 0:
                self.mm_items([(pbk[cc], b, slice(cc * 128, (cc + 1) * 128)) for cc in range(4)], src, skey, TT, True)
            for cc in range(4):
                c = 4 * j + cc
                if j == 0:
                    p = pbk[cc]
                else:
                    p = self.bank()
                    self.mm_items([(p, b, slice(cc * 128, (cc + 1) * 128))], src, skey, TT, False)
                if bias_name is not None:
                    self.stt(self.hT[:, c, :], self.ps[p][:], self.vcol(bias_name, c), self.hT[:, c, :],
                             ALU.add, ALU.add, [("ps", p), ("hT", c), "vecs"], [("hT", c)])
                else:
                    self.tt(self.hT[:, c, :], self.ps[p][:], self.hT[:, c, :], ALU.add,
                            [("ps", p), ("hT", c)], [("hT", c)])
            if j > 0:
                for c in range(4 * (j - 1), 4 * j):
                    self.rms_stat(c, lambda c: self.hT[:, c, :], lambda c: ("hT", c))
        for c in range(12, 16):
            self.rms_stat(c, lambda c: self.hT[:, c, :], lambda c: ("hT", c))

    def proj_out(self, w_d, dst, dkey, scale, eng):
        for j in range(4):
            b = self.wload(w_d[:, 512 * j:512 * j + 512], wid=(w_d.name, j))
            xsrc, xkey = (lambda dc: self.xn[:, dc, :]), (lambda dc: ("xn", dc))
            pbk = [self.bank() for _ in range(4)] if j == 0 else None
            if j == 0:
                self.mm_items([(pbk[cc], b, slice(cc * 128, (cc + 1) * 128)) for cc in range(4)], xsrc, xkey, TT, True)
            for cc in range(4):
                c = 4 * j + cc
                if j == 0:
                    p = pbk[cc]
                else:
                    p = self.bank()
                    self.mm_items([(p, b, slice(cc * 128, (cc + 1) * 128))], xsrc, xkey, TT, False)
                if scale is not None:
                    self.actv(dst(c), self.ps[p][:], AF.Identity, [("ps", p)], dkey(c), scale=scale)
                else:
                    self.cp(dst(c), self.ps[p][:], [("ps", p)], dkey(c), eng=eng)

    def ffn(self, g_d, u_d, d_d, gname):
        self.rmsnorm(lambda c: self.hT[:, c, :], lambda c: ("hT", c),
                     lambda c: self.xn[:, c, :], lambda c: ("xn", c), gname, TT, stats="done")
        for j in range(NF // 4):
            bg = self.wload(g_d[:, 512 * j:512 * j + 512], wid=(g_d.name, j))
            bu = self.wload(u_d[:, 512 * j:512 * j + 512], wid=(u_d.name, j))
            if self.preconv and j % 2 == 0:
                self.preconvert_one()
            xsrc, xkey = (lambda dc: self.xn[:, dc, :]), (lambda dc: ("xn", dc))
            for half in range(2):
                ccs = (2 * half, 2 * half + 1)
                bk = {cc: (self.bank(), self.bank()) for cc in ccs}
                items = []
                for cc in ccs:
                    cs = slice(cc * 128, (cc + 1) * 128)
                    items += [(bk[cc][0], bg, cs), (bk[cc][1], bu, cs)]
                self.mm_items(items, xsrc, xkey, TT, dc_outer=(j == 0 and half == 0))
                for cc in ccs:
                    f = 4 * j + cc
                    pg, pu = bk[cc]
                    t = self.tmp[f % 3]
                    self.actv(t[:], self.ps[pg][:], AF.Silu, [("ps", pg)], [("tmp", f % 3)])
                    self.tt(self.act[:, f, :], t[:], self.ps[pu][:], ALU.mult, [("tmp", f % 3), ("ps", pu)],
                            [("act", f)])
        for q in range(4):
            banks = [self.bank() for _ in range(4)]
            for g in range(3):
                nk = 16 if g < 2 else NF - 32
                b = self.wload(d_d[g * 2048:g * 2048 + nk * 128, 512 * q:512 * q + 512], nk, wid=(d_d.name, g, q))
                if self.preconv and g == 0:
                    self.preconvert_one()
                for cc in range(4):
                    for fk in range(nk):
                        f = g * 16 + fk
                        self.mm(self.ps[banks[cc]][:], self.wbuf[b][:, fk, cc * 128:(cc + 1) * 128],
                                self.act[:, f, :], f == 0, f == NF - 1, [("w", b), ("act", f)],
                                [("ps", banks[cc])])
            if q > 0:
                for c in range(4 * (q - 1), 4 * q):
                    self.rms_stat(c, lambda c: self.hT[:, c, :], lambda c: ("hT", c))
            for cc in range(4):
                c = 4 * q + cc
                self.tt(self.hT[:, c, :], self.ps[banks[cc]][:], self.hT[:, c, :], ALU.add,
                        [("ps", banks[cc]), ("hT", c)], [("hT", c)])
        for c in range(12, 16):
            self.rms_stat(c, lambda c: self.hT[:, c, :], lambda c: ("hT", c))

    def phaseA(self):
        self.plan_preconvert()
        for i in range(NTILE):
            cols = slice(i * TT, (i + 1) * TT)
            self.load_x_tile(i)
            self.rmsnorm(lambda c: self.hT[:, c, :], lambda c: ("hT", c),
                         lambda c: self.xn[:, c, :], lambda c: ("xn", c), "a_g", TT)
            self.w1_glu(lambda c: self.xn[:, c, :], lambda c: ("xn", c), TT,
                        lambda c: self.u[:, c, HALO:HALO + TT], lambda c: ("u", c), False)
            if i == 0:
                self.load_halo()
                self.rmsnorm(lambda c: self.hTh[:, c, :], lambda c: ("hTh", c),
                             lambda c: self.xnh[:, c, :], lambda c: ("xnh", c), "a_g", HALO)
                self.w1_glu(lambda c: self.xnh[:, c, :], lambda c: ("xnh", c), HALO,
                            lambda c: self.u[:, c, 0:HALO], lambda c: ("uh", c), True)
            self.conv_ln(i)
            self.proj_resid(self.w2_d, lambda c: self.xn[:, c, :], lambda c: ("xn", c), "b2")
            self.preconv = (i >= 1 and not NO_PRECONV)
            self.ffn(self.g0_d, self.u0_d, self.d0_d, "f_g0")
            self.preconv = False
            self.dma(self.h1T_d[:, :, cols].rearrange("c p t -> p c t"), self.hT[:, :, :],
                     [("hT", c) for c in range(16)], [("dram", "h1", i)], "h1o")
            self.rmsnorm(lambda c: self.hT[:, c, :], lambda c: ("hT", c),
                         lambda c: self.xn[:, c, :], lambda c: ("xn", c), "kv_g", TT, stats="done")
            skh = lambda c: [("stage", c // 8)]
            for w_d, dst_d, nm, key, eng in ((self.wk_d, self.k_own[i], "k", "kto", "act"),
                                             (self.wv_d, self.v_own[i], "v", "vo", "dve")):
                self.proj_out(w_d, lambda c: self.st16[:, c, :], skh, None, eng)
                dv = dst_d.rearrange("(h e) t -> e h t", e=128)
                for hh in range(2):
                    self.dma(dv[:, 8 * hh:8 * hh + 8, :], self.st16[:, 8 * hh:8 * hh + 8, :], [("stage", hh)],
                             [("dram", nm, i)], key)
            if "B" in self.phase:
                self.rmsnorm(lambda c: self.hT[:, c, :], lambda c: ("hT", c),
                             lambda c: self.xn[:, c, :], lambda c: ("xn", c), "b_g", TT, stats="reuse")
                self.proj_out(self.wq_d, lambda c: self.st16[:, c, :], skh, QSCALE, "act")
                qv = self.qT_d[:, :, cols].rearrange("h e t -> e h t")
                for hh in range(2):
                    self.dma(qv[:, 8 * hh:8 * hh + 8, :], self.st16[:, 8 * hh:8 * hh + 8, :], [("stage", hh)], [],
                             "qo")
        self.exchange(NTILE - 1)

    def exchange(self, i):
        if self.phase != "AB" or DEBUG_NOCC:
            return
        rg = [[0, 1], [2, 3], [4, 5], [6, 7]]
        self.P.op("pool", lambda e: e.collective_compute(
            "AllGather", ALU.bypass, replica_groups=rg,
            ins=[self.k_own[i].rearrange("(a b) t -> a (b t)", b=4)],
            outs=[self.k_all[i].rearrange("(a b) t -> a (b t)", b=4)]),
            reads=[("dram", "k", i)], writes=[("dram", "kall")], dma="cc", inc=1)
        self.P.op("pool", lambda e: e.collective_compute(
            "AllGather", ALU.bypass, replica_groups=rg,
            ins=[self.v_own[i].rearrange("(a b) t -> a (b t)", b=4)],
            outs=[self.v_all[i].rearrange("(a b) t -> a (b t)", b=4)]),
            reads=[("dram", "v", i)], writes=[("dram", "vall")], dma="cc", inc=1)

    def phaseB1(self):
        for i in range(NTILE):
            cols = slice(i * TT, (i + 1) * TT)
            self.dma(self.hT[:, :, :], self.h1T_d[:, :, cols].rearrange("c p t -> p c t"), [("dram", "h1", i)],
                     [("hT", c) for c in range(16)], "h1i")
            self.rmsnorm(lambda c: self.hT[:, c, :], lambda c: ("hT", c),
                         lambda c: self.xn[:, c, :], lambda c: ("xn", c), "b_g", TT)
            sk = [("stage", 0), ("stage", 1)]
            self.proj_out(self.wq_d, lambda c: self.st16[:, c, :], lambda c: sk, QSCALE, "act")
            self.dma(self.qT_d[:, :, cols].rearrange("h e t -> e h t"), self.st16[:, :, :], sk, [], "qo")

    def att_loads(self, h):
        s = h % 2
        hs = slice(h * 128, (h + 1) * 128)
        for dst, prev, own, key, dk in ((self.kTs[s], self.k_prev, self.k_own, ("kTs", s), "kall"),
                                        (self.vTs[s], self.v_prev, self.v_own, ("vTs", s), "vall")):
            self.dma(dst[:, 0:NTOK].rearrange("p (i t) -> p i t", i=NTILE),
                     prev[:, hs, :].rearrange("i e t -> e i t"), [("dram", dk)], [key], ("kl", s))
            self.dma(dst[:, NTOK:2 * NTOK].rearrange("p (i t) -> p i t", i=NTILE),
                     own[:, hs, :].rearrange("i e t -> e i t"), [], [key], ("kl", s))
        self.dma(self.qTs[s][:], self.qT_d[h], [], [("qTs", s)], ("ql", s))

    def att_vtiles(self, h):
        s = h % 2
        vT = self.vTs[s]
        jobs = []
        cols = [slice(NTOK + 128 * kb, NTOK + 128 * (kb + 1)) for kb in range(-1, 16)]
        jobs.append((self.V1[s].rearrange("p t e -> p (t e)"), cols))
        cols = []
        for r in range(4):
            for kb in range(-1, 4):
                st = NTOK + r + 512 * kb
                cols.append(slice(st, st + 509, 4))
        jobs.append((self.V4[s].rearrange("p r t e -> p (r t e)"), cols))
        cols = []
        for r in range(16):
            for kb in range(-1, 1):
                st = NTOK * (kb + 1) + r
                cols.append(slice(st, st + 2033, 16))
        jobs.append((self.V16[s].rearrange("p r t e -> p (r t e)"), cols))
        n = 0
        for dest, cols in jobs:
            for c0 in range(0, len(cols), 8):
                grp = cols[c0:c0 + 8]
                b = n % 4
                n += 1
                pb = self.ps[b][:].bitcast(BF16)
                for k, cs in enumerate(grp):
                    self.tr(pb[:, 128 * k:128 * (k + 1)], vT[:, cs], self.identb[:],
                            [("vTs", s), "identb"], [("ps", b)])
                self.cp(dest[:, 128 * c0:128 * (c0 + len(grp))], pb[:, 0:128 * len(grp)], [("ps", b)],
                        [("V", s)], eng=("act" if n % 2 else "dve"))

    def att_groups(self, s):
        kT, qT = self.kTs[s], self.qTs[s]
        groups = []
        for g in range(4):
            blocks = []
            for bb in range(4):
                nb = 4 * g + bb
                blocks.append((kT[:, NTOK + 128 * (nb - 1):NTOK + 128 * nb],
                               kT[:, NTOK + 128 * nb:NTOK + 128 * (nb + 1)],
                               qT[:, 128 * nb:128 * (nb + 1)],
                               self.V1[s][:, nb, :], self.V1[s][:, nb + 1, :]))
            groups.append((self.TA if g == 0 else self.TB, blocks,
                           (lambda a, g=g: a[:, 512 * g:512 * (g + 1)]), 1))
        for r in range(4):
            blocks = []
            for nb in range(4):
                kb0 = NTOK + r + 512 * (nb - 1)
                kb1 = NTOK + r + 512 * nb
                blocks.append((kT[:, kb0:kb0 + 509:4], kT[:, kb1:kb1 + 509:4],
                               qT[:, r + 512 * nb:r + 512 * nb + 509:4],
                               self.V4[s][:, r, nb, :], self.V4[s][:, r, nb + 1, :]))
            groups.append((self.TA, blocks, (lambda a, r=r: a[:, 512 * r:512 * (r + 1)]), 4))
        for g in range(4):
            blocks = []
            for bb in range(4):
                r = 4 * g + bb
                blocks.append((kT[:, r:NTOK:16], kT[:, NTOK + r:2 * NTOK:16], qT[:, r:NTOK:16],
                               self.V16[s][:, r, 0, :], self.V16[s][:, r, 1, :]))
            groups.append((self.TD, blocks, (lambda a, g=g: a[:, 512 * g:512 * (g + 1)]), 16))
        return groups

    def att_scores(self, h, gi, grp):
        s = h % 2
        Tx, blocks, accf, d = grp
        par = gi % 2
        X, Y = self.ps[par * 2], self.ps[par * 2 + 1]
        for bb, (kp, kc, q, vp, vc) in enumerate(blocks):
            self.mm(X[:, 128 * bb:128 * (bb + 1)], kp, q, True, True, [("kTs", s), ("qTs", s)], [("ps", par * 2)])
        for bb, (kp, kc, q, vp, vc) in enumerate(blocks):
            self.mm(Y[:, 128 * bb:128 * (bb + 1)], kc, q, True, True, [("kTs", s), ("qTs", s)],
                    [("ps", par * 2 + 1)])
        slope = 2.0 ** (-8.0 * (h + 1) / H)
        for w, (T, Pb) in enumerate(((Tx, X), (self.TC, Y))):
            scb, ptb = self.sc[par][w], self.pT[par][w]
            self.stt(scb[:], T[:], -slope * d, Pb[:], ALU.mult, ALU.add, ["T", ("ps", par * 2 + w)],
                     [("sc", par, w)])
            self.actv(ptb[:], scb[:], AF.Exp, [("sc", par, w)], [("pT", par, w)])

    def att_pv(self, h, gi, grp):
        s = h % 2
        Tx, blocks, accf, d = grp
        par = gi % 2
        N_, Dn = self.ps[4 + par * 2], self.ps[5 + par * 2]
        pX, pY = self.pT[par][0], self.pT[par][1]
        for bb, (kp, kc, q, vp, vc) in enumerate(blocks):
            cs = slice(128 * bb, 128 * (bb + 1))
            self.mm(N_[:, cs], vp, pX[:, cs], True, False, [("V", s), ("pT", par, 0)], [("ps", 4 + par * 2)])
            self.mm(N_[:, cs], vc, pY[:, cs], False, True, [("V", s), ("pT", par, 1)], [("ps", 4 + par * 2)])
        for bb in range(4):
            cs = slice(128 * bb, 128 * (bb + 1))
            self.mm(Dn[:, cs], self.onesb[:], pX[:, cs], True, False, ["onesb", ("pT", par, 0)],
                    [("ps", 5 + par * 2)])
            self.mm(Dn[:, cs], self.onesb[:], pY[:, cs], False, True, ["onesb", ("pT", par, 1)],
                    [("ps", 5 + par * 2)])
        br = gi // 4
        an, ad = accf(self.accn[br]), accf(self.accd[br])
        pn, pd = N_[:], Dn[:]
        self.cp(an, pn, [("ps", 4 + par * 2)], [("accn", br, gi % 4)], eng="act")
        self.cp(ad, pd, [("ps", 5 + par * 2)], [("accd", br, gi % 4)], eng=("act" if gi % 2 else "dve"))

    def att_finalize(self, h):
        s = h % 2
        nk = lambda br: [("accn", br, g) for g in range(4)]
        dk = lambda br: [("accd", br, g) for g in range(4)]
        v4 = lambda a: a[:].rearrange("p (r i) -> p i r", r=4)
        n4 = lambda a: a[:].rearrange("p (i r) -> p i r", r=4)
        v16 = lambda a: a[:].rearrange("p (r q) -> p q r", r=16)
        n16 = lambda a: a[:].rearrange("p (q r) -> p q r", r=16)
        for g in range(4):
            qs = slice(128 * g, 128 * (g + 1))
            self.tt(n4(self.nsum)[:, qs, :], n4(self.accn[0])[:, qs, :], v4(self.accn[1])[:, qs, :], ALU.add,
                    [("accn", 0, g)] + nk(1), [("nsum", g)], eng="pool")
            self.tt(n4(self.dsum)[:, qs, :], n4(self.accd[0])[:, qs, :], v4(self.accd[1])[:, qs, :], ALU.add,
                    [("accd", 0, g)] + dk(1), [("dsum", g)], eng="pool")
        sg = lambda nm: [(nm, g) for g in range(4)]
        self.tt(n16(self.dsum), n16(self.dsum), v16(self.accd[2]), ALU.add, sg("dsum") + dk(2), sg("dsum"),
                eng="pool")
        self.tt(n16(self.nsum), n16(self.nsum), v16(self.accn[2]), ALU.add, sg("nsum") + nk(2), sg("nsum"),
                eng="pool")

    def att_finalize_b(self, h, part):
        s = h % 2
        qs = slice(512 * part, 512 * (part + 1))
        self.actv(self.rden[:, qs], self.dsum[:, qs], AF.Ln, [("dsum", part)], [("rden", part)])
        self.actv(self.rden[:, qs], self.rden[:, qs], AF.Exp, [("rden", part)], [("rden", part)], scale=-1.0)
        if part == 3:
            sg = lambda nm: [(nm, g) for g in range(4)]
            ao = self.atto[s]
            self.tt(ao[:], self.nsum[:], self.rden[:], ALU.mult, sg("nsum") + sg("rden"), [("atto", s)],
                    eng="pool")
            self.dma(self.attT_d[h], ao[:], [("atto", s)], [], ("ao", s))

    def phaseB2(self):
        P = self.P
        P.fence()
        self.dma(self.TB[:], self.tc_d[:, 0, :], [], ["T"], "c2")
        self.dma(self.TC[:], self.tc_d[:, 1, :], [], ["T"], "c2")
        self.cp(self.TA[:], self.TB[:], ["T"], ["TA"])
        self.ts(self.TD[:], self.TB[:], self.vcol("pm"), None, ALU.add, None, ["T", "vecs"], ["T"])
        self.ts(self.TA[:, 0:128], self.TB[:, 0:128], self.vcol("pm"), None, ALU.add, None,
                ["T", "TA", "vecs"], ["T"])
        self.att_loads(0)
        for h in range(H):
            s = h % 2
            if h + 1 < H:
                self.att_loads(h + 1)
            groups = self.att_groups(s)
            self.att_vtiles(h)
            self.att_scores(h, 0, groups[0])
            for gi in range(len(groups)):
                if gi + 1 < len(groups):
                    self.att_scores(h, gi + 1, groups[gi + 1])
                self.att_pv(h, gi, groups[gi])
                if 4 <= gi < 8 and h > 0:
                    self.att_finalize_b(h - 1, gi - 4)
            self.att_finalize(h)
        for part in range(4):
            self.att_finalize_b(H - 1, part)
        P.fence()

    def phaseB3(self):
        for i in range(NTILE):
            cols = slice(i * TT, (i + 1) * TT)
            sk = [("stage", 0), ("stage", 1)]
            self.dma(self.hT[:, :, :], self.h1T_d[:, :, cols].rearrange("c p t -> p c t"), [("dram", "h1", i)],
                     [("hT", c) for c in range(16)], "h1i")
            self.dma(self.st16[:, :, :], self.attT_d[:, :, cols].rearrange("h e t -> e h t"), [], sk, "ati")
            self.proj_resid(self.wo_d, lambda c: self.st16[:, c, :], lambda c: ("stage", c // 8), None)
            self.ffn(self.g1_d, self.u1_d, self.d1_d, "f_g1")
            self.rmsnorm(lambda c: self.hT[:, c, :], lambda c: ("hT", c),
                         lambda c: self.v[:, c, :], lambda c: ("v", c), "fin_g", TT, stats="done")
            for tb in range(4):
                xs = self.xs[tb % 2]
                for cg in range(4):
                    b = self.bank()
                    for cc in range(4):
                        c = cg * 4 + cc
                        self.tr(self.ps[b][:, cc * 128:(cc + 1) * 128], self.v[:, c, tb * 128:(tb + 1) * 128],
                                self.ident[:], [("v", c), "ident"], [("ps", b)])
                    self.cp(xs[:, cg * 512:(cg + 1) * 512], self.ps[b][:], [("ps", b)], [("stage", tb % 2)],
                            eng=("act" if cg % 2 else "dve"))
                r0 = i * TT + tb * 128
                self.dma(self.out_d[r0:r0 + 128, :], xs, [("stage", tb % 2)], [], ("oo", tb % 2))

    def build(self):
        self.setup()
        if "A" in self.phase:
            self.phaseA()
        if "B" in self.phase:
            if "A" not in self.phase:
                self.phaseB1()
            self.phaseB2()
            self.phaseB3()
        self.P.emit()
        return self.nc


_NC_CACHE = {}


def get_nc(phase):
    if phase not in _NC_CACHE:
        _NC_CACHE[phase] = Builder(phase).build()
    return _NC_CACHE[phase]


def kernel(**inp):
    inp = {k: np.asarray(v) for k, v in inp.items()}
    x = inp["x"]
    ident, tconst = make_consts()
    ncores = 8
    c32 = lambda a: np.ascontiguousarray(a, dtype=np.float32)
    w = {"conv_w1": c32(inp["conv_w1"][0]), "conv_w2": c32(inp["conv_w2"][0]), "w_k": c32(inp["w_k"]),
         "w_v": c32(inp["w_v"]), "gate0": c32(inp["ffn_w_gate"][0]), "up0": c32(inp["ffn_w_up"][0]),
         "down0": c32(inp["ffn_w_down"][0]),
         "w_q": c32(inp["w_q"][0]), "w_o": c32(inp["w_o"][0]), "gate1": c32(inp["ffn_w_gate"][1]),
         "up1": c32(inp["ffn_w_up"][1]), "down1": c32(inp["ffn_w_down"][1]), "tconst": tconst,
         "ident": ident}
    maps = []
    for core in range(ncores):
        b, half = core // 2, core % 2
        t0 = half * NTOK
        xo = c32(x[b, t0:t0 + NTOK])
        xh = c32(x[b, t0 - HALO:t0]) if half == 1 else np.zeros((HALO, D), np.float32)
        m = {"x": xo, "xh": xh, "vecs": make_vecs(inp, half)}
        m.update(w)
        maps.append(m)
    res = run_bass_kernel_spmd(get_nc("AB"), maps, core_ids=list(range(ncores)))
    out = np.empty((4, SEQ, D), np.float32)
    for core in range(ncores):
        b, half = core // 2, core % 2
        out[b, half * NTOK:(half + 1) * NTOK] = res.results[core]["out"]
    return out
```

```python
import contextlib
import numpy as np
import ml_dtypes
import concourse.bass as bass
import concourse.mybir as mybir
from concourse.bass_utils import run_bass_kernel_spmd

F32 = mybir.dt.float32
BF16 = mybir.dt.bfloat16
AF = mybir.ActivationFunctionType
ALU = mybir.AluOpType

D = 2048
FF = 5632
NF = FF // 128
SEQ = 4096
NTOK = 2048
TT = 512
NTILE = NTOK // TT
H = 16
CW = 31
HALO = 32
RMS_EPS = 1e-6
LN_EPS = 1e-5
BIG = 30000.0
QSCALE = 128.0 ** -0.5

ENGS = ("pe", "act", "dve", "pool", "sp")
import os
DEBUG_NOCC = bool(os.environ.get("KDEBUG_NOCC"))
NO_PRECONV = False


class Op:
    __slots__ = ("eng", "idx", "fn", "deps", "signal", "dma_key", "dma_cnt", "cum", "inc")

    def __init__(self, eng, idx, fn):
        self.eng = eng
        self.idx = idx
        self.fn = fn
        self.deps = []
        self.signal = False
        self.dma_key = None
        self.dma_cnt = 0
        self.cum = 0
        self.inc = 16


class Prog:
    def __init__(self, nc):
        self.nc = nc
        self.ops = {e: [] for e in ENGS}
        self.last_w = {}
        self.readers = {}
        self.dma_cnt = {}
        self.dma_inc = {}

    def op(self, eng, fn, reads=(), writes=(), dma=None, inc=16):
        o = Op(eng, len(self.ops[eng]), fn)
        if dma is not None:
            o.dma_key = dma
            o.inc = inc
            self.dma_inc[dma] = inc
            self.dma_cnt[dma] = self.dma_cnt.get(dma, 0) + 1
            o.dma_cnt = self.dma_cnt[dma]
        deps = []
        for k in reads:
            w = self.last_w.get(k)
            if w is not None:
                deps.append(w)
        for k in writes:
            w = self.last_w.get(k)
            if w is not None:
                deps.append(w)
            for r in self.readers.get(k, {}).values():
                deps.append(r)
        seen = set()
        for d in deps:
            if d is o or id(d) in seen:
                continue
            seen.add(id(d))
            if d.dma_key is None and d.eng == eng:
                if eng in ("pe", "sp"):
                    continue
                if o.dma_key is None and o.idx - d.idx > 3:
                    continue
            o.deps.append(d)
            if d.dma_key is None:
                d.signal = True
        for k in reads:
            self.readers.setdefault(k, {})[(eng, o.dma_key)] = o
        for k in writes:
            self.last_w[k] = o
            self.readers[k] = {}
        self.ops[eng].append(o)
        return o

    def fence(self):
        lasts = []
        for e in ENGS:
            for o in reversed(self.ops[e]):
                if o.dma_key is None and o.fn is not None:
                    lasts.append(o)
                    break
        dmas = []
        for k, c in self.dma_cnt.items():
            p = Op("sp", -1, None)
            p.dma_key = k
            p.dma_cnt = c
            p.inc = self.dma_inc[k]
            dmas.append(p)
        for e in ENGS:
            o = Op(e, len(self.ops[e]), None)
            for d in lasts:
                if d.eng != e:
                    o.deps.append(d)
                    d.signal = True
            o.deps.extend(dmas)
            self.ops[e].append(o)
        self.last_w = {}
        self.readers = {}

    def emit(self):
        nc = self.nc
        with contextlib.ExitStack() as st:
            esem = {e: st.enter_context(nc.semaphore("s_" + e)) for e in ENGS}
            dsem = {}
            for n, k in enumerate(self.dma_cnt):
                dsem[k] = st.enter_context(nc.semaphore("d%d" % n))
            block = st.enter_context(nc.Block())
            for e in ENGS:
                c = 0
                for o in self.ops[e]:
                    if o.signal:
                        c += 1
                    o.cum = c

            def run(e, engine):
                waited = {}
                for o in self.ops[e]:
                    for d in o.deps:
                        if d.dma_key is not None:
                            s, v, key = dsem[d.dma_key], d.inc * d.dma_cnt, ("d", d.dma_key)
                        else:
                            s, v, key = esem[d.eng], d.cum, ("e", d.eng)
                        if waited.get(key, 0) >= v:
                            continue
                        waited[key] = v
                        engine.wait_ge(s, v)
                    if o.fn is None:
                        continue
                    ins = o.fn(engine)
                    if o.dma_key is not None:
                        ins.then_inc(dsem[o.dma_key], o.inc)
                    elif o.signal:
                        ins.then_inc(esem[e], 1)
                if e == "sp":
                    for k, c in self.dma_cnt.items():
                        if waited.get(("d", k), 0) < self.dma_inc[k] * c:
                            engine.wait_ge(dsem[k], self.dma_inc[k] * c)

            @block.tensor
            def _(eng):
                run("pe", eng)

            @block.scalar
            def _(eng):
                run("act", eng)

            @block.vector
            def _(eng):
                run("dve", eng)

            @block.gpsimd
            def _(eng):
                run("pool", eng)

            @block.sync
            def _(eng):
                run("sp", eng)


VEC_COLS = {}
_c = 0
for _n, _w in (("a_g", 16), ("b1", 32), ("dwb", 16), ("lng", 16), ("lnb", 16), ("b2", 16),
               ("kv_g", 16), ("b_g", 16), ("f_g0", 16), ("f_g1", 16), ("fin_g", 16),
               ("dw", 16 * CW), ("hm", 1), ("pm", 1)):
    VEC_COLS[_n] = (_c, _w)
    _c += _w
NV = _c


def _pack(v):
    return np.ascontiguousarray(np.asarray(v, np.float32).reshape(-1, 128).T)


def make_vecs(inp, half):
    vecs = np.zeros((128, NV), np.float32)

    def put(name, arr):
        s, w = VEC_COLS[name]
        vecs[:, s:s + w] = arr

    put("a_g", _pack(inp["a_norm_g"][0]))
    put("b1", _pack(inp["conv_b1"][0]))
    put("dwb", _pack(inp["conv_dw_b"][0]))
    put("lng", _pack(inp["conv_ln_g"][0]))
    put("lnb", _pack(inp["conv_ln_b"][0]))
    put("b2", _pack(inp["conv_b2"][0]))
    put("kv_g", _pack(inp["kv_norm_g"]))
    put("b_g", _pack(inp["b_norm_g"][0]))
    put("f_g0", _pack(inp["ffn_norm_g"][0]))
    put("f_g1", _pack(inp["ffn_norm_g"][1]))
    put("fin_g", _pack(inp["final_norm_g"]))
    dw = np.asarray(inp["conv_dw"][0], np.float32)
    dwp = dw.T.reshape(16, 128, CW).transpose(1, 0, 2).reshape(128, 16 * CW)
    put("dw", dwp)
    put("hm", np.full((128, 1), 1.0 if half == 1 else 0.0, np.float32))
    put("pm", np.full((128, 1), 0.0 if half == 1 else BIG, np.float32))
    return vecs


def make_consts():
    k = np.arange(128)[:, None]
    q = np.arange(128)[None, :]
    t0 = np.where(k >= q, q - k + 128, BIG).astype(np.float32)
    t1 = np.where(k <= q, q - k, BIG).astype(np.float32)
    tc = np.zeros((128, 2, 512), np.float32)
    tc[:, 0, :] = np.tile(t0, (1, 4))
    tc[:, 1, :] = np.tile(t1, (1, 4))
    return np.eye(128, dtype=np.float32), tc


class Builder:
    def __init__(self, phase):
        self.phase = phase
        nc = self.nc = bass.Bass("TRN2", target_bir_lowering=False)
        self.P = Prog(nc)
        self.nbank = 0
        self.nw = 0
        din = lambda n, s, dt=F32: nc.dram_tensor(n, s, dt, kind="ExternalInput").ap()
        dout = lambda n, s, dt=F32: nc.dram_tensor(n, s, dt, kind="ExternalOutput").ap()
        dint = lambda n, s, dt=F32: nc.dram_tensor(n, s, dt, kind="Internal").ap()
        self.vecs_d = din("vecs", [128, NV])
        self.ident_d = din("ident", [128, 128])
        hasA = "A" in phase
        hasB = "B" in phase
        if hasA:
            self.x_d = din("x", [NTOK, D])
            self.xh_d = din("xh", [HALO, D])
            self.w1_d = din("conv_w1", [D, 2 * D])
            self.w2_d = din("conv_w2", [D, D])
            self.wk_d = din("w_k", [D, D])
            self.wv_d = din("w_v", [D, D])
            self.g0_d = din("gate0", [D, FF])
            self.u0_d = din("up0", [D, FF])
            self.d0_d = din("down0", [FF, D])
        if hasB:
            self.tc_d = din("tconst", [128, 2, 512])
            self.wq_d = din("w_q", [D, D])
            self.wo_d = din("w_o", [D, D])
            self.g1_d = din("gate1", [D, FF])
            self.u1_d = din("up1", [D, FF])
            self.d1_d = din("down1", [FF, D])
            self.out_d = dout("out", [NTOK, D])
            self.qT_d = dint("qT", [H, 128, NTOK], BF16)
            self.attT_d = dint("attT", [H, 128, NTOK], BF16)
        KSH = [NTILE, H * 128, TT]
        assert phase == "AB", "only the fused program is supported"
        self.h1T_d = dint("h1T", [16, 128, NTOK])
        self.wscr = dint("wscr", [96, 128, 16 * 512], BF16)
        self.wmap = {}
        self.preconv = False
        self.preq = []
        self.k_own = dint("k_own", KSH, BF16)
        self.v_own = dint("v_own", KSH, BF16)
        self.k_all = dint("k_all", [NTILE, 2 * H * 128, TT], BF16)
        self.v_all = dint("v_all", [NTILE, 2 * H * 128, TT], BF16)
        self.k_prev = self.k_all[:, 0:H * 128, :]
        self.v_prev = self.v_all[:, 0:H * 128, :]
        self.alloc()

    def alloc(self):
        nc = self.nc
        self.ps = [nc.alloc_psum_tensor("ps%d" % i, [128, 512], F32) for i in range(8)]
        off = [0]
        NB = 101 * 1024
        arena = self.arena = nc.alloc_sbuf_tensor("arena", [128, NB], BF16)

        def carve(nbytes, dt=BF16):
            n = nbytes // 2
            a = arena[:, off[0]:off[0] + n]
            off[0] += n
            assert off[0] <= NB, off[0]
            return a.bitcast(F32) if dt == F32 else a

        self.carve = carve
        self.vecs = nc.alloc_sbuf_tensor("vecs_sb", [128, NV], F32)
        self.ident = nc.alloc_sbuf_tensor("ident_sb", [128, 128], F32)
        self.identb = nc.alloc_sbuf_tensor("identb", [128, 128], BF16)
        self.onesb = nc.alloc_sbuf_tensor("onesb", [128, 128], BF16)
        self.onesf = nc.alloc_sbuf_tensor("onesf", [128, 128], F32)
        self.eps_rms = nc.alloc_sbuf_tensor("eps_rms", [128, 1], F32)
        self.eps_ln = nc.alloc_sbuf_tensor("eps_ln", [128, 1], F32)
        self.m0 = off[0]
        self.hT = carve(16 * 512 * 4, F32).rearrange("p (c t) -> p c t", c=16)
        self.xn = carve(16 * 512 * 2).rearrange("p (c t) -> p c t", c=16)
        u0 = off[0]
        self.u = carve(16 * 544 * 2).rearrange("p (c t) -> p c t", c=16)
        self.v = carve(16 * 512 * 4, F32).rearrange("p (c t) -> p c t", c=16)
        u1 = off[0]
        self.act = arena[:, u0:u0 + NF * 512].rearrange("p (c t) -> p c t", c=NF)
        assert u0 + NF * 512 <= u1
        s0 = off[0]
        stage = carve(16 * 512 * 2)
        self.stage = stage
        self.st16 = stage.rearrange("p (c t) -> p c t", c=16)
        self.vst = stage.rearrange("p (b f) -> p b f", b=4)
        self.xs = [arena[:, s0 + i * 4096:s0 + (i + 1) * 4096].bitcast(F32) for i in range(2)]
        self.wbuf = [carve(16 * 512 * 2).rearrange("p (k n) -> p k n", k=16) for _ in range(3)]
        self.dg = [carve(CW * 128 * 2).rearrange("p (j m) -> p j m", j=CW) for _ in range(2)]
        self.sq = [carve(512 * 2) for _ in range(4)]
        self.tmp = [carve(512 * 4, F32) for _ in range(3)]
        self.rstd = carve(512 * 4, F32)
        self.lnA = carve(512 * 4, F32)
        self.lnB = carve(512 * 4, F32)
        self.mean = carve(512 * 4, F32)
        self.hTh = carve(16 * HALO * 4, F32).rearrange("p (c t) -> p c t", c=16)
        self.xnh = carve(16 * HALO * 2).rearrange("p (c t) -> p c t", c=16)
        self.uhs = carve(16 * HALO * 2).rearrange("p (c t) -> p c t", c=16)
        self.m1 = off[0]
        if "B" in self.phase:
            off[0] = self.m0
            self.kTs = [carve(2 * NTOK * 2) for _ in range(2)]
            self.qTs = [carve(NTOK * 2) for _ in range(2)]
            self.V1 = [carve(17 * 128 * 2).rearrange("p (t e) -> p t e", e=128) for _ in range(2)]
            self.V4 = [carve(20 * 128 * 2).rearrange("p (r t e) -> p r t e", r=4, e=128) for _ in range(2)]
            self.V16 = [carve(32 * 128 * 2).rearrange("p (r t e) -> p r t e", r=16, e=128) for _ in range(2)]
            self.vTs = [carve(2 * NTOK * 2) for _ in range(2)]
            self.accn = [carve(NTOK * 4, F32) for _ in range(3)]
            self.accd = [carve(NTOK * 4, F32) for _ in range(3)]
            self.nsum = carve(NTOK * 4, F32)
            self.dsum = carve(NTOK * 4, F32)
            self.rden = carve(NTOK * 4, F32)
            self.atto = [carve(NTOK * 2) for _ in range(2)]
            self.sc = [[carve(512 * 4, F32) for _ in range(2)] for _ in range(2)]
            self.pT = [[carve(512 * 2) for _ in range(2)] for _ in range(2)]
            self.TA = carve(512 * 4, F32)
            self.TB = carve(512 * 4, F32)
            self.TD = carve(512 * 4, F32)
            self.TC = carve(512 * 4, F32)
            assert off[0] <= self.m1, (off[0], self.m1)
            off[0] = self.m1

    def vcol(self, name, c=0, w=1):
        s, _ = VEC_COLS[name]
        return self.vecs[:, s + c:s + c + w]

    def bank(self):
        b = self.nbank % 6
        self.nbank += 1
        return b

    def mm(self, out, lhsT, rhs, start, stop, reads, writes):
        self.P.op("pe", lambda e: e.matmul(out, lhsT=lhsT, rhs=rhs, start=start, stop=stop),
                  reads=reads, writes=writes)

    def mm_items(self, items, src, skey, n, dc_outer):
        if dc_outer:
            for dc in range(16):
                for p, b, cs in items:
                    self.mm(self.ps[p][:, 0:n], self.wbuf[b][:, dc, cs], src(dc), dc == 0, dc == 15,
                            [("w", b), skey(dc)], [("ps", p)])
        else:
            for p, b, cs in items:
                for dc in range(16):
                    self.mm(self.ps[p][:, 0:n], self.wbuf[b][:, dc, cs], src(dc), dc == 0, dc == 15,
                            [("w", b), skey(dc)], [("ps", p)])

    def tr(self, out, in_, ident, reads, writes):
        self.P.op("pe", lambda e: e.transpose(out=out, in_=in_, identity=ident), reads=reads, writes=writes)

    def actv(self, out, in_, func, reads, writes, bias=0.0, scale=1.0):
        self.P.op("act", lambda e: e.activation(out=out, in_=in_, func=func, bias=bias, scale=scale),
                  reads=reads, writes=writes)

    def stt(self, out, in0, scalar, in1, op0, op1, reads, writes, eng="dve"):
        self.P.op(eng, lambda e: e.scalar_tensor_tensor(out=out, in0=in0, scalar=scalar, in1=in1, op0=op0, op1=op1),
                  reads=reads, writes=writes)

    def tt(self, out, in0, in1, op, reads, writes, eng="dve"):
        self.P.op(eng, lambda e: e.tensor_tensor(out=out, in0=in0, in1=in1, op=op), reads=reads, writes=writes)

    def ts(self, out, in0, s1, s2, op0, op1, reads, writes, eng="dve"):
        if s2 is None:
            self.P.op(eng, lambda e: e.tensor_scalar(out=out, in0=in0, scalar1=s1, scalar2=None, op0=op0),
                      reads=reads, writes=writes)
        else:
            self.P.op(eng, lambda e: e.tensor_scalar(out=out, in0=in0, scalar1=s1, scalar2=s2, op0=op0, op1=op1),
                      reads=reads, writes=writes)

    def cp(self, out, in_, reads, writes, eng="dve"):
        if eng == "act":
            self.P.op("act", lambda e: e.copy(out=out, in_=in_), reads=reads, writes=writes)
        else:
            self.P.op(eng, lambda e: e.tensor_copy(out=out, in_=in_), reads=reads, writes=writes)

    def dma(self, out, in_, reads, writes, key, eng="sp"):
        return self.P.op(eng, lambda e: e.dma_start(out=out, in_=in_), reads=reads, writes=writes, dma=key)

    def wload(self, src, nk=16, wid=None):
        b = self.nw % 3
        self.nw += 1
        dst = self.wbuf[b][:, 0:nk, :]
        if wid not in self.wmap:
            idx = self.wmap[wid] = len(self.wmap)
            self.dma(dst, src.rearrange("(k p) n -> p k n", p=128), reads=[], writes=[("w", b)],
                     key=("w", b), eng="pool")
            self.dma(self.wscr[idx][:, 0:nk * 512].rearrange("p (k n) -> p k n", k=nk), dst,
                     reads=[("w", b)], writes=[("dram", "wscr", idx)], key="wst")
        else:
            idx = self.wmap[wid]
            self.dma(dst, self.wscr[idx][:, 0:nk * 512].rearrange("p (k n) -> p k n", k=nk),
                     reads=[("dram", "wscr", idx)], writes=[("w", b)], key=("w", b), eng="pool")
        return b

    def plan_preconvert(self):
        q = []
        for j in range(4):
            q.append((self.wo_d[:, 512 * j:512 * j + 512], 16, (self.wo_d.name, j)))
        for j in range(NF // 4):
            q.append((self.g1_d[:, 512 * j:512 * j + 512], 16, (self.g1_d.name, j)))
            q.append((self.u1_d[:, 512 * j:512 * j + 512], 16, (self.u1_d.name, j)))
        for qq in range(4):
            for g in range(3):
                nk = 16 if g < 2 else NF - 32
                q.append((self.d1_d[g * 2048:g * 2048 + nk * 128, 512 * qq:512 * qq + 512], nk,
                          (self.d1_d.name, g, qq)))
        self.preq = q
        q0 = []
        for j in range(4):
            q0.append((self.g0_d[:, 512 * j:512 * j + 512], 16, (self.g0_d.name, j)))
            q0.append((self.u0_d[:, 512 * j:512 * j + 512], 16, (self.u0_d.name, j)))
        self.preq0 = q0

    def preconvert_one(self, queue=None):
        queue = self.preq if queue is None else queue
        if not queue or NO_PRECONV:
            return
        src, nk, wid = queue.pop(0)
        idx = self.wmap[wid] = len(self.wmap)
        self.dma(self.wscr[idx][:, 0:nk * 512].rearrange("p (k n) -> p k n", k=nk),
                 src.rearrange("(k p) n -> p k n", p=128), reads=[], writes=[("dram", "wscr", idx)],
                 key="wpc", eng="pool")

    def setup(self):
        self.dma(self.vecs[:], self.vecs_d, [], ["vecs"], "c0")
        self.dma(self.ident[:], self.ident_d, [], ["ident"], "c1")
        self.cp(self.identb[:], self.ident[:], ["ident"], ["identb"])
        self.P.op("dve", lambda e: e.memset(self.onesb[:], 1.0), writes=["onesb"])
        self.P.op("dve", lambda e: e.memset(self.onesf[:], 1.0), writes=["onesf"])
        self.P.op("dve", lambda e: e.memset(self.eps_rms[:], RMS_EPS), writes=["eps"])
        self.P.op("dve", lambda e: e.memset(self.eps_ln[:], LN_EPS), writes=["eps"])

    def rms_stat(self, c, src, skey, n=TT):
        S = self.ps[6]
        sq = self.sq[c % 4]
        self.actv(sq[:, 0:n], src(c), AF.Square, [skey(c)], [("sq", c % 4)])
        self.mm(S[:, 0:n], self.onesb[:], sq[:, 0:n], c == 0, c == 15, [("sq", c % 4), "onesb"], [("ps", 6)])

    def rmsnorm(self, src, skey, dst, dkey, gname, n, stats="compute"):
        S = self.ps[6]
        r = self.rstd[:, 0:n]
        if stats == "compute":
            for c in range(16):
                self.rms_stat(c, src, skey, n)
        if stats != "reuse":
            self.actv(r, S[:, 0:n], AF.Sqrt, [("ps", 6)], ["rstd"], bias=self.eps_rms[:], scale=1.0 / D)
            self.P.op("dve", lambda e: e.reciprocal(out=r, in_=r), reads=["rstd"], writes=["rstd"])
        for c in range(16):
            self.stt(dst(c), src(c), self.vcol(gname, c), r, ALU.mult, ALU.mult,
                     [skey(c), "rstd", "vecs"], [dkey(c)])

    def load_x_tile(self, i):
        for tb in range(4):
            xs = self.xs[tb % 2]
            r0 = i * TT + tb * 128
            self.dma(xs, self.x_d[r0:r0 + 128, :], [], [("stage", tb % 2)], ("xs", tb % 2))
            for cg in range(4):
                b = self.bank()
                for cc in range(4):
                    c = cg * 4 + cc
                    self.tr(self.ps[b][:, cc * 128:(cc + 1) * 128], xs[:, c * 128:(c + 1) * 128], self.ident[:],
                            [("stage", tb % 2), "ident"], [("ps", b)])
                self.cp(self.hT[:, cg * 4:(cg + 1) * 4, tb * 128:(tb + 1) * 128],
                        self.ps[b][:].rearrange("p (c t) -> p c t", c=4), [("ps", b)],
                        [("hT", cg * 4 + k) for k in range(4)], eng=("act" if cg % 2 else "dve"))

    def load_halo(self):
        self.xsh = self.xs[1]
        self.dma(self.xsh[0:HALO, :], self.xh_d, [], [("stage", 1)], "xsh")
        for cg in range(4):
            b = self.bank()
            for cc in range(4):
                c = cg * 4 + cc
                self.tr(self.ps[b][:, cc * HALO:(cc + 1) * HALO], self.xsh[0:HALO, c * 128:(c + 1) * 128],
                        self.ident[0:HALO, 0:HALO], [("stage", 1), "ident"], [("ps", b)])
            self.cp(self.hTh[:, cg * 4:(cg + 1) * 4, :],
                    self.ps[b][:, 0:4 * HALO].rearrange("p (c t) -> p c t", c=4), [("ps", b)],
                    [("hTh", cg * 4 + k) for k in range(4)])

    def w1_glu(self, src, skey, n, dst, dkey, halo):
        for j in range(4):
            ba = self.wload(self.w1_d[:, 512 * j:512 * j + 512], wid=("w1a", j))
            bg = self.wload(self.w1_d[:, D + 512 * j:D + 512 * j + 512], wid=("w1g", j))
            for half in range(2):
                ccs = (2 * half, 2 * half + 1)
                bk = {cc: (self.bank(), self.bank()) for cc in ccs}
                items = []
                for cc in ccs:
                    cs = slice(cc * 128, (cc + 1) * 128)
                    items += [(bk[cc][0], ba, cs), (bk[cc][1], bg, cs)]
                self.mm_items(items, src, skey, n, dc_outer=(j == 0 and half == 0))
                for cc in ccs:
                    c = 4 * j + cc
                    pa, pg = bk[cc]
                    t = self.tmp[c % 3]
                    self.actv(t[:, 0:n], self.ps[pg][:, 0:n], AF.Sigmoid, [("ps", pg), "vecs"],
                              [("tmp", c % 3)], bias=self.vcol("b1", 16 + c))
                    self.stt(dst(c), self.ps[pa][:, 0:n], self.vcol("b1", c), t[:, 0:n], ALU.add, ALU.mult,
                             [("ps", pa), ("tmp", c % 3), "vecs"], [dkey(c)])
                    if halo:
                        self.ts(dst(c), dst(c), self.vcol("hm"), None, ALU.mult, None, [dkey(c), "vecs"],
                                [dkey(c)])

    def conv_ln(self, i):
        S1, S2 = self.ps[6], self.ps[7]
        if i > 0:
            self.exchange(i - 1)
            self.cp(self.u[:, :, 0:HALO], self.uhs[:, :, :], ["uhs"], [("uh", c) for c in range(16)], eng="dve")
        for c in range(16):
            if i == 0 and c % 2 == 1:
                self.preconvert_one(self.preq0)
            elif i >= 1 and c % 4 == 3:
                self.preconvert_one()
            dg = self.dg[c % 2]
            s, _ = VEC_COLS["dw"]
            dwc = self.vecs[:, s + c * CW:s + (c + 1) * CW]
            self.tt(dg[:], self.identb[:].unsqueeze(1).broadcast_to([128, CW, 128]),
                    dwc.unsqueeze(2).broadcast_to([128, CW, 128]), ALU.mult,
                    ["identb", "vecs"], [("dg", c % 2)])
            b = self.bank()
            for j in range(CW):
                self.mm(self.ps[b][:], dg[:, j, :], self.u[:, c, HALO - (CW - 1 - j):HALO - (CW - 1 - j) + TT], j == 0, j == CW - 1,
                        [("dg", c % 2), ("u", c), ("uh", c)], [("ps", b)])
            self.actv(self.v[:, c, :], self.ps[b][:], AF.Identity, [("ps", b), "vecs"], [("v", c)],
                      bias=self.vcol("dwb", c))
            sq = self.sq[c % 4]
            self.actv(sq[:], self.ps[b][:], AF.Square, [("ps", b), "vecs"], [("sq", c % 4)],
                      bias=self.vcol("dwb", c))
            self.mm(S1[:], self.onesf[:], self.v[:, c, :], c == 0, c == 15, [("v", c), "onesf"], [("ps", 6)])
            self.mm(S2[:], self.onesb[:], sq[:], c == 0, c == 15, [("sq", c % 4), "onesb"], [("ps", 7)])
        if i < NTILE - 1:
            self.cp(self.uhs[:, :, :], self.u[:, :, TT:TT + HALO], [("u", c) for c in range(16)],
                    ["uhs"], eng="dve")
        mean, A, B = self.mean, self.lnA, self.lnB
        self.ts(mean[:], S1[:], 1.0 / D, None, ALU.mult, None, [("ps", 6)], ["mean"])
        self.tt(B[:], mean[:], mean[:], ALU.mult, ["mean"], ["lnB"])
        self.stt(A[:], S2[:], 1.0 / D, B[:], ALU.mult, ALU.subtract, [("ps", 7), "lnB"], ["lnA"])
        self.actv(A[:], A[:], AF.Sqrt, ["lnA"], ["lnA"], bias=self.eps_ln[:], scale=1.0)
        self.P.op("dve", lambda e: e.reciprocal(out=A[:], in_=A[:]), reads=["lnA"], writes=["lnA"])
        self.stt(B[:], mean[:], -1.0, A[:], ALU.mult, ALU.mult, ["mean", "lnA"], ["lnB"])
        for c in range(16):
            self.tt(self.v[:, c, :], self.v[:, c, :], A[:], ALU.mult, [("v", c), "lnA"], [("v", c)])
            if c > 0:
                self.ln_tail(c - 1, B)
        self.ln_tail(15, B)

    def ln_tail(self, c, B):
        if True:
            self.tt(self.v[:, c, :], self.v[:, c, :], B[:], ALU.add, [("v", c), "lnB"], [("v", c)])
            self.actv(self.xn[:, c, :], self.v[:, c, :], AF.Silu, [("v", c), "vecs"], [("xn", c)],
                      bias=self.vcol("lnb", c), scale=self.vcol("lng", c))

    def proj_resid(self, w_d, src, skey, bias_name):
        for j in range(4):
            b = self.wload(w_d[:, 512 * j:512 * j + 512], wid=(w_d.name, j))
            pbk = [self.bank() for _ in range(4)] if j == 0 else None
            if j == 0:
                self.mm_items([(pbk[cc], b, slice(cc * 128, (cc + 1) * 128)) for cc in range(4)], src, skey, TT, True)
            for cc in range(4):
                c = 4 * j + cc
                if j == 0:
                    p = pbk[cc]
                else:
                    p = self.bank()
                    self.mm_items([(p, b, slice(cc * 128, (cc + 1) * 128))], src, skey, TT, False)
                if bias_name is not None:
                    self.stt(self.hT[:, c, :], self.ps[p][:], self.vcol(bias_name, c), self.hT[:, c, :],
                             ALU.add, ALU.add, [("ps", p), ("hT", c), "vecs"], [("hT", c)])
                else:
                    self.tt(self.hT[:, c, :], self.ps[p][:], self.hT[:, c, :], ALU.add,
                            [("ps", p), ("hT", c)], [("hT", c)])
            if j > 0:
                for c in range(4 * (j - 1), 4 * j):
                    self.rms_stat(c, lambda c: self.hT[:, c, :], lambda c: ("hT", c))
        for c in range(12, 16):
            self.rms_stat(c, lambda c: self.hT[:, c, :], lambda c: ("hT", c))

    def proj_out(self, w_d, dst, dkey, scale, eng):
        for j in range(4):
            b = self.wload(w_d[:, 512 * j:512 * j + 512], wid=(w_d.name, j))
            xsrc, xkey = (lambda dc: self.xn[:, dc, :]), (lambda dc: ("xn", dc))
            pbk = [self.bank() for _ in range(4)] if j == 0 else None
            if j == 0:
                self.mm_items([(pbk[cc], b, slice(cc * 128, (cc + 1) * 128)) for cc in range(4)], xsrc, xkey, TT, True)
            for cc in range(4):
                c = 4 * j + cc
                if j == 0:
                    p = pbk[cc]
                else:
                    p = self.bank()
                    self.mm_items([(p, b, slice(cc * 128, (cc + 1) * 128))], xsrc, xkey, TT, False)
                if scale is not None:
                    self.actv(dst(c), self.ps[p][:], AF.Identity, [("ps", p)], dkey(c), scale=scale)
                else:
                    self.cp(dst(c), self.ps[p][:], [("ps", p)], dkey(c), eng=eng)

    def ffn(self, g_d, u_d, d_d, gname):
        self.rmsnorm(lambda c: self.hT[:, c, :], lambda c: ("hT", c),
                     lambda c: self.xn[:, c, :], lambda c: ("xn", c), gname, TT, stats="done")
        for j in range(NF // 4):
            bg = self.wload(g_d[:, 512 * j:512 * j + 512], wid=(g_d.name, j))
            bu = self.wload(u_d[:, 512 * j:512 * j + 512], wid=(u_d.name, j))
            if self.preconv and j % 2 == 0:
                self.preconvert_one()
            xsrc, xkey = (lambda dc: self.xn[:, dc, :]), (lambda dc: ("xn", dc))
            for half in range(2):
                ccs = (2 * half, 2 * half + 1)
                bk = {cc: (self.bank(), self.bank()) for cc in ccs}
                items = []
                for cc in ccs:
                    cs = slice(cc * 128, (cc + 1) * 128)
                    items += [(bk[cc][0], bg, cs), (bk[cc][1], bu, cs)]
                self.mm_items(items, xsrc, xkey, TT, dc_outer=(j == 0 and half == 0))
                for cc in ccs:
                    f = 4 * j + cc
                    pg, pu = bk[cc]
                    t = self.tmp[f % 3]
                    self.actv(t[:], self.ps[pg][:], AF.Silu, [("ps", pg)], [("tmp", f % 3)])
                    self.tt(self.act[:, f, :], t[:], self.ps[pu][:], ALU.mult, [("tmp", f % 3), ("ps", pu)],
                            [("act", f)])
        for q in range(4):
            banks = [self.bank() for _ in range(4)]
            for g in range(3):
                nk = 16 if g < 2 else NF - 32
                b = self.wload(d_d[g * 2048:g * 2048 + nk * 128, 512 * q:512 * q + 512], nk, wid=(d_d.name, g, q))
                if self.preconv and g == 0:
                    self.preconvert_one()
                for cc in range(4):
                    for fk in range(nk):
                        f = g * 16 + fk
                        self.mm(self.ps[banks[cc]][:], self.wbuf[b][:, fk, cc * 128:(cc + 1) * 128],
                                self.act[:, f, :], f == 0, f == NF - 1, [("w", b), ("act", f)],
                                [("ps", banks[cc])])
            if q > 0:
                for c in range(4 * (q - 1), 4 * q):
                    self.rms_stat(c, lambda c: self.hT[:, c, :], lambda c: ("hT", c))
            for cc in range(4):
                c = 4 * q + cc
                self.tt(self.hT[:, c, :], self.ps[banks[cc]][:], self.hT[:, c, :], ALU.add,
                        [("ps", banks[cc]), ("hT", c)], [("hT", c)])
        for c in range(12, 16):
            self.rms_stat(c, lambda c: self.hT[:, c, :], lambda c: ("hT", c))

    def phaseA(self):
        self.plan_preconvert()
        for i in range(NTILE):
            cols = slice(i * TT, (i + 1) * TT)
            self.load_x_tile(i)
            self.rmsnorm(lambda c: self.hT[:, c, :], lambda c: ("hT", c),
                         lambda c: self.xn[:, c, :], lambda c: ("xn", c), "a_g", TT)
            self.w1_glu(lambda c: self.xn[:, c, :], lambda c: ("xn", c), TT,
                        lambda c: self.u[:, c, HALO:HALO + TT], lambda c: ("u", c), False)
            if i == 0:
                self.load_halo()
                self.rmsnorm(lambda c: self.hTh[:, c, :], lambda c: ("hTh", c),
                             lambda c: self.xnh[:, c, :], lambda c: ("xnh", c), "a_g", HALO)
                self.w1_glu(lambda c: self.xnh[:, c, :], lambda c: ("xnh", c), HALO,
                            lambda c: self.u[:, c, 0:HALO], lambda c: ("uh", c), True)
            self.conv_ln(i)
            self.proj_resid(self.w2_d, lambda c: self.xn[:, c, :], lambda c: ("xn", c), "b2")
            self.preconv = (i >= 1 and not NO_PRECONV)
            self.ffn(self.g0_d, self.u0_d, self.d0_d, "f_g0")
            self.preconv = False
            self.dma(self.h1T_d[:, :, cols].rearrange("c p t -> p c t"), self.hT[:, :, :],
                     [("hT", c) for c in range(16)], [("dram", "h1", i)], "h1o")
            self.rmsnorm(lambda c: self.hT[:, c, :], lambda c: ("hT", c),
                         lambda c: self.xn[:, c, :], lambda c: ("xn", c), "kv_g", TT, stats="done")
            skh = lambda c: [("stage", c // 8)]
            for w_d, dst_d, nm, key, eng in ((self.wk_d, self.k_own[i], "k", "kto", "act"),
                                             (self.wv_d, self.v_own[i], "v", "vo", "dve")):
                self.proj_out(w_d, lambda c: self.st16[:, c, :], skh, None, eng)
                dv = dst_d.rearrange("(h e) t -> e h t", e=128)
                for hh in range(2):
                    self.dma(dv[:, 8 * hh:8 * hh + 8, :], self.st16[:, 8 * hh:8 * hh + 8, :], [("stage", hh)],
                             [("dram", nm, i)], key)
            if i == NTILE - 1:
                self.exchange(i, ("k",))
            if "B" in self.phase:
                self.rmsnorm(lambda c: self.hT[:, c, :], lambda c: ("hT", c),
                             lambda c: self.xn[:, c, :], lambda c: ("xn", c), "b_g", TT, stats="reuse")
                self.proj_out(self.wq_d, lambda c: self.st16[:, c, :], skh, QSCALE, "act")
                qv = self.qT_d[:, :, cols].rearrange("h e t -> e h t")
                for hh in range(2):
                    self.dma(qv[:, 8 * hh:8 * hh + 8, :], self.st16[:, 8 * hh:8 * hh + 8, :], [("stage", hh)], [],
                             "qo")
        self.exchange(NTILE - 1, ("v",))

    def exchange(self, i, which=("k", "v")):
        if self.phase != "AB" or DEBUG_NOCC:
            return
        rg = [[0, 1], [2, 3], [4, 5], [6, 7]]
        for nm, own, allb in (("k", self.k_own, self.k_all), ("v", self.v_own, self.v_all)):
            if nm not in which:
                continue
            self.P.op("pool", lambda e, own=own, allb=allb: e.collective_compute(
                "AllGather", ALU.bypass, replica_groups=rg,
                ins=[own[i].rearrange("(a b) t -> a (b t)", b=4)],
                outs=[allb[i].rearrange("(a b) t -> a (b t)", b=4)]),
                reads=[("dram", nm, i)], writes=[("dram", nm + "all")], dma="cc", inc=1)

    def phaseB1(self):
        for i in range(NTILE):
            cols = slice(i * TT, (i + 1) * TT)
            self.dma(self.hT[:, :, :], self.h1T_d[:, :, cols].rearrange("c p t -> p c t"), [("dram", "h1", i)],
                     [("hT", c) for c in range(16)], "h1i")
            self.rmsnorm(lambda c: self.hT[:, c, :], lambda c: ("hT", c),
                         lambda c: self.xn[:, c, :], lambda c: ("xn", c), "b_g", TT)
            sk = [("stage", 0), ("stage", 1)]
            self.proj_out(self.wq_d, lambda c: self.st16[:, c, :], lambda c: sk, QSCALE, "act")
            self.dma(self.qT_d[:, :, cols].rearrange("h e t -> e h t"), self.st16[:, :, :], sk, [], "qo")

    def att_loads(self, h):
        s = h % 2
        hs = slice(h * 128, (h + 1) * 128)
        for dst, prev, own, key, dk in ((self.kTs[s], self.k_prev, self.k_own, ("kTs", s), "kall"),
                                        (self.vTs[s], self.v_prev, self.v_own, ("vTs", s), "vall")):
            self.dma(dst[:, 0:NTOK].rearrange("p (i t) -> p i t", i=NTILE),
                     prev[:, hs, :].rearrange("i e t -> e i t"), [("dram", dk)], [key], ("kl", s))
            self.dma(dst[:, NTOK:2 * NTOK].rearrange("p (i t) -> p i t", i=NTILE),
                     own[:, hs, :].rearrange("i e t -> e i t"), [], [key], ("kl", s))
        self.dma(self.qTs[s][:], self.qT_d[h], [], [("qTs", s)], ("ql", s))

    def att_vtiles(self, h):
        s = h % 2
        vT = self.vTs[s]
        jobs = []
        cols = [slice(NTOK + 128 * kb, NTOK + 128 * (kb + 1)) for kb in range(-1, 16)]
        jobs.append((self.V1[s].rearrange("p t e -> p (t e)"), cols))
        cols = []
        for r in range(4):
            for kb in range(-1, 4):
                st = NTOK + r + 512 * kb
                cols.append(slice(st, st + 509, 4))
        jobs.append((self.V4[s].rearrange("p r t e -> p (r t e)"), cols))
        cols = []
        for r in range(16):
            for kb in range(-1, 1):
                st = NTOK * (kb + 1) + r
                cols.append(slice(st, st + 2033, 16))
        jobs.append((self.V16[s].rearrange("p r t e -> p (r t e)"), cols))
        n = 0
        for dest, cols in jobs:
            for c0 in range(0, len(cols), 8):
                grp = cols[c0:c0 + 8]
                b = n % 4
                n += 1
                pb = self.ps[b][:].bitcast(BF16)
                for k, cs in enumerate(grp):
                    self.tr(pb[:, 128 * k:128 * (k + 1)], vT[:, cs], self.identb[:],
                            [("vTs", s), "identb"], [("ps", b)])
                self.cp(dest[:, 128 * c0:128 * (c0 + len(grp))], pb[:, 0:128 * len(grp)], [("ps", b)],
                        [("V", s)], eng=("act" if n % 2 else "dve"))

    def att_groups(self, s):
        kT, qT = self.kTs[s], self.qTs[s]
        groups = []
        for g in range(4):
            blocks = []
            for bb in range(4):
                nb = 4 * g + bb
                blocks.append((kT[:, NTOK + 128 * (nb - 1):NTOK + 128 * nb],
                               kT[:, NTOK + 128 * nb:NTOK + 128 * (nb + 1)],
                               qT[:, 128 * nb:128 * (nb + 1)],
                               self.V1[s][:, nb, :], self.V1[s][:, nb + 1, :]))
            groups.append((self.TA if g == 0 else self.TB, blocks,
                           (lambda a, g=g: a[:, 512 * g:512 * (g + 1)]), 1))
        for r in range(4):
            blocks = []
            for nb in range(4):
                kb0 = NTOK + r + 512 * (nb - 1)
                kb1 = NTOK + r + 512 * nb
                blocks.append((kT[:, kb0:kb0 + 509:4], kT[:, kb1:kb1 + 509:4],
                               qT[:, r + 512 * nb:r + 512 * nb + 509:4],
                               self.V4[s][:, r, nb, :], self.V4[s][:, r, nb + 1, :]))
            groups.append((self.TA, blocks, (lambda a, r=r: a[:, 512 * r:512 * (r + 1)]), 4))
        for g in range(4):
            blocks = []
            for bb in range(4):
                r = 4 * g + bb
                blocks.append((kT[:, r:NTOK:16], kT[:, NTOK + r:2 * NTOK:16], qT[:, r:NTOK:16],
                               self.V16[s][:, r, 0, :], self.V16[s][:, r, 1, :]))
            groups.append((self.TD, blocks, (lambda a, g=g: a[:, 512 * g:512 * (g + 1)]), 16))
        return groups

    def att_scores(self, h, gi, grp):
        s = h % 2
        Tx, blocks, accf, d = grp
        par = gi % 2
        X, Y = self.ps[par * 2], self.ps[par * 2 + 1]
        for bb, (kp, kc, q, vp, vc) in enumerate(blocks):
            self.mm(X[:, 128 * bb:128 * (bb + 1)], kp, q, True, True, [("kTs", s), ("qTs", s)], [("ps", par * 2)])
        for bb, (kp, kc, q, vp, vc) in enumerate(blocks):
            self.mm(Y[:, 128 * bb:128 * (bb + 1)], kc, q, True, True, [("kTs", s), ("qTs", s)],
                    [("ps", par * 2 + 1)])
        slope = 2.0 ** (-8.0 * (h + 1) / H)
        for w, (T, Pb) in enumerate(((Tx, X), (self.TC, Y))):
            scb, ptb = self.sc[par][w], self.pT[par][w]
            self.stt(scb[:], T[:], -slope * d, Pb[:], ALU.mult, ALU.add, ["T", ("ps", par * 2 + w)],
                     [("sc", par, w)])
            self.actv(ptb[:], scb[:], AF.Exp, [("sc", par, w)], [("pT", par, w)])

    def att_pv(self, h, gi, grp):
        s = h % 2
        Tx, blocks, accf, d = grp
        par = gi % 2
        N_, Dn = self.ps[4 + par * 2], self.ps[5 + par * 2]
        pX, pY = self.pT[par][0], self.pT[par][1]
        for bb, (kp, kc, q, vp, vc) in enumerate(blocks):
            cs = slice(128 * bb, 128 * (bb + 1))
            self.mm(N_[:, cs], vp, pX[:, cs], True, False, [("V", s), ("pT", par, 0)], [("ps", 4 + par * 2)])
            self.mm(N_[:, cs], vc, pY[:, cs], False, True, [("V", s), ("pT", par, 1)], [("ps", 4 + par * 2)])
        for bb in range(4):
            cs = slice(128 * bb, 128 * (bb + 1))
            self.mm(Dn[:, cs], self.onesb[:], pX[:, cs], True, False, ["onesb", ("pT", par, 0)],
                    [("ps", 5 + par * 2)])
            self.mm(Dn[:, cs], self.onesb[:], pY[:, cs], False, True, ["onesb", ("pT", par, 1)],
                    [("ps", 5 + par * 2)])
        br = gi // 4
        an, ad = accf(self.accn[br]), accf(self.accd[br])
        pn, pd = N_[:], Dn[:]
        self.cp(an, pn, [("ps", 4 + par * 2)], [("accn", br, gi % 4)], eng="act")
        self.cp(ad, pd, [("ps", 5 + par * 2)], [("accd", br, gi % 4)], eng=("act" if gi % 2 else "dve"))

    def att_finalize(self, h):
        s = h % 2
        nk = lambda br: [("accn", br, g) for g in range(4)]
        dk = lambda br: [("accd", br, g) for g in range(4)]
        v4 = lambda a: a[:].rearrange("p (r i) -> p i r", r=4)
        n4 = lambda a: a[:].rearrange("p (i r) -> p i r", r=4)
        v16 = lambda a: a[:].rearrange("p (r q) -> p q r", r=16)
        n16 = lambda a: a[:].rearrange("p (q r) -> p q r", r=16)
        for g in range(4):
            qs = slice(128 * g, 128 * (g + 1))
            self.tt(n4(self.nsum)[:, qs, :], n4(self.accn[0])[:, qs, :], v4(self.accn[1])[:, qs, :], ALU.add,
                    [("accn", 0, g)] + nk(1), [("nsum", g)], eng="pool")
            self.tt(n4(self.dsum)[:, qs, :], n4(self.accd[0])[:, qs, :], v4(self.accd[1])[:, qs, :], ALU.add,
                    [("accd", 0, g)] + dk(1), [("dsum", g)], eng="pool")
        sg = lambda nm: [(nm, g) for g in range(4)]
        self.tt(n16(self.dsum), n16(self.dsum), v16(self.accd[2]), ALU.add, sg("dsum") + dk(2), sg("dsum"),
                eng="pool")
        self.tt(n16(self.nsum), n16(self.nsum), v16(self.accn[2]), ALU.add, sg("nsum") + nk(2), sg("nsum"),
                eng="pool")

    def att_finalize_b(self, h, part):
        s = h % 2
        qs = slice(512 * part, 512 * (part + 1))
        self.actv(self.rden[:, qs], self.dsum[:, qs], AF.Ln, [("dsum", part)], [("rden", part)])
        self.actv(self.rden[:, qs], self.rden[:, qs], AF.Exp, [("rden", part)], [("rden", part)], scale=-1.0)
        if part == 3:
            sg = lambda nm: [(nm, g) for g in range(4)]
            ao = self.atto[s]
            self.tt(ao[:], self.nsum[:], self.rden[:], ALU.mult, sg("nsum") + sg("rden"), [("atto", s)],
                    eng="pool")
            self.dma(self.attT_d[h], ao[:], [("atto", s)], [], ("ao", s))

    def phaseB2(self):
        P = self.P
        P.fence()
        self.dma(self.TB[:], self.tc_d[:, 0, :], [], ["T"], "c2")
        self.dma(self.TC[:], self.tc_d[:, 1, :], [], ["T"], "c2")
        self.cp(self.TA[:], self.TB[:], ["T"], ["TA"])
        self.ts(self.TD[:], self.TB[:], self.vcol("pm"), None, ALU.add, None, ["T", "vecs"], ["T"])
        self.ts(self.TA[:, 0:128], self.TB[:, 0:128], self.vcol("pm"), None, ALU.add, None,
                ["T", "TA", "vecs"], ["T"])
        self.att_loads(0)
        for h in range(H):
            s = h % 2
            if h + 1 < H:
                self.att_loads(h + 1)
            groups = self.att_groups(s)
            self.att_vtiles(h)
            self.att_scores(h, 0, groups[0])
            for gi in range(len(groups)):
                if gi + 1 < len(groups):
                    self.att_scores(h, gi + 1, groups[gi + 1])
                self.att_pv(h, gi, groups[gi])
                if 4 <= gi < 8 and h > 0:
                    self.att_finalize_b(h - 1, gi - 4)
            self.att_finalize(h)
        for part in range(4):
            self.att_finalize_b(H - 1, part)
        P.fence()

    def phaseB3(self):
        for i in range(NTILE):
            cols = slice(i * TT, (i + 1) * TT)
            sk = [("stage", 0), ("stage", 1)]
            self.dma(self.hT[:, :, :], self.h1T_d[:, :, cols].rearrange("c p t -> p c t"), [("dram", "h1", i)],
                     [("hT", c) for c in range(16)], "h1i")
            self.dma(self.st16[:, :, :], self.attT_d[:, :, cols].rearrange("h e t -> e h t"), [], sk, "ati")
            self.proj_resid(self.wo_d, lambda c: self.st16[:, c, :], lambda c: ("stage", c // 8), None)
            self.ffn(self.g1_d, self.u1_d, self.d1_d, "f_g1")
            self.rmsnorm(lambda c: self.hT[:, c, :], lambda c: ("hT", c),
                         lambda c: self.v[:, c, :], lambda c: ("v", c), "fin_g", TT, stats="done")
            for tb in range(4):
                xs = self.xs[tb % 2]
                for cg in range(4):
                    b = self.bank()
                    for cc in range(4):
                        c = cg * 4 + cc
                        self.tr(self.ps[b][:, cc * 128:(cc + 1) * 128], self.v[:, c, tb * 128:(tb + 1) * 128],
                                self.ident[:], [("v", c), "ident"], [("ps", b)])
                    self.cp(xs[:, cg * 512:(cg + 1) * 512], self.ps[b][:], [("ps", b)], [("stage", tb % 2)],
                            eng=("act" if cg % 2 else "dve"))
                r0 = i * TT + tb * 128
                self.dma(self.out_d[r0:r0 + 128, :], xs, [("stage", tb % 2)], [], ("oo", tb % 2))

    def build(self):
        self.setup()
        if "A" in self.phase:
            self.phaseA()
        if "B" in self.phase:
            if "A" not in self.phase:
                self.phaseB1()
            self.phaseB2()
            self.phaseB3()
        self.P.emit()
        return self.nc


_NC_CACHE = {}


def get_nc(phase):
    if phase not in _NC_CACHE:
        _NC_CACHE[phase] = Builder(phase).build()
    return _NC_CACHE[phase]


def kernel(**inp):
    inp = {k: np.asarray(v) for k, v in inp.items()}
    x = inp["x"]
    ident, tconst = make_consts()
    ncores = 8
    c32 = lambda a: np.ascontiguousarray(a, dtype=np.float32)
    w = {"conv_w1": c32(inp["conv_w1"][0]), "conv_w2": c32(inp["conv_w2"][0]), "w_k": c32(inp["w_k"]),
         "w_v": c32(inp["w_v"]), "gate0": c32(inp["ffn_w_gate"][0]), "up0": c32(inp["ffn_w_up"][0]),
         "down0": c32(inp["ffn_w_down"][0]),
         "w_q": c32(inp["w_q"][0]), "w_o": c32(inp["w_o"][0]), "gate1": c32(inp["ffn_w_gate"][1]),
         "up1": c32(inp["ffn_w_up"][1]), "down1": c32(inp["ffn_w_down"][1]), "tconst": tconst,
         "ident": ident}
    maps = []
    for core in range(ncores):
        b, half = core // 2, core % 2
        t0 = half * NTOK
        xo = c32(x[b, t0:t0 + NTOK])
        xh = c32(x[b, t0 - HALO:t0]) if half == 1 else np.zeros((HALO, D), np.float32)
        m = {"x": xo, "xh": xh, "vecs": make_vecs(inp, half)}
        m.update(w)
        maps.append(m)
    res = run_bass_kernel_spmd(get_nc("AB"), maps, core_ids=list(range(ncores)))
    out = np.empty((4, SEQ, D), np.float32)
    for core in range(ncores):
        b, half = core // 2, core % 2
        out[b, half * NTOK:(half + 1) * NTOK] = res.results[core]["out"]
    return out
```

```python
import contextlib
import numpy as np
import ml_dtypes
import concourse.bass as bass
import concourse.mybir as mybir
from concourse.bass_utils import run_bass_kernel_spmd

F32 = mybir.dt.float32
BF16 = mybir.dt.bfloat16
AF = mybir.ActivationFunctionType
ALU = mybir.AluOpType

D = 2048
FF = 5632
NF = FF // 128
SEQ = 4096
NTOK = 2048
TT = 512
NTILE = NTOK // TT
H = 16
CW = 31
HALO = 32
RMS_EPS = 1e-6
LN_EPS = 1e-5
BIG = 30000.0
QSCALE = 128.0 ** -0.5

ENGS = ("pe", "act", "dve", "pool", "sp")
import os
DEBUG_NOCC = bool(os.environ.get("KDEBUG_NOCC"))
NO_PRECONV = False


class Op:
    __slots__ = ("eng", "idx", "fn", "deps", "signal", "dma_key", "dma_cnt", "cum", "inc")

    def __init__(self, eng, idx, fn):
        self.eng = eng
        self.idx = idx
        self.fn = fn
        self.deps = []
        self.signal = False
        self.dma_key = None
        self.dma_cnt = 0
        self.cum = 0
        self.inc = 16


class Prog:
    def __init__(self, nc):
        self.nc = nc
        self.ops = {e: [] for e in ENGS}
        self.last_w = {}
        self.readers = {}
        self.dma_cnt = {}
        self.dma_inc = {}

    def op(self, eng, fn, reads=(), writes=(), dma=None, inc=16):
        o = Op(eng, len(self.ops[eng]), fn)
        if dma is not None:
            o.dma_key = dma
            o.inc = inc
            self.dma_inc[dma] = inc
            self.dma_cnt[dma] = self.dma_cnt.get(dma, 0) + 1
            o.dma_cnt = self.dma_cnt[dma]
        deps = []
        for k in reads:
            w = self.last_w.get(k)
            if w is not None:
                deps.append(w)
        for k in writes:
            w = self.last_w.get(k)
            if w is not None:
                deps.append(w)
            for r in self.readers.get(k, {}).values():
                deps.append(r)
        seen = set()
        for d in deps:
            if d is o or id(d) in seen:
                continue
            seen.add(id(d))
            if d.dma_key is None and d.eng == eng:
                if eng in ("pe", "sp"):
                    continue
                if o.dma_key is None and o.idx - d.idx > 3:
                    continue
            o.deps.append(d)
            if d.dma_key is None:
                d.signal = True
        for k in reads:
            self.readers.setdefault(k, {})[(eng, o.dma_key)] = o
        for k in writes:
            self.last_w[k] = o
            self.readers[k] = {}
        self.ops[eng].append(o)
        return o

    def fence(self):
        lasts = []
        for e in ENGS:
            for o in reversed(self.ops[e]):
                if o.dma_key is None and o.fn is not None:
                    lasts.append(o)
                    break
        dmas = []
        for k, c in self.dma_cnt.items():
            p = Op("sp", -1, None)
            p.dma_key = k
            p.dma_cnt = c
            p.inc = self.dma_inc[k]
            dmas.append(p)
        for e in ENGS:
            o = Op(e, len(self.ops[e]), None)
            for d in lasts:
                if d.eng != e:
                    o.deps.append(d)
                    d.signal = True
            o.deps.extend(dmas)
            self.ops[e].append(o)
        self.last_w = {}
        self.readers = {}

    def emit(self):
        nc = self.nc
        with contextlib.ExitStack() as st:
            esem = {e: st.enter_context(nc.semaphore("s_" + e)) for e in ENGS}
            dsem = {}
            for n, k in enumerate(self.dma_cnt):
                dsem[k] = st.enter_context(nc.semaphore("d%d" % n))
            block = st.enter_context(nc.Block())
            for e in ENGS:
                c = 0
                for o in self.ops[e]:
                    if o.signal:
                        c += 1
                    o.cum = c

            def run(e, engine):
                waited = {}
                for o in self.ops[e]:
                    for d in o.deps:
                        if d.dma_key is not None:
                            s, v, key = dsem[d.dma_key], d.inc * d.dma_cnt, ("d", d.dma_key)
                        else:
                            s, v, key = esem[d.eng], d.cum, ("e", d.eng)
                        if waited.get(key, 0) >= v:
                            continue
                        waited[key] = v
                        engine.wait_ge(s, v)
                    if o.fn is None:
                        continue
                    ins = o.fn(engine)
                    if o.dma_key is not None:
                        ins.then_inc(dsem[o.dma_key], o.inc)
                    elif o.signal:
                        ins.then_inc(esem[e], 1)
                if e == "sp":
                    for k, c in self.dma_cnt.items():
                        if waited.get(("d", k), 0) < self.dma_inc[k] * c:
                            engine.wait_ge(dsem[k], self.dma_inc[k] * c)

            @block.tensor
            def _(eng):
                run("pe", eng)

            @block.scalar
            def _(eng):
                run("act", eng)

            @block.vector
            def _(eng):
                run("dve", eng)

            @block.gpsimd
            def _(eng):
                run("pool", eng)

            @block.sync
            def _(eng):
                run("sp", eng)


VEC_COLS = {}
_c = 0
for _n, _w in (("a_g", 16), ("b1", 32), ("dwb", 16), ("lng", 16), ("lnb", 16), ("b2", 16),
               ("kv_g", 16), ("b_g", 16), ("f_g0", 16), ("f_g1", 16), ("fin_g", 16),
               ("dw", 16 * CW), ("hm", 1), ("pm", 1)):
    VEC_COLS[_n] = (_c, _w)
    _c += _w
NV = _c


def _pack(v):
    return np.ascontiguousarray(np.asarray(v, np.float32).reshape(-1, 128).T)


def make_vecs(inp, half):
    vecs = np.zeros((128, NV), np.float32)

    def put(name, arr):
        s, w = VEC_COLS[name]
        vecs[:, s:s + w] = arr

    put("a_g", _pack(inp["a_norm_g"][0]))
    put("b1", _pack(inp["conv_b1"][0]))
    put("dwb", _pack(inp["conv_dw_b"][0]))
    put("lng", _pack(inp["conv_ln_g"][0]))
    put("lnb", _pack(inp["conv_ln_b"][0]))
    put("b2", _pack(inp["conv_b2"][0]))
    put("kv_g", _pack(inp["kv_norm_g"]))
    put("b_g", _pack(inp["b_norm_g"][0]))
    put("f_g0", _pack(inp["ffn_norm_g"][0]))
    put("f_g1", _pack(inp["ffn_norm_g"][1]))
    put("fin_g", _pack(inp["final_norm_g"]))
    dw = np.asarray(inp["conv_dw"][0], np.float32)
    dwp = dw.T.reshape(16, 128, CW).transpose(1, 0, 2).reshape(128, 16 * CW)
    put("dw", dwp)
    put("hm", np.full((128, 1), 1.0 if half == 1 else 0.0, np.float32))
    put("pm", np.full((128, 1), 0.0 if half == 1 else BIG, np.float32))
    return vecs


def make_consts():
    k = np.arange(128)[:, None]
    q = np.arange(128)[None, :]
    t0 = np.where(k >= q, q - k + 128, BIG).astype(np.float32)
    t1 = np.where(k <= q, q - k, BIG).astype(np.float32)
    tc = np.zeros((128, 2, 512), np.float32)
    tc[:, 0, :] = np.tile(t0, (1, 4))
    tc[:, 1, :] = np.tile(t1, (1, 4))
    return np.eye(128, dtype=np.float32), tc


class Builder:
    def __init__(self, phase):
        self.phase = phase
        nc = self.nc = bass.Bass("TRN2", target_bir_lowering=False)
        self.P = Prog(nc)
        self.nbank = 0
        self.nw = 0
        din = lambda n, s, dt=F32: nc.dram_tensor(n, s, dt, kind="ExternalInput").ap()
        dout = lambda n, s, dt=F32: nc.dram_tensor(n, s, dt, kind="ExternalOutput").ap()
        dint = lambda n, s, dt=F32: nc.dram_tensor(n, s, dt, kind="Internal").ap()
        self.vecs_d = din("vecs", [128, NV])
        self.ident_d = din("ident", [128, 128])
        hasA = "A" in phase
        hasB = "B" in phase
        if hasA:
            self.x_d = din("x", [NTOK, D])
            self.xh_d = din("xh", [HALO, D])
            self.w1_d = din("conv_w1", [D, 2 * D])
            self.w2_d = din("conv_w2", [D, D])
            self.wk_d = din("w_k", [D, D])
            self.wv_d = din("w_v", [D, D])
            self.g0_d = din("gate0", [D, FF])
            self.u0_d = din("up0", [D, FF])
            self.d0_d = din("down0", [FF, D])
        if hasB:
            self.tc_d = din("tconst", [128, 2, 512])
            self.wq_d = din("w_q", [D, D])
            self.wo_d = din("w_o", [D, D])
            self.g1_d = din("gate1", [D, FF])
            self.u1_d = din("up1", [D, FF])
            self.d1_d = din("down1", [FF, D])
            self.out_d = dout("out", [NTOK, D])
            self.qT_d = dint("qT", [H, 128, NTOK], BF16)
            self.attT_d = dint("attT", [H, 128, NTOK], BF16)
        KSH = [NTILE, H * 128, TT]
        assert phase == "AB", "only the fused program is supported"
        self.h1T_d = dint("h1T", [16, 128, NTOK])
        self.wscr = dint("wscr", [96, 128, 16 * 512], BF16)
        self.wmap = {}
        self.preconv = False
        self.preq = []
        self.k_own = dint("k_own", KSH, BF16)
        self.v_own = dint("v_own", KSH, BF16)
        self.k_all = dint("k_all", [NTILE, 2 * H * 128, TT], BF16)
        self.v_all = dint("v_all", [NTILE, 2 * H * 128, TT], BF16)
        self.k_prev = self.k_all[:, 0:H * 128, :]
        self.v_prev = self.v_all[:, 0:H * 128, :]
        self.alloc()

    def alloc(self):
        nc = self.nc
        self.ps = [nc.alloc_psum_tensor("ps%d" % i, [128, 512], F32) for i in range(8)]
        off = [0]
        NB = 101 * 1024
        arena = self.arena = nc.alloc_sbuf_tensor("arena", [128, NB], BF16)

        def carve(nbytes, dt=BF16):
            n = nbytes // 2
            a = arena[:, off[0]:off[0] + n]
            off[0] += n
            assert off[0] <= NB, off[0]
            return a.bitcast(F32) if dt == F32 else a

        self.carve = carve
        self.vecs = nc.alloc_sbuf_tensor("vecs_sb", [128, NV], F32)
        self.ident = nc.alloc_sbuf_tensor("ident_sb", [128, 128], F32)
        self.identb = nc.alloc_sbuf_tensor("identb", [128, 128], BF16)
        self.onesb = nc.alloc_sbuf_tensor("onesb", [128, 128], BF16)
        self.onesf = nc.alloc_sbuf_tensor("onesf", [128, 128], F32)
        self.eps_rms = nc.alloc_sbuf_tensor("eps_rms", [128, 1], F32)
        self.eps_ln = nc.alloc_sbuf_tensor("eps_ln", [128, 1], F32)
        self.m0 = off[0]
        self.hT = carve(16 * 512 * 4, F32).rearrange("p (c t) -> p c t", c=16)
        self.xn = carve(16 * 512 * 2).rearrange("p (c t) -> p c t", c=16)
        u0 = off[0]
        self.u = carve(16 * 544 * 2).rearrange("p (c t) -> p c t", c=16)
        self.v = carve(16 * 512 * 4, F32).rearrange("p (c t) -> p c t", c=16)
        u1 = off[0]
        self.act = arena[:, u0:u0 + NF * 512].rearrange("p (c t) -> p c t", c=NF)
        assert u0 + NF * 512 <= u1
        s0 = off[0]
        stage = carve(16 * 512 * 2)
        self.stage = stage
        self.st16 = stage.rearrange("p (c t) -> p c t", c=16)
        self.vst = stage.rearrange("p (b f) -> p b f", b=4)
        self.xs = [arena[:, s0 + i * 4096:s0 + (i + 1) * 4096].bitcast(F32) for i in range(2)]
        self.wbuf = [carve(16 * 512 * 2).rearrange("p (k n) -> p k n", k=16) for _ in range(3)]
        self.dg = [carve(CW * 128 * 2).rearrange("p (j m) -> p j m", j=CW) for _ in range(2)]
        self.sq = [carve(512 * 2) for _ in range(4)]
        self.tmp = [carve(512 * 4, F32) for _ in range(3)]
        self.rstd = carve(512 * 4, F32)
        self.lnA = carve(512 * 4, F32)
        self.lnB = carve(512 * 4, F32)
        self.mean = carve(512 * 4, F32)
        self.hTh = carve(16 * HALO * 4, F32).rearrange("p (c t) -> p c t", c=16)
        self.xnh = carve(16 * HALO * 2).rearrange("p (c t) -> p c t", c=16)
        self.uhs = carve(16 * HALO * 2).rearrange("p (c t) -> p c t", c=16)
        self.m1 = off[0]
        if "B" in self.phase:
            off[0] = self.m0
            self.kTs = [carve(2 * NTOK * 2) for _ in range(2)]
            self.qTs = [carve(NTOK * 2) for _ in range(2)]
            self.V1 = [carve(17 * 128 * 2).rearrange("p (t e) -> p t e", e=128) for _ in range(2)]
            self.V4 = [carve(20 * 128 * 2).rearrange("p (r t e) -> p r t e", r=4, e=128) for _ in range(2)]
            self.V16 = [carve(32 * 128 * 2).rearrange("p (r t e) -> p r t e", r=16, e=128) for _ in range(2)]
            self.vTs = [carve(2 * NTOK * 2) for _ in range(2)]
            self.accn = [carve(NTOK * 4, F32) for _ in range(3)]
            self.accd = [carve(NTOK * 4, F32) for _ in range(3)]
            self.nsum = carve(NTOK * 4, F32)
            self.dsum = carve(NTOK * 4, F32)
            self.rden = carve(NTOK * 4, F32)
            self.atto = [carve(NTOK * 2) for _ in range(2)]
            self.sc = [[carve(512 * 4, F32) for _ in range(2)] for _ in range(2)]
            self.pT = [[carve(512 * 2) for _ in range(2)] for _ in range(2)]
            self.TA = carve(512 * 4, F32)
            self.TB = carve(512 * 4, F32)
            self.TD = carve(512 * 4, F32)
            self.TC = carve(512 * 4, F32)
            assert off[0] <= self.m1, (off[0], self.m1)
            off[0] = self.m1

    def vcol(self, name, c=0, w=1):
        s, _ = VEC_COLS[name]
        return self.vecs[:, s + c:s + c + w]

    def bank(self):
        b = self.nbank % 6
        self.nbank += 1
        return b

    def mm(self, out, lhsT, rhs, start, stop, reads, writes):
        self.P.op("pe", lambda e: e.matmul(out, lhsT=lhsT, rhs=rhs, start=start, stop=stop),
                  reads=reads, writes=writes)

    def mm_items(self, items, src, skey, n, dc_outer):
        if dc_outer:
            for dc in range(16):
                for p, b, cs in items:
                    self.mm(self.ps[p][:, 0:n], self.wbuf[b][:, dc, cs], src(dc), dc == 0, dc == 15,
                            [("w", b), skey(dc)], [("ps", p)])
        else:
            for p, b, cs in items:
                for dc in range(16):
                    self.mm(self.ps[p][:, 0:n], self.wbuf[b][:, dc, cs], src(dc), dc == 0, dc == 15,
                            [("w", b), skey(dc)], [("ps", p)])

    def tr(self, out, in_, ident, reads, writes):
        self.P.op("pe", lambda e: e.transpose(out=out, in_=in_, identity=ident), reads=reads, writes=writes)

    def actv(self, out, in_, func, reads, writes, bias=0.0, scale=1.0):
        self.P.op("act", lambda e: e.activation(out=out, in_=in_, func=func, bias=bias, scale=scale),
                  reads=reads, writes=writes)

    def stt(self, out, in0, scalar, in1, op0, op1, reads, writes, eng="dve"):
        self.P.op(eng, lambda e: e.scalar_tensor_tensor(out=out, in0=in0, scalar=scalar, in1=in1, op0=op0, op1=op1),
                  reads=reads, writes=writes)

    def tt(self, out, in0, in1, op, reads, writes, eng="dve"):
        self.P.op(eng, lambda e: e.tensor_tensor(out=out, in0=in0, in1=in1, op=op), reads=reads, writes=writes)

    def ts(self, out, in0, s1, s2, op0, op1, reads, writes, eng="dve"):
        if s2 is None:
            self.P.op(eng, lambda e: e.tensor_scalar(out=out, in0=in0, scalar1=s1, scalar2=None, op0=op0),
                      reads=reads, writes=writes)
        else:
            self.P.op(eng, lambda e: e.tensor_scalar(out=out, in0=in0, scalar1=s1, scalar2=s2, op0=op0, op1=op1),
                      reads=reads, writes=writes)

    def cp(self, out, in_, reads, writes, eng="dve"):
        if eng == "act":
            self.P.op("act", lambda e: e.copy(out=out, in_=in_), reads=reads, writes=writes)
        else:
            self.P.op(eng, lambda e: e.tensor_copy(out=out, in_=in_), reads=reads, writes=writes)

    def dma(self, out, in_, reads, writes, key, eng="sp"):
        return self.P.op(eng, lambda e: e.dma_start(out=out, in_=in_), reads=reads, writes=writes, dma=key)

    def wload(self, src, nk=16, wid=None):
        b = self.nw % 3
        self.nw += 1
        dst = self.wbuf[b][:, 0:nk, :]
        if wid not in self.wmap:
            idx = self.wmap[wid] = len(self.wmap)
            self.dma(dst, src.rearrange("(k p) n -> p k n", p=128), reads=[], writes=[("w", b)],
                     key=("w", b), eng="pool")
            self.dma(self.wscr[idx][:, 0:nk * 512].rearrange("p (k n) -> p k n", k=nk), dst,
                     reads=[("w", b)], writes=[("dram", "wscr", idx)], key="wst")
        else:
            idx = self.wmap[wid]
            self.dma(dst, self.wscr[idx][:, 0:nk * 512].rearrange("p (k n) -> p k n", k=nk),
                     reads=[("dram", "wscr", idx)], writes=[("w", b)], key=("w", b), eng="pool")
        return b

    def plan_preconvert(self):
        q = []
        for j in range(4):
            q.append((self.wo_d[:, 512 * j:512 * j + 512], 16, (self.wo_d.name, j)))
        for j in range(NF // 4):
            q.append((self.g1_d[:, 512 * j:512 * j + 512], 16, (self.g1_d.name, j)))
            q.append((self.u1_d[:, 512 * j:512 * j + 512], 16, (self.u1_d.name, j)))
        for qq in range(4):
            for g in range(3):
                nk = 16 if g < 2 else NF - 32
                q.append((self.d1_d[g * 2048:g * 2048 + nk * 128, 512 * qq:512 * qq + 512], nk,
                          (self.d1_d.name, g, qq)))
        self.preq = q
        q0 = []
        for j in range(4):
            q0.append((self.g0_d[:, 512 * j:512 * j + 512], 16, (self.g0_d.name, j)))
            q0.append((self.u0_d[:, 512 * j:512 * j + 512], 16, (self.u0_d.name, j)))
        self.preq0 = q0

    def preconvert_one(self, queue=None):
        queue = self.preq if queue is None else queue
        if not queue or NO_PRECONV:
            return
        src, nk, wid = queue.pop(0)
        idx = self.wmap[wid] = len(self.wmap)
        self.dma(self.wscr[idx][:, 0:nk * 512].rearrange("p (k n) -> p k n", k=nk),
                 src.rearrange("(k p) n -> p k n", p=128), reads=[], writes=[("dram", "wscr", idx)],
                 key="wpc", eng="pool")

    def setup(self):
        self.dma(self.vecs[:], self.vecs_d, [], ["vecs"], "c0")
        self.dma(self.ident[:], self.ident_d, [], ["ident"], "c1")
        self.cp(self.identb[:], self.ident[:], ["ident"], ["identb"])
        self.P.op("dve", lambda e: e.memset(self.onesb[:], 1.0), writes=["onesb"])
        self.P.op("dve", lambda e: e.memset(self.onesf[:], 1.0), writes=["onesf"])
        self.P.op("dve", lambda e: e.memset(self.eps_rms[:], RMS_EPS), writes=["eps"])
        self.P.op("dve", lambda e: e.memset(self.eps_ln[:], LN_EPS), writes=["eps"])

    def rms_stat(self, c, src, skey, n=TT):
        S = self.ps[6]
        sq = self.sq[c % 4]
        self.actv(sq[:, 0:n], src(c), AF.Square, [skey(c)], [("sq", c % 4)])
        self.mm(S[:, 0:n], self.onesb[:], sq[:, 0:n], c == 0, c == 15, [("sq", c % 4), "onesb"], [("ps", 6)])

    def rmsnorm(self, src, skey, dst, dkey, gname, n, stats="compute"):
        S = self.ps[6]
        r = self.rstd[:, 0:n]
        if stats == "compute":
            for c in range(16):
                self.rms_stat(c, src, skey, n)
        if stats != "reuse":
            self.actv(r, S[:, 0:n], AF.Sqrt, [("ps", 6)], ["rstd"], bias=self.eps_rms[:], scale=1.0 / D)
            self.P.op("dve", lambda e: e.reciprocal(out=r, in_=r), reads=["rstd"], writes=["rstd"])
        for c in range(16):
            self.stt(dst(c), src(c), self.vcol(gname, c), r, ALU.mult, ALU.mult,
                     [skey(c), "rstd", "vecs"], [dkey(c)])

    def load_x_tile(self, i):
        for tb in range(4):
            xs = self.xs[tb % 2]
            r0 = i * TT + tb * 128
            self.dma(xs, self.x_d[r0:r0 + 128, :], [], [("stage", tb % 2)], ("xs", tb % 2))
            for cg in range(4):
                b = self.bank()
                for cc in range(4):
                    c = cg * 4 + cc
                    self.tr(self.ps[b][:, cc * 128:(cc + 1) * 128], xs[:, c * 128:(c + 1) * 128], self.ident[:],
                            [("stage", tb % 2), "ident"], [("ps", b)])
                self.cp(self.hT[:, cg * 4:(cg + 1) * 4, tb * 128:(tb + 1) * 128],
                        self.ps[b][:].rearrange("p (c t) -> p c t", c=4), [("ps", b)],
                        [("hT", cg * 4 + k) for k in range(4)], eng=("act" if cg % 2 else "dve"))

    def load_halo(self):
        self.xsh = self.xs[1]
        self.dma(self.xsh[0:HALO, :], self.xh_d, [], [("stage", 1)], "xsh")
        for cg in range(4):
            b = self.bank()
            for cc in range(4):
                c = cg * 4 + cc
                self.tr(self.ps[b][:, cc * HALO:(cc + 1) * HALO], self.xsh[0:HALO, c * 128:(c + 1) * 128],
                        self.ident[0:HALO, 0:HALO], [("stage", 1), "ident"], [("ps", b)])
            self.cp(self.hTh[:, cg * 4:(cg + 1) * 4, :],
                    self.ps[b][:, 0:4 * HALO].rearrange("p (c t) -> p c t", c=4), [("ps", b)],
                    [("hTh", cg * 4 + k) for k in range(4)])

    def w1_glu(self, src, skey, n, dst, dkey, halo):
        for j in range(4):
            ba = self.wload(self.w1_d[:, 512 * j:512 * j + 512], wid=("w1a", j))
            bg = self.wload(self.w1_d[:, D + 512 * j:D + 512 * j + 512], wid=("w1g", j))
            for half in range(2):
                ccs = (2 * half, 2 * half + 1)
                bk = {cc: (self.bank(), self.bank()) for cc in ccs}
                items = []
                for cc in ccs:
                    cs = slice(cc * 128, (cc + 1) * 128)
                    items += [(bk[cc][0], ba, cs), (bk[cc][1], bg, cs)]
                self.mm_items(items, src, skey, n, dc_outer=(j == 0 and half == 0))
                for cc in ccs:
                    c = 4 * j + cc
                    pa, pg = bk[cc]
                    t = self.tmp[c % 3]
                    self.actv(t[:, 0:n], self.ps[pg][:, 0:n], AF.Sigmoid, [("ps", pg), "vecs"],
                              [("tmp", c % 3)], bias=self.vcol("b1", 16 + c))
                    self.stt(dst(c), self.ps[pa][:, 0:n], self.vcol("b1", c), t[:, 0:n], ALU.add, ALU.mult,
                             [("ps", pa), ("tmp", c % 3), "vecs"], [dkey(c)])
                    if halo:
                        self.ts(dst(c), dst(c), self.vcol("hm"), None, ALU.mult, None, [dkey(c), "vecs"],
                                [dkey(c)])

    def conv_ln(self, i):
        S1, S2 = self.ps[6], self.ps[7]
        if i > 0:
            self.exchange(i - 1)
            self.cp(self.u[:, :, 0:HALO], self.uhs[:, :, :], ["uhs"], [("uh", c) for c in range(16)], eng="dve")
        for c in range(16):
            if i == 0 and c % 2 == 1:
                self.preconvert_one(self.preq0)
            elif i >= 1 and c % 4 == 3:
                self.preconvert_one()
            dg = self.dg[c % 2]
            s, _ = VEC_COLS["dw"]
            dwc = self.vecs[:, s + c * CW:s + (c + 1) * CW]
            self.tt(dg[:], self.identb[:].unsqueeze(1).broadcast_to([128, CW, 128]),
                    dwc.unsqueeze(2).broadcast_to([128, CW, 128]), ALU.mult,
                    ["identb", "vecs"], [("dg", c % 2)])
            b = self.bank()
            for j in range(CW):
                self.mm(self.ps[b][:], dg[:, j, :], self.u[:, c, HALO - (CW - 1 - j):HALO - (CW - 1 - j) + TT], j == 0, j == CW - 1,
                        [("dg", c % 2), ("u", c), ("uh", c)], [("ps", b)])
            self.actv(self.v[:, c, :], self.ps[b][:], AF.Identity, [("ps", b), "vecs"], [("v", c)],
                      bias=self.vcol("dwb", c))
            sq = self.sq[c % 4]
            self.actv(sq[:], self.ps[b][:], AF.Square, [("ps", b), "vecs"], [("sq", c % 4)],
                      bias=self.vcol("dwb", c))
            self.mm(S1[:], self.onesf[:], self.v[:, c, :], c == 0, c == 15, [("v", c), "onesf"], [("ps", 6)])
            self.mm(S2[:], self.onesb[:], sq[:], c == 0, c == 15, [("sq", c % 4), "onesb"], [("ps", 7)])
        if i < NTILE - 1:
            self.cp(self.uhs[:, :, :], self.u[:, :, TT:TT + HALO], [("u", c) for c in range(16)],
                    ["uhs"], eng="dve")
        mean, A, B = self.mean, self.lnA, self.lnB
        self.ts(mean[:], S1[:], 1.0 / D, None, ALU.mult, None, [("ps", 6)], ["mean"])
        self.tt(B[:], mean[:], mean[:], ALU.mult, ["mean"], ["lnB"])
        self.stt(A[:], S2[:], 1.0 / D, B[:], ALU.mult, ALU.subtract, [("ps", 7), "lnB"], ["lnA"])
        self.actv(A[:], A[:], AF.Sqrt, ["lnA"], ["lnA"], bias=self.eps_ln[:], scale=1.0)
        self.P.op("dve", lambda e: e.reciprocal(out=A[:], in_=A[:]), reads=["lnA"], writes=["lnA"])
        self.stt(B[:], mean[:], -1.0, A[:], ALU.mult, ALU.mult, ["mean", "lnA"], ["lnB"])
        for c in range(16):
            self.tt(self.v[:, c, :], self.v[:, c, :], A[:], ALU.mult, [("v", c), "lnA"], [("v", c)])
            if c > 0:
                self.ln_tail(c - 1, B)
        self.ln_tail(15, B)

    def ln_tail(self, c, B):
        if True:
            self.tt(self.v[:, c, :], self.v[:, c, :], B[:], ALU.add, [("v", c), "lnB"], [("v", c)])
            self.actv(self.xn[:, c, :], self.v[:, c, :], AF.Silu, [("v", c), "vecs"], [("xn", c)],
                      bias=self.vcol("lnb", c), scale=self.vcol("lng", c))

    def proj_resid(self, w_d, src, skey, bias_name):
        for j in range(4):
            b = self.wload(w_d[:, 512 * j:512 * j + 512], wid=(w_d.name, j))
            pbk = [self.bank() for _ in range(4)] if j == 0 else None
            if j == 0:
                self.mm_items([(pbk[cc], b, slice(cc * 128, (cc + 1) * 128)) for cc in range(4)], src, skey, TT, True)
            for cc in range(4):
                c = 4 * j + cc
                if j == 0:
                    p = pbk[cc]
                else:
                    p = self.bank()
                    self.mm_items([(p, b, slice(cc * 128, (cc + 1) * 128))], src, skey, TT, False)
                if bias_name is not None:
                    self.stt(self.hT[:, c, :], self.ps[p][:], self.vcol(bias_name, c), self.hT[:, c, :],
                             ALU.add, ALU.add, [("ps", p), ("hT", c), "vecs"], [("hT", c)])
                else:
                    self.tt(self.hT[:, c, :], self.ps[p][:], self.hT[:, c, :], ALU.add,
                            [("ps", p), ("hT", c)], [("hT", c)])
            if j > 0:
                for c in range(4 * (j - 1), 4 * j):
                    self.rms_stat(c, lambda c: self.hT[:, c, :], lambda c: ("hT", c))
        for c in range(12, 16):
            self.rms_stat(c, lambda c: self.hT[:, c, :], lambda c: ("hT", c))

    def proj_out(self, w_d, dst, dkey, scale, eng):
        for j in range(4):
            b = self.wload(w_d[:, 512 * j:512 * j + 512], wid=(w_d.name, j))
            xsrc, xkey = (lambda dc: self.xn[:, dc, :]), (lambda dc: ("xn", dc))
            pbk = [self.bank() for _ in range(4)] if j == 0 else None
            if j == 0:
                self.mm_items([(pbk[cc], b, slice(cc * 128, (cc + 1) * 128)) for cc in range(4)], xsrc, xkey, TT, True)
            for cc in range(4):
                c = 4 * j + cc
                if j == 0:
                    p = pbk[cc]
                else:
                    p = self.bank()
                    self.mm_items([(p, b, slice(cc * 128, (cc + 1) * 128))], xsrc, xkey, TT, False)
                if scale is not None:
                    self.actv(dst(c), self.ps[p][:], AF.Identity, [("ps", p)], dkey(c), scale=scale)
                else:
                    self.cp(dst(c), self.ps[p][:], [("ps", p)], dkey(c), eng=eng)

    def ffn(self, g_d, u_d, d_d, gname):
        self.rmsnorm(lambda c: self.hT[:, c, :], lambda c: ("hT", c),
                     lambda c: self.xn[:, c, :], lambda c: ("xn", c), gname, TT, stats="done")
        for j in range(NF // 4):
            bg = self.wload(g_d[:, 512 * j:512 * j + 512], wid=(g_d.name, j))
            bu = self.wload(u_d[:, 512 * j:512 * j + 512], wid=(u_d.name, j))
            if self.preconv and j % 2 == 0:
                self.preconvert_one()
            xsrc, xkey = (lambda dc: self.xn[:, dc, :]), (lambda dc: ("xn", dc))
            for half in range(2):
                ccs = (2 * half, 2 * half + 1)
                bk = {cc: (self.bank(), self.bank()) for cc in ccs}
                items = []
                for cc in ccs:
                    cs = slice(cc * 128, (cc + 1) * 128)
                    items += [(bk[cc][0], bg, cs), (bk[cc][1], bu, cs)]
                self.mm_items(items, xsrc, xkey, TT, dc_outer=(j == 0 and half == 0))
                for cc in ccs:
                    f = 4 * j + cc
                    pg, pu = bk[cc]
                    t = self.tmp[f % 3]
                    self.actv(t[:], self.ps[pg][:], AF.Silu, [("ps", pg)], [("tmp", f % 3)])
                    self.tt(self.act[:, f, :], t[:], self.ps[pu][:], ALU.mult, [("tmp", f % 3), ("ps", pu)],
                            [("act", f)])
        for q in range(4):
            banks = [self.bank() for _ in range(4)]
            for g in range(3):
                nk = 16 if g < 2 else NF - 32
                b = self.wload(d_d[g * 2048:g * 2048 + nk * 128, 512 * q:512 * q + 512], nk, wid=(d_d.name, g, q))
                if self.preconv and g == 0:
                    self.preconvert_one()
                for cc in range(4):
                    for fk in range(nk):
                        f = g * 16 + fk
                        self.mm(self.ps[banks[cc]][:], self.wbuf[b][:, fk, cc * 128:(cc + 1) * 128],
                                self.act[:, f, :], f == 0, f == NF - 1, [("w", b), ("act", f)],
                                [("ps", banks[cc])])
            if q > 0:
                for c in range(4 * (q - 1), 4 * q):
                    self.rms_stat(c, lambda c: self.hT[:, c, :], lambda c: ("hT", c))
            for cc in range(4):
                c = 4 * q + cc
                self.tt(self.hT[:, c, :], self.ps[banks[cc]][:], self.hT[:, c, :], ALU.add,
                        [("ps", banks[cc]), ("hT", c)], [("hT", c)])
        for c in range(12, 16):
            self.rms_stat(c, lambda c: self.hT[:, c, :], lambda c: ("hT", c))

    def phaseA(self):
        self.plan_preconvert()
        for i in range(NTILE):
            cols = slice(i * TT, (i + 1) * TT)
            self.load_x_tile(i)
            self.rmsnorm(lambda c: self.hT[:, c, :], lambda c: ("hT", c),
                         lambda c: self.xn[:, c, :], lambda c: ("xn", c), "a_g", TT)
            self.w1_glu(lambda c: self.xn[:, c, :], lambda c: ("xn", c), TT,
                        lambda c: self.u[:, c, HALO:HALO + TT], lambda c: ("u", c), False)
            if i == 0:
                self.load_halo()
                self.rmsnorm(lambda c: self.hTh[:, c, :], lambda c: ("hTh", c),
                             lambda c: self.xnh[:, c, :], lambda c: ("xnh", c), "a_g", HALO)
                self.w1_glu(lambda c: self.xnh[:, c, :], lambda c: ("xnh", c), HALO,
                            lambda c: self.u[:, c, 0:HALO], lambda c: ("uh", c), True)
            self.conv_ln(i)
            self.proj_resid(self.w2_d, lambda c: self.xn[:, c, :], lambda c: ("xn", c), "b2")
            self.preconv = (i >= 1 and not NO_PRECONV)
            self.ffn(self.g0_d, self.u0_d, self.d0_d, "f_g0")
            self.preconv = False
            self.dma(self.h1T_d[:, :, cols].rearrange("c p t -> p c t"), self.hT[:, :, :],
                     [("hT", c) for c in range(16)], [("dram", "h1", i)], "h1o")
            self.rmsnorm(lambda c: self.hT[:, c, :], lambda c: ("hT", c),
                         lambda c: self.xn[:, c, :], lambda c: ("xn", c), "kv_g", TT, stats="done")
            skh = lambda c: [("stage", c // 8)]
            for w_d, dst_d, nm, key, eng in ((self.wk_d, self.k_own[i], "k", "kto", "act"),
                                             (self.wv_d, self.v_own[i], "v", "vo", "dve")):
                self.proj_out(w_d, lambda c: self.st16[:, c, :], skh, None, eng)
                dv = dst_d.rearrange("(h e) t -> e h t", e=128)
                for hh in range(2):
                    self.dma(dv[:, 8 * hh:8 * hh + 8, :], self.st16[:, 8 * hh:8 * hh + 8, :], [("stage", hh)],
                             [("dram", nm, i)], key)
            if i == NTILE - 1:
                self.exchange(i, ("k",))
            if "B" in self.phase:
                self.rmsnorm(lambda c: self.hT[:, c, :], lambda c: ("hT", c),
                             lambda c: self.xn[:, c, :], lambda c: ("xn", c), "b_g", TT, stats="reuse")
                self.proj_out(self.wq_d, lambda c: self.st16[:, c, :], skh, QSCALE, "act")
                qv = self.qT_d[:, :, cols].rearrange("h e t -> e h t")
                for hh in range(2):
                    self.dma(qv[:, 8 * hh:8 * hh + 8, :], self.st16[:, 8 * hh:8 * hh + 8, :], [("stage", hh)], [],
                             "qo")
        self.exchange(NTILE - 1, ("v",))

    def exchange(self, i, which=("k", "v")):
        if self.phase != "AB" or DEBUG_NOCC:
            return
        rg = [[0, 1], [2, 3], [4, 5], [6, 7]]
        for nm, own, allb in (("k", self.k_own, self.k_all), ("v", self.v_own, self.v_all)):
            if nm not in which:
                continue
            self.P.op("pool", lambda e, own=own, allb=allb: e.collective_compute(
                "AllGather", ALU.bypass, replica_groups=rg,
                ins=[own[i].rearrange("(a b) t -> a (b t)", b=4)],
                outs=[allb[i].rearrange("(a b) t -> a (b t)", b=4)]),
                reads=[("dram", nm, i)], writes=[("dram", nm + "all")], dma="cc", inc=1)

    def phaseB1(self):
        for i in range(NTILE):
            cols = slice(i * TT, (i + 1) * TT)
            self.dma(self.hT[:, :, :], self.h1T_d[:, :, cols].rearrange("c p t -> p c t"), [("dram", "h1", i)],
                     [("hT", c) for c in range(16)], "h1i")
            self.rmsnorm(lambda c: self.hT[:, c, :], lambda c: ("hT", c),
                         lambda c: self.xn[:, c, :], lambda c: ("xn", c), "b_g", TT)
            sk = [("stage", 0), ("stage", 1)]
            self.proj_out(self.wq_d, lambda c: self.st16[:, c, :], lambda c: sk, QSCALE, "act")
            self.dma(self.qT_d[:, :, cols].rearrange("h e t -> e h t"), self.st16[:, :, :], sk, [], "qo")

    def att_loads(self, h):
        s = h % 2
        hs = slice(h * 128, (h + 1) * 128)
        for dst, prev, own, key, dk in ((self.kTs[s], self.k_prev, self.k_own, ("kTs", s), "kall"),
                                        (self.vTs[s], self.v_prev, self.v_own, ("vTs", s), "vall")):
            self.dma(dst[:, 0:NTOK].rearrange("p (i t) -> p i t", i=NTILE),
                     prev[:, hs, :].rearrange("i e t -> e i t"), [("dram", dk)], [key], ("kl", s))
            self.dma(dst[:, NTOK:2 * NTOK].rearrange("p (i t) -> p i t", i=NTILE),
                     own[:, hs, :].rearrange("i e t -> e i t"), [], [key], ("kl", s))
        self.dma(self.qTs[s][:], self.qT_d[h], [], [("qTs", s)], ("ql", s))

    def att_vtiles(self, h):
        s = h % 2
        vT = self.vTs[s]
        jobs = []
        cols = [slice(NTOK + 128 * kb, NTOK + 128 * (kb + 1)) for kb in range(-1, 16)]
        jobs.append((self.V1[s].rearrange("p t e -> p (t e)"), cols))
        cols = []
        for r in range(4):
            for kb in range(-1, 4):
                st = NTOK + r + 512 * kb
                cols.append(slice(st, st + 509, 4))
        jobs.append((self.V4[s].rearrange("p r t e -> p (r t e)"), cols))
        cols = []
        for r in range(16):
            for kb in range(-1, 1):
                st = NTOK * (kb + 1) + r
                cols.append(slice(st, st + 2033, 16))
        jobs.append((self.V16[s].rearrange("p r t e -> p (r t e)"), cols))
        n = 0
        for dest, cols in jobs:
            for c0 in range(0, len(cols), 8):
                grp = cols[c0:c0 + 8]
                b = n % 4
                n += 1
                pb = self.ps[b][:].bitcast(BF16)
                for k, cs in enumerate(grp):
                    self.tr(pb[:, 128 * k:128 * (k + 1)], vT[:, cs], self.identb[:],
                            [("vTs", s), "identb"], [("ps", b)])
                self.cp(dest[:, 128 * c0:128 * (c0 + len(grp))], pb[:, 0:128 * len(grp)], [("ps", b)],
                        [("V", s)], eng=("act" if n % 2 else "dve"))

    def att_groups(self, s):
        kT, qT = self.kTs[s], self.qTs[s]
        groups = []
        for g in range(4):
            blocks = []
            for bb in range(4):
                nb = 4 * g + bb
                blocks.append((kT[:, NTOK + 128 * (nb - 1):NTOK + 128 * nb],
                               kT[:, NTOK + 128 * nb:NTOK + 128 * (nb + 1)],
                               qT[:, 128 * nb:128 * (nb + 1)],
                               self.V1[s][:, nb, :], self.V1[s][:, nb + 1, :]))
            groups.append((self.TA if g == 0 else self.TB, blocks,
                           (lambda a, g=g: a[:, 512 * g:512 * (g + 1)]), 1))
        for r in range(4):
            blocks = []
            for nb in range(4):
                kb0 = NTOK + r + 512 * (nb - 1)
                kb1 = NTOK + r + 512 * nb
                blocks.append((kT[:, kb0:kb0 + 509:4], kT[:, kb1:kb1 + 509:4],
                               qT[:, r + 512 * nb:r + 512 * nb + 509:4],
                               self.V4[s][:, r, nb, :], self.V4[s][:, r, nb + 1, :]))
            groups.append((self.TA, blocks, (lambda a, r=r: a[:, 512 * r:512 * (r + 1)]), 4))
        for g in range(4):
            blocks = []
            for bb in range(4):
                r = 4 * g + bb
                blocks.append((kT[:, r:NTOK:16], kT[:, NTOK + r:2 * NTOK:16], qT[:, r:NTOK:16],
                               self.V16[s][:, r, 0, :], self.V16[s][:, r, 1, :]))
            groups.append((self.TD, blocks, (lambda a, g=g: a[:, 512 * g:512 * (g + 1)]), 16))
        return groups

    def att_scores(self, h, gi, grp):
        s = h % 2
        Tx, blocks, accf, d = grp
        par = gi % 2
        X, Y = self.ps[par * 2], self.ps[par * 2 + 1]
        for bb, (kp, kc, q, vp, vc) in enumerate(blocks):
            self.mm(X[:, 128 * bb:128 * (bb + 1)], kp, q, True, True, [("kTs", s), ("qTs", s)], [("ps", par * 2)])
        for bb, (kp, kc, q, vp, vc) in enumerate(blocks):
            self.mm(Y[:, 128 * bb:128 * (bb + 1)], kc, q, True, True, [("kTs", s), ("qTs", s)],
                    [("ps", par * 2 + 1)])
        slope = 2.0 ** (-8.0 * (h + 1) / H)
        for w, (T, Pb) in enumerate(((Tx, X), (self.TC, Y))):
            scb, ptb = self.sc[par][w], self.pT[par][w]
            self.stt(scb[:], T[:], -slope * d, Pb[:], ALU.mult, ALU.add, ["T", ("ps", par * 2 + w)],
                     [("sc", par, w)])
            self.actv(ptb[:], scb[:], AF.Exp, [("sc", par, w)], [("pT", par, w)])

    def att_pv(self, h, gi, grp):
        s = h % 2
        Tx, blocks, accf, d = grp
        par = gi % 2
        N_, Dn = self.ps[4 + par * 2], self.ps[5 + par * 2]
        pX, pY = self.pT[par][0], self.pT[par][1]
        for bb, (kp, kc, q, vp, vc) in enumerate(blocks):
            cs = slice(128 * bb, 128 * (bb + 1))
            self.mm(N_[:, cs], vp, pX[:, cs], True, False, [("V", s), ("pT", par, 0)], [("ps", 4 + par * 2)])
            self.mm(N_[:, cs], vc, pY[:, cs], False, True, [("V", s), ("pT", par, 1)], [("ps", 4 + par * 2)])
        self.mm(Dn[:], self.onesb[:], pX[:], True, False, ["onesb", ("pT", par, 0)], [("ps", 5 + par * 2)])
        self.mm(Dn[:], self.onesb[:], pY[:], False, True, ["onesb", ("pT", par, 1)], [("ps", 5 + par * 2)])
        br = gi // 4
        an, ad = accf(self.accn[br]), accf(self.accd[br])
        pn, pd = N_[:], Dn[:]
        self.cp(an, pn, [("ps", 4 + par * 2)], [("accn", br, gi % 4)], eng="act")
        self.cp(ad, pd, [("ps", 5 + par * 2)], [("accd", br, gi % 4)], eng=("act" if gi % 2 else "dve"))

    def att_finalize(self, h):
        s = h % 2
        nk = lambda br: [("accn", br, g) for g in range(4)]
        dk = lambda br: [("accd", br, g) for g in range(4)]
        v4 = lambda a: a[:].rearrange("p (r i) -> p i r", r=4)
        n4 = lambda a: a[:].rearrange("p (i r) -> p i r", r=4)
        v16 = lambda a: a[:].rearrange("p (r q) -> p q r", r=16)
        n16 = lambda a: a[:].rearrange("p (q r) -> p q r", r=16)
        for g in range(4):
            qs = slice(128 * g, 128 * (g + 1))
            self.tt(n4(self.nsum)[:, qs, :], n4(self.accn[0])[:, qs, :], v4(self.accn[1])[:, qs, :], ALU.add,
                    [("accn", 0, g)] + nk(1), [("nsum", g)], eng="pool")
            self.tt(n4(self.dsum)[:, qs, :], n4(self.accd[0])[:, qs, :], v4(self.accd[1])[:, qs, :], ALU.add,
                    [("accd", 0, g)] + dk(1), [("dsum", g)], eng="pool")
        sg = lambda nm: [(nm, g) for g in range(4)]
        self.tt(n16(self.dsum), n16(self.dsum), v16(self.accd[2]), ALU.add, sg("dsum") + dk(2), sg("dsum"),
                eng="pool")
        self.tt(n16(self.nsum), n16(self.nsum), v16(self.accn[2]), ALU.add, sg("nsum") + nk(2), sg("nsum"),
                eng="pool")

    def att_finalize_b(self, h, part):
        s = h % 2
        qs = slice(512 * part, 512 * (part + 1))
        self.actv(self.rden[:, qs], self.dsum[:, qs], AF.Ln, [("dsum", part)], [("rden", part)])
        self.actv(self.rden[:, qs], self.rden[:, qs], AF.Exp, [("rden", part)], [("rden", part)], scale=-1.0)
        if part == 3:
            sg = lambda nm: [(nm, g) for g in range(4)]
            ao = self.atto[s]
            self.tt(ao[:], self.nsum[:], self.rden[:], ALU.mult, sg("nsum") + sg("rden"), [("atto", s)],
                    eng="pool")
            self.dma(self.attT_d[h], ao[:], [("atto", s)], [], ("ao", s))

    def phaseB2(self):
        P = self.P
        P.fence()
        self.dma(self.TB[:], self.tc_d[:, 0, :], [], ["T"], "c2")
        self.dma(self.TC[:], self.tc_d[:, 1, :], [], ["T"], "c2")
        self.cp(self.TA[:], self.TB[:], ["T"], ["TA"])
        self.ts(self.TD[:], self.TB[:], self.vcol("pm"), None, ALU.add, None, ["T", "vecs"], ["T"])
        self.ts(self.TA[:, 0:128], self.TB[:, 0:128], self.vcol("pm"), None, ALU.add, None,
                ["T", "TA", "vecs"], ["T"])
        self.att_loads(0)
        for h in range(H):
            s = h % 2
            if h + 1 < H:
                self.att_loads(h + 1)
            groups = self.att_groups(s)
            self.att_vtiles(h)
            self.att_scores(h, 0, groups[0])
            for gi in range(len(groups)):
                if gi + 1 < len(groups):
                    self.att_scores(h, gi + 1, groups[gi + 1])
                self.att_pv(h, gi, groups[gi])
                if 4 <= gi < 8 and h > 0:
                    self.att_finalize_b(h - 1, gi - 4)
            self.att_finalize(h)
        for part in range(4):
            self.att_finalize_b(H - 1, part)
        P.fence()

    def phaseB3(self):
        for i in range(NTILE):
            cols = slice(i * TT, (i + 1) * TT)
            sk = [("stage", 0), ("stage", 1)]
            self.dma(self.hT[:, :, :], self.h1T_d[:, :, cols].rearrange("c p t -> p c t"), [("dram", "h1", i)],
                     [("hT", c) for c in range(16)], "h1i")
            self.dma(self.st16[:, :, :], self.attT_d[:, :, cols].rearrange("h e t -> e h t"), [], sk, "ati")
            self.proj_resid(self.wo_d, lambda c: self.st16[:, c, :], lambda c: ("stage", c // 8), None)
            self.ffn(self.g1_d, self.u1_d, self.d1_d, "f_g1")
            self.rmsnorm(lambda c: self.hT[:, c, :], lambda c: ("hT", c),
                         lambda c: self.v[:, c, :], lambda c: ("v", c), "fin_g", TT, stats="done")
            for tb in range(4):
                xs = self.xs[tb % 2]
                for cg in range(4):
                    b = self.bank()
                    for cc in range(4):
                        c = cg * 4 + cc
                        self.tr(self.ps[b][:, cc * 128:(cc + 1) * 128], self.v[:, c, tb * 128:(tb + 1) * 128],
                                self.ident[:], [("v", c), "ident"], [("ps", b)])
                    self.cp(xs[:, cg * 512:(cg + 1) * 512], self.ps[b][:], [("ps", b)], [("stage", tb % 2)],
                            eng=("act" if cg % 2 else "dve"))
                r0 = i * TT + tb * 128
                self.dma(self.out_d[r0:r0 + 128, :], xs, [("stage", tb % 2)], [], ("oo", tb % 2))

    def build(self):
        self.setup()
        if "A" in self.phase:
            self.phaseA()
        if "B" in self.phase:
            if "A" not in self.phase:
                self.phaseB1()
            self.phaseB2()
            self.phaseB3()
        self.P.emit()
        return self.nc


_NC_CACHE = {}


def get_nc(phase):
    if phase not in _NC_CACHE:
        _NC_CACHE[phase] = Builder(phase).build()
    return _NC_CACHE[phase]


def kernel(**inp):
    inp = {k: np.asarray(v) for k, v in inp.items()}
    x = inp["x"]
    ident, tconst = make_consts()
    ncores = 8
    c32 = lambda a: np.ascontiguousarray(a, dtype=np.float32)
    w = {"conv_w1": c32(inp["conv_w1"][0]), "conv_w2": c32(inp["conv_w2"][0]), "w_k": c32(inp["w_k"]),
         "w_v": c32(inp["w_v"]), "gate0": c32(inp["ffn_w_gate"][0]), "up0": c32(inp["ffn_w_up"][0]),
         "down0": c32(inp["ffn_w_down"][0]),
         "w_q": c32(inp["w_q"][0]), "w_o": c32(inp["w_o"][0]), "gate1": c32(inp["ffn_w_gate"][1]),
         "up1": c32(inp["ffn_w_up"][1]), "down1": c32(inp["ffn_w_down"][1]), "tconst": tconst,
         "ident": ident}
    maps = []
    for core in range(ncores):
        b, half = core // 2, core % 2
        t0 = half * NTOK
        xo = c32(x[b, t0:t0 + NTOK])
        xh = c32(x[b, t0 - HALO:t0]) if half == 1 else np.zeros((HALO, D), np.float32)
        m = {"x": xo, "xh": xh, "vecs": make_vecs(inp, half)}
        m.update(w)
        maps.append(m)
    res = run_bass_kernel_spmd(get_nc("AB"), maps, core_ids=list(range(ncores)))
    out = np.empty((4, SEQ, D), np.float32)
    for core in range(ncores):
        b, half = core // 2, core % 2
        out[b, half * NTOK:(half + 1) * NTOK] = res.results[core]["out"]
    return out
```

```python
import contextlib
import numpy as np
import ml_dtypes
import concourse.bass as bass
import concourse.mybir as mybir
from concourse.bass_utils import run_bass_kernel_spmd

F32 = mybir.dt.float32
BF16 = mybir.dt.bfloat16
AF = mybir.ActivationFunctionType
ALU = mybir.AluOpType

D = 2048
FF = 5632
NF = FF // 128
SEQ = 4096
NTOK = 2048
TT = 512
NTILE = NTOK // TT
H = 16
CW = 31
HALO = 32
RMS_EPS = 1e-6
LN_EPS = 1e-5
BIG = 30000.0
QSCALE = 128.0 ** -0.5

ENGS = ("pe", "act", "dve", "pool", "sp")
import os
DEBUG_NOCC = bool(os.environ.get("KDEBUG_NOCC"))
NO_PRECONV = False


class Op:
    __slots__ = ("eng", "idx", "fn", "deps", "signal", "dma_key", "dma_cnt", "cum", "inc")

    def __init__(self, eng, idx, fn):
        self.eng = eng
        self.idx = idx
        self.fn = fn
        self.deps = []
        self.signal = False
        self.dma_key = None
        self.dma_cnt = 0
        self.cum = 0
        self.inc = 16


class Prog:
    def __init__(self, nc):
        self.nc = nc
        self.ops = {e: [] for e in ENGS}
        self.last_w = {}
        self.readers = {}
        self.dma_cnt = {}
        self.dma_inc = {}

    def op(self, eng, fn, reads=(), writes=(), dma=None, inc=16):
        o = Op(eng, len(self.ops[eng]), fn)
        if dma is not None:
            o.dma_key = dma
            o.inc = inc
            self.dma_inc[dma] = inc
            self.dma_cnt[dma] = self.dma_cnt.get(dma, 0) + 1
            o.dma_cnt = self.dma_cnt[dma]
        deps = []
        for k in reads:
            w = self.last_w.get(k)
            if w is not None:
                deps.append(w)
        for k in writes:
            w = self.last_w.get(k)
            if w is not None:
                deps.append(w)
            for r in self.readers.get(k, {}).values():
                deps.append(r)
        seen = set()
        for d in deps:
            if d is o or id(d) in seen:
                continue
            seen.add(id(d))
            if d.dma_key is None and d.eng == eng:
                if eng in ("pe", "sp"):
                    continue
                if o.dma_key is None and o.idx - d.idx > 3:
                    continue
            o.deps.append(d)
            if d.dma_key is None:
                d.signal = True
        for k in reads:
            self.readers.setdefault(k, {})[(eng, o.dma_key)] = o
        for k in writes:
            self.last_w[k] = o
            self.readers[k] = {}
        self.ops[eng].append(o)
        return o

    def fence(self):
        lasts = []
        for e in ENGS:
            for o in reversed(self.ops[e]):
                if o.dma_key is None and o.fn is not None:
                    lasts.append(o)
                    break
        dmas = []
        for k, c in self.dma_cnt.items():
            p = Op("sp", -1, None)
            p.dma_key = k
            p.dma_cnt = c
            p.inc = self.dma_inc[k]
            dmas.append(p)
        for e in ENGS:
            o = Op(e, len(self.ops[e]), None)
            for d in lasts:
                if d.eng != e:
                    o.deps.append(d)
                    d.signal = True
            o.deps.extend(dmas)
            self.ops[e].append(o)
        self.last_w = {}
        self.readers = {}

    def emit(self):
        nc = self.nc
        with contextlib.ExitStack() as st:
            esem = {e: st.enter_context(nc.semaphore("s_" + e)) for e in ENGS}
            dsem = {}
            for n, k in enumerate(self.dma_cnt):
                dsem[k] = st.enter_context(nc.semaphore("d%d" % n))
            block = st.enter_context(nc.Block())
            for e in ENGS:
                c = 0
                for o in self.ops[e]:
                    if o.signal:
                        c += 1
                    o.cum = c

            def run(e, engine):
                waited = {}
                for o in self.ops[e]:
                    for d in o.deps:
                        if d.dma_key is not None:
                            s, v, key = dsem[d.dma_key], d.inc * d.dma_cnt, ("d", d.dma_key)
                        else:
                            s, v, key = esem[d.eng], d.cum, ("e", d.eng)
                        if waited.get(key, 0) >= v:
                            continue
                        waited[key] = v
                        engine.wait_ge(s, v)
                    if o.fn is None:
                        continue
                    ins = o.fn(engine)
                    if o.dma_key is not None:
                        ins.then_inc(dsem[o.dma_key], o.inc)
                    elif o.signal:
                        ins.then_inc(esem[e], 1)
                if e == "sp":
                    for k, c in self.dma_cnt.items():
                        if waited.get(("d", k), 0) < self.dma_inc[k] * c:
                            engine.wait_ge(dsem[k], self.dma_inc[k] * c)

            @block.tensor
            def _(eng):
                run("pe", eng)

            @block.scalar
            def _(eng):
                run("act", eng)

            @block.vector
            def _(eng):
                run("dve", eng)

            @block.gpsimd
            def _(eng):
                run("pool", eng)

            @block.sync
            def _(eng):
                run("sp", eng)


VEC_COLS = {}
_c = 0
for _n, _w in (("a_g", 16), ("b1", 32), ("dwb", 16), ("lng", 16), ("lnb", 16), ("b2", 16),
               ("kv_g", 16), ("b_g", 16), ("f_g0", 16), ("f_g1", 16), ("fin_g", 16),
               ("dw", 16 * CW), ("hm", 1), ("pm", 1)):
    VEC_COLS[_n] = (_c, _w)
    _c += _w
NV = _c


def _pack(v):
    return np.ascontiguousarray(np.asarray(v, np.float32).reshape(-1, 128).T)


def make_vecs(inp, half):
    vecs = np.zeros((128, NV), np.float32)

    def put(name, arr):
        s, w = VEC_COLS[name]
        vecs[:, s:s + w] = arr

    put("a_g", _pack(inp["a_norm_g"][0]))
    put("b1", _pack(inp["conv_b1"][0]))
    put("dwb", _pack(inp["conv_dw_b"][0]))
    put("lng", _pack(inp["conv_ln_g"][0]))
    put("lnb", _pack(inp["conv_ln_b"][0]))
    put("b2", _pack(inp["conv_b2"][0]))
    put("kv_g", _pack(inp["kv_norm_g"]))
    put("b_g", _pack(inp["b_norm_g"][0]))
    put("f_g0", _pack(inp["ffn_norm_g"][0]))
    put("f_g1", _pack(inp["ffn_norm_g"][1]))
    put("fin_g", _pack(inp["final_norm_g"]))
    dw = np.asarray(inp["conv_dw"][0], np.float32)
    dwp = dw.T.reshape(16, 128, CW).transpose(1, 0, 2).reshape(128, 16 * CW)
    put("dw", dwp)
    put("hm", np.full((128, 1), 1.0 if half == 1 else 0.0, np.float32))
    put("pm", np.full((128, 1), 0.0 if half == 1 else BIG, np.float32))
    return vecs


def make_consts():
    k = np.arange(128)[:, None]
    q = np.arange(128)[None, :]
    t0 = np.where(k >= q, q - k + 128, BIG).astype(np.float32)
    t1 = np.where(k <= q, q - k, BIG).astype(np.float32)
    tc = np.zeros((128, 2, 512), np.float32)
    tc[:, 0, :] = np.tile(t0, (1, 4))
    tc[:, 1, :] = np.tile(t1, (1, 4))
    return np.eye(128, dtype=np.float32), tc


class Builder:
    def __init__(self, phase):
        self.phase = phase
        nc = self.nc = bass.Bass("TRN2", target_bir_lowering=False)
        self.P = Prog(nc)
        self.nbank = 0
        self.nw = 0
        din = lambda n, s, dt=F32: nc.dram_tensor(n, s, dt, kind="ExternalInput").ap()
        dout = lambda n, s, dt=F32: nc.dram_tensor(n, s, dt, kind="ExternalOutput").ap()
        dint = lambda n, s, dt=F32: nc.dram_tensor(n, s, dt, kind="Internal").ap()
        self.vecs_d = din("vecs", [128, NV])
        self.ident_d = din("ident", [128, 128])
        hasA = "A" in phase
        hasB = "B" in phase
        if hasA:
            self.x_d = din("x", [NTOK, D])
            self.xh_d = din("xh", [HALO, D])
            self.w1_d = din("conv_w1", [D, 2 * D])
            self.w2_d = din("conv_w2", [D, D])
            self.wk_d = din("w_k", [D, D])
            self.wv_d = din("w_v", [D, D])
            self.g0_d = din("gate0", [D, FF])
            self.u0_d = din("up0", [D, FF])
            self.d0_d = din("down0", [FF, D])
        if hasB:
            self.tc_d = din("tconst", [128, 2, 512])
            self.wq_d = din("w_q", [D, D])
            self.wo_d = din("w_o", [D, D])
            self.g1_d = din("gate1", [D, FF])
            self.u1_d = din("up1", [D, FF])
            self.d1_d = din("down1", [FF, D])
            self.out_d = dout("out", [NTOK, D])
            self.qT_d = dint("qT", [H, 128, NTOK], BF16)
            self.attT_d = dint("attT", [H, 128, NTOK], BF16)
        KSH = [NTILE, H * 128, TT]
        assert phase == "AB", "only the fused program is supported"
        self.h1T_d = dint("h1T", [16, 128, NTOK])
        self.wscr = dint("wscr", [96, 128, 16 * 512], BF16)
        self.wmap = {}
        self.preconv = False
        self.preq = []
        self.k_own = dint("k_own", KSH, BF16)
        self.v_own = dint("v_own", KSH, BF16)
        self.k_all = dint("k_all", [NTILE, 2 * H * 128, TT], BF16)
        self.v_all = dint("v_all", [NTILE, 2 * H * 128, TT], BF16)
        self.k_prev = self.k_all[:, 0:H * 128, :]
        self.v_prev = self.v_all[:, 0:H * 128, :]
        self.alloc()

    def alloc(self):
        nc = self.nc
        self.ps = [nc.alloc_psum_tensor("ps%d" % i, [128, 512], F32) for i in range(8)]
        off = [0]
        NB = 101 * 1024
        arena = self.arena = nc.alloc_sbuf_tensor("arena", [128, NB], BF16)

        def carve(nbytes, dt=BF16):
            n = nbytes // 2
            a = arena[:, off[0]:off[0] + n]
            off[0] += n
            assert off[0] <= NB, off[0]
            return a.bitcast(F32) if dt == F32 else a

        self.carve = carve
        self.vecs = nc.alloc_sbuf_tensor("vecs_sb", [128, NV], F32)
        self.ident = nc.alloc_sbuf_tensor("ident_sb", [128, 128], F32)
        self.identb = nc.alloc_sbuf_tensor("identb", [128, 128], BF16)
        self.onesb = nc.alloc_sbuf_tensor("onesb", [128, 128], BF16)
        self.onesf = nc.alloc_sbuf_tensor("onesf", [128, 128], F32)
        self.eps_rms = nc.alloc_sbuf_tensor("eps_rms", [128, 1], F32)
        self.eps_ln = nc.alloc_sbuf_tensor("eps_ln", [128, 1], F32)
        self.m0 = off[0]
        self.hT = carve(16 * 512 * 4, F32).rearrange("p (c t) -> p c t", c=16)
        self.xn = carve(16 * 512 * 2).rearrange("p (c t) -> p c t", c=16)
        u0 = off[0]
        self.u = carve(16 * 544 * 2).rearrange("p (c t) -> p c t", c=16)
        self.vflat = carve(16 * 512 * 4, F32)
        self.v = self.vflat.rearrange("p (c t) -> p c t", c=16)
        u1 = off[0]
        self.act = arena[:, u0:u0 + NF * 512].rearrange("p (c t) -> p c t", c=NF)
        assert u0 + NF * 512 <= u1
        s0 = off[0]
        stage = carve(16 * 512 * 2)
        self.stage = stage
        self.st16 = stage.rearrange("p (c t) -> p c t", c=16)
        self.vst = stage.rearrange("p (b f) -> p b f", b=4)
        self.xs = [arena[:, s0 + i * 4096:s0 + (i + 1) * 4096].bitcast(F32) for i in range(2)]
        self.wbuf = [carve(16 * 512 * 2).rearrange("p (k n) -> p k n", k=16) for _ in range(3)]
        self.dg = [carve(CW * 128 * 2).rearrange("p (j m) -> p j m", j=CW) for _ in range(2)]
        self.sq = [carve(512 * 2) for _ in range(4)]
        self.tmp = [carve(512 * 4, F32) for _ in range(3)]
        self.rstd = carve(512 * 4, F32)
        self.lnA = carve(512 * 4, F32)
        self.lnB = carve(512 * 4, F32)
        self.mean = carve(512 * 4, F32)
        self.hTh = carve(16 * HALO * 4, F32).rearrange("p (c t) -> p c t", c=16)
        self.xnh = carve(16 * HALO * 2).rearrange("p (c t) -> p c t", c=16)
        self.uhs = carve(16 * HALO * 2).rearrange("p (c t) -> p c t", c=16)
        self.m1 = off[0]
        if "B" in self.phase:
            off[0] = self.m0
            self.kTs = [carve(2 * NTOK * 2) for _ in range(2)]
            self.qTs = [carve(NTOK * 2) for _ in range(2)]
            self.V1 = [carve(17 * 128 * 2).rearrange("p (t e) -> p t e", e=128) for _ in range(2)]
            self.V4 = [carve(20 * 128 * 2).rearrange("p (r t e) -> p r t e", r=4, e=128) for _ in range(2)]
            self.V16 = [carve(32 * 128 * 2).rearrange("p (r t e) -> p r t e", r=16, e=128) for _ in range(2)]
            self.vTs = [carve(2 * NTOK * 2) for _ in range(2)]
            self.accn = [carve(NTOK * 4, F32) for _ in range(3)]
            self.accd = [carve(NTOK * 4, F32) for _ in range(3)]
            self.nsum = carve(NTOK * 4, F32)
            self.dsum = carve(NTOK * 4, F32)
            self.rden = carve(NTOK * 4, F32)
            self.atto = [carve(NTOK * 2) for _ in range(2)]
            self.sc = [[carve(512 * 4, F32) for _ in range(2)] for _ in range(2)]
            self.pT = [[carve(512 * 2) for _ in range(2)] for _ in range(2)]
            self.TA = carve(512 * 4, F32)
            self.TB = carve(512 * 4, F32)
            self.TD = carve(512 * 4, F32)
            self.TC = carve(512 * 4, F32)
            assert off[0] <= self.m1, (off[0], self.m1)
            off[0] = self.m1

    def vcol(self, name, c=0, w=1):
        s, _ = VEC_COLS[name]
        return self.vecs[:, s + c:s + c + w]

    def bank(self):
        b = self.nbank % 6
        self.nbank += 1
        return b

    def mm(self, out, lhsT, rhs, start, stop, reads, writes):
        self.P.op("pe", lambda e: e.matmul(out, lhsT=lhsT, rhs=rhs, start=start, stop=stop),
                  reads=reads, writes=writes)

    def mm_items(self, items, src, skey, n, dc_outer):
        if dc_outer:
            for dc in range(16):
                for p, b, cs in items:
                    self.mm(self.ps[p][:, 0:n], self.wbuf[b][:, dc, cs], src(dc), dc == 0, dc == 15,
                            [("w", b), skey(dc)], [("ps", p)])
        else:
            for p, b, cs in items:
                for dc in range(16):
                    self.mm(self.ps[p][:, 0:n], self.wbuf[b][:, dc, cs], src(dc), dc == 0, dc == 15,
                            [("w", b), skey(dc)], [("ps", p)])

    def tr(self, out, in_, ident, reads, writes):
        self.P.op("pe", lambda e: e.transpose(out=out, in_=in_, identity=ident), reads=reads, writes=writes)

    def actv(self, out, in_, func, reads, writes, bias=0.0, scale=1.0):
        self.P.op("act", lambda e: e.activation(out=out, in_=in_, func=func, bias=bias, scale=scale),
                  reads=reads, writes=writes)

    def stt(self, out, in0, scalar, in1, op0, op1, reads, writes, eng="dve"):
        self.P.op(eng, lambda e: e.scalar_tensor_tensor(out=out, in0=in0, scalar=scalar, in1=in1, op0=op0, op1=op1),
                  reads=reads, writes=writes)

    def tt(self, out, in0, in1, op, reads, writes, eng="dve"):
        self.P.op(eng, lambda e: e.tensor_tensor(out=out, in0=in0, in1=in1, op=op), reads=reads, writes=writes)

    def ts(self, out, in0, s1, s2, op0, op1, reads, writes, eng="dve"):
        if s2 is None:
            self.P.op(eng, lambda e: e.tensor_scalar(out=out, in0=in0, scalar1=s1, scalar2=None, op0=op0),
                      reads=reads, writes=writes)
        else:
            self.P.op(eng, lambda e: e.tensor_scalar(out=out, in0=in0, scalar1=s1, scalar2=s2, op0=op0, op1=op1),
                      reads=reads, writes=writes)

    def cp(self, out, in_, reads, writes, eng="dve"):
        if eng == "act":
            self.P.op("act", lambda e: e.copy(out=out, in_=in_), reads=reads, writes=writes)
        else:
            self.P.op(eng, lambda e: e.tensor_copy(out=out, in_=in_), reads=reads, writes=writes)

    def dma(self, out, in_, reads, writes, key, eng="sp"):
        return self.P.op(eng, lambda e: e.dma_start(out=out, in_=in_), reads=reads, writes=writes, dma=key)

    def wload(self, src, nk=16, wid=None):
        b = self.nw % 3
        self.nw += 1
        dst = self.wbuf[b][:, 0:nk, :]
        if wid not in self.wmap:
            idx = self.wmap[wid] = len(self.wmap)
            self.dma(dst, src.rearrange("(k p) n -> p k n", p=128), reads=[], writes=[("w", b)],
                     key=("w", b), eng="pool")
            self.dma(self.wscr[idx][:, 0:nk * 512].rearrange("p (k n) -> p k n", k=nk), dst,
                     reads=[("w", b)], writes=[("dram", "wscr", idx)], key="wst")
        else:
            idx = self.wmap[wid]
            self.dma(dst, self.wscr[idx][:, 0:nk * 512].rearrange("p (k n) -> p k n", k=nk),
                     reads=[("dram", "wscr", idx)], writes=[("w", b)], key=("w", b), eng="pool")
        return b

    def plan_preconvert(self):
        q = []
        for j in range(4):
            q.append((self.wo_d[:, 512 * j:512 * j + 512], 16, (self.wo_d.name, j)))
        for j in range(NF // 4):
            q.append((self.g1_d[:, 512 * j:512 * j + 512], 16, (self.g1_d.name, j)))
            q.append((self.u1_d[:, 512 * j:512 * j + 512], 16, (self.u1_d.name, j)))
        for qq in range(4):
            for g in range(3):
                nk = 16 if g < 2 else NF - 32
                q.append((self.d1_d[g * 2048:g * 2048 + nk * 128, 512 * qq:512 * qq + 512], nk,
                          (self.d1_d.name, g, qq)))
        self.preq = q
        q0 = []
        for j in range(4):
            q0.append((self.g0_d[:, 512 * j:512 * j + 512], 16, (self.g0_d.name, j)))
            q0.append((self.u0_d[:, 512 * j:512 * j + 512], 16, (self.u0_d.name, j)))
        self.preq0 = q0

    def preconvert_one(self, queue=None):
        queue = self.preq if queue is None else queue
        if not queue or NO_PRECONV:
            return
        src, nk, wid = queue.pop(0)
        idx = self.wmap[wid] = len(self.wmap)
        self.dma(self.wscr[idx][:, 0:nk * 512].rearrange("p (k n) -> p k n", k=nk),
                 src.rearrange("(k p) n -> p k n", p=128), reads=[], writes=[("dram", "wscr", idx)],
                 key="wpc", eng="pool")

    def setup(self):
        self.dma(self.vecs[:], self.vecs_d, [], ["vecs"], "c0")
        self.dma(self.ident[:], self.ident_d, [], ["ident"], "c1")
        self.cp(self.identb[:], self.ident[:], ["ident"], ["identb"])
        self.P.op("dve", lambda e: e.memset(self.onesb[:], 1.0), writes=["onesb"])
        self.P.op("dve", lambda e: e.memset(self.onesf[:], 1.0), writes=["onesf"])
        self.P.op("dve", lambda e: e.memset(self.eps_rms[:], RMS_EPS), writes=["eps"])
        self.P.op("dve", lambda e: e.memset(self.eps_ln[:], LN_EPS), writes=["eps"])

    def rms_stat(self, c, src, skey, n=TT):
        S = self.ps[6]
        sq = self.sq[c % 4]
        self.actv(sq[:, 0:n], src(c), AF.Square, [skey(c)], [("sq", c % 4)])
        self.mm(S[:, 0:n], self.onesb[:], sq[:, 0:n], c == 0, c == 15, [("sq", c % 4), "onesb"], [("ps", 6)])

    def rmsnorm(self, src, skey, dst, dkey, gname, n, stats="compute"):
        S = self.ps[6]
        r = self.rstd[:, 0:n]
        if stats == "compute":
            for c in range(16):
                self.rms_stat(c, src, skey, n)
        if stats != "reuse":
            self.actv(r, S[:, 0:n], AF.Sqrt, [("ps", 6)], ["rstd"], bias=self.eps_rms[:], scale=1.0 / D)
            self.P.op("dve", lambda e: e.reciprocal(out=r, in_=r), reads=["rstd"], writes=["rstd"])
        for c in range(16):
            self.stt(dst(c), src(c), self.vcol(gname, c), r, ALU.mult, ALU.mult,
                     [skey(c), "rstd", "vecs"], [dkey(c)])

    def issue_x_load(self, i):
        self.dma(self.vflat.rearrange("p (b d) -> p b d", b=4),
                 self.x_d[i * TT:(i + 1) * TT, :].rearrange("(b p) d -> p b d", p=128), [],
                 [("v", c) for c in range(16)] + [("act", f) for f in range(17, NF)], "xld")

    def load_x_tile(self, i):
        for tb in range(4):
            xs = self.vflat[:, tb * D:(tb + 1) * D]
            xk = [("v", 4 * tb + k) for k in range(4)]
            for cg in range(4):
                b = self.bank()
                for cc in range(4):
                    c = cg * 4 + cc
                    self.tr(self.ps[b][:, cc * 128:(cc + 1) * 128], xs[:, c * 128:(c + 1) * 128], self.ident[:],
                            xk + ["ident"], [("ps", b)])
                self.cp(self.hT[:, cg * 4:(cg + 1) * 4, tb * 128:(tb + 1) * 128],
                        self.ps[b][:].rearrange("p (c t) -> p c t", c=4), [("ps", b)],
                        [("hT", cg * 4 + k) for k in range(4)], eng=("act" if cg % 2 else "dve"))

    def load_halo(self):
        self.xsh = self.xs[1]
        self.dma(self.xsh[0:HALO, :], self.xh_d, [], [("stage", 1)], "xsh")
        for cg in range(4):
            b = self.bank()
            for cc in range(4):
                c = cg * 4 + cc
                self.tr(self.ps[b][:, cc * HALO:(cc + 1) * HALO], self.xsh[0:HALO, c * 128:(c + 1) * 128],
                        self.ident[0:HALO, 0:HALO], [("stage", 1), "ident"], [("ps", b)])
            self.cp(self.hTh[:, cg * 4:(cg + 1) * 4, :],
                    self.ps[b][:, 0:4 * HALO].rearrange("p (c t) -> p c t", c=4), [("ps", b)],
                    [("hTh", cg * 4 + k) for k in range(4)])

    def w1_glu(self, src, skey, n, dst, dkey, halo):
        for j in range(4):
            ba = self.wload(self.w1_d[:, 512 * j:512 * j + 512], wid=("w1a", j))
            bg = self.wload(self.w1_d[:, D + 512 * j:D + 512 * j + 512], wid=("w1g", j))
            for half in range(2):
                ccs = (2 * half, 2 * half + 1)
                bk = {cc: (self.bank(), self.bank()) for cc in ccs}
                items = []
                for cc in ccs:
                    cs = slice(cc * 128, (cc + 1) * 128)
                    items += [(bk[cc][0], ba, cs), (bk[cc][1], bg, cs)]
                self.mm_items(items, src, skey, n, dc_outer=(j == 0 and half == 0))
                for cc in ccs:
                    c = 4 * j + cc
                    pa, pg = bk[cc]
                    t = self.tmp[c % 3]
                    self.actv(t[:, 0:n], self.ps[pg][:, 0:n], AF.Sigmoid, [("ps", pg), "vecs"],
                              [("tmp", c % 3)], bias=self.vcol("b1", 16 + c))
                    self.stt(dst(c), self.ps[pa][:, 0:n], self.vcol("b1", c), t[:, 0:n], ALU.add, ALU.mult,
                             [("ps", pa), ("tmp", c % 3), "vecs"], [dkey(c)])
                    if halo:
                        self.ts(dst(c), dst(c), self.vcol("hm"), None, ALU.mult, None, [dkey(c), "vecs"],
                                [dkey(c)])

    def conv_ln(self, i):
        S1, S2 = self.ps[6], self.ps[7]
        if i > 0:
            self.exchange(i - 1)
            self.cp(self.u[:, :, 0:HALO], self.uhs[:, :, :], ["uhs"], [("uh", c) for c in range(16)], eng="dve")
        for c in range(16):
            if i == 0 and c % 2 == 1:
                self.preconvert_one(self.preq0)
            elif i >= 1 and c % 4 == 3:
                self.preconvert_one()
            dg = self.dg[c % 2]
            s, _ = VEC_COLS["dw"]
            dwc = self.vecs[:, s + c * CW:s + (c + 1) * CW]
            self.tt(dg[:], self.identb[:].unsqueeze(1).broadcast_to([128, CW, 128]),
                    dwc.unsqueeze(2).broadcast_to([128, CW, 128]), ALU.mult,
                    ["identb", "vecs"], [("dg", c % 2)])
            b = self.bank()
            for j in range(CW):
                self.mm(self.ps[b][:], dg[:, j, :], self.u[:, c, HALO - (CW - 1 - j):HALO - (CW - 1 - j) + TT], j == 0, j == CW - 1,
                        [("dg", c % 2), ("u", c), ("uh", c)], [("ps", b)])
            self.actv(self.v[:, c, :], self.ps[b][:], AF.Identity, [("ps", b), "vecs"], [("v", c)],
                      bias=self.vcol("dwb", c))
            sq = self.sq[c % 4]
            self.actv(sq[:], self.ps[b][:], AF.Square, [("ps", b), "vecs"], [("sq", c % 4)],
                      bias=self.vcol("dwb", c))
            self.mm(S1[:], self.onesf[:], self.v[:, c, :], c == 0, c == 15, [("v", c), "onesf"], [("ps", 6)])
            self.mm(S2[:], self.onesb[:], sq[:], c == 0, c == 15, [("sq", c % 4), "onesb"], [("ps", 7)])
        if i < NTILE - 1:
            self.cp(self.uhs[:, :, :], self.u[:, :, TT:TT + HALO], [("u", c) for c in range(16)],
                    ["uhs"], eng="dve")
        mean, A, B = self.mean, self.lnA, self.lnB
        self.ts(mean[:], S1[:], 1.0 / D, None, ALU.mult, None, [("ps", 6)], ["mean"])
        self.tt(B[:], mean[:], mean[:], ALU.mult, ["mean"], ["lnB"])
        self.stt(A[:], S2[:], 1.0 / D, B[:], ALU.mult, ALU.subtract, [("ps", 7), "lnB"], ["lnA"])
        self.actv(A[:], A[:], AF.Sqrt, ["lnA"], ["lnA"], bias=self.eps_ln[:], scale=1.0)
        self.P.op("dve", lambda e: e.reciprocal(out=A[:], in_=A[:]), reads=["lnA"], writes=["lnA"])
        self.stt(B[:], mean[:], -1.0, A[:], ALU.mult, ALU.mult, ["mean", "lnA"], ["lnB"])
        for c in range(16):
            self.tt(self.v[:, c, :], self.v[:, c, :], A[:], ALU.mult, [("v", c), "lnA"], [("v", c)])
            if c > 0:
                self.ln_tail(c - 1, B)
        self.ln_tail(15, B)

    def ln_tail(self, c, B):
        if True:
            self.tt(self.v[:, c, :], self.v[:, c, :], B[:], ALU.add, [("v", c), "lnB"], [("v", c)])
            self.actv(self.xn[:, c, :], self.v[:, c, :], AF.Silu, [("v", c), "vecs"], [("xn", c)],
                      bias=self.vcol("lnb", c), scale=self.vcol("lng", c))

    def proj_resid(self, w_d, src, skey, bias_name):
        for j in range(4):
            b = self.wload(w_d[:, 512 * j:512 * j + 512], wid=(w_d.name, j))
            pbk = [self.bank() for _ in range(4)] if j == 0 else None
            if j == 0:
                self.mm_items([(pbk[cc], b, slice(cc * 128, (cc + 1) * 128)) for cc in range(4)], src, skey, TT, True)
            for cc in range(4):
                c = 4 * j + cc
                if j == 0:
                    p = pbk[cc]
                else:
                    p = self.bank()
                    self.mm_items([(p, b, slice(cc * 128, (cc + 1) * 128))], src, skey, TT, False)
                if bias_name is not None:
                    self.stt(self.hT[:, c, :], self.ps[p][:], self.vcol(bias_name, c), self.hT[:, c, :],
                             ALU.add, ALU.add, [("ps", p), ("hT", c), "vecs"], [("hT", c)])
                else:
                    self.tt(self.hT[:, c, :], self.ps[p][:], self.hT[:, c, :], ALU.add,
                            [("ps", p), ("hT", c)], [("hT", c)])
            if j > 0:
                for c in range(4 * (j - 1), 4 * j):
                    self.rms_stat(c, lambda c: self.hT[:, c, :], lambda c: ("hT", c))
        for c in range(12, 16):
            self.rms_stat(c, lambda c: self.hT[:, c, :], lambda c: ("hT", c))

    def proj_out(self, w_d, dst, dkey, scale, eng):
        for j in range(4):
            b = self.wload(w_d[:, 512 * j:512 * j + 512], wid=(w_d.name, j))
            xsrc, xkey = (lambda dc: self.xn[:, dc, :]), (lambda dc: ("xn", dc))
            pbk = [self.bank() for _ in range(4)] if j == 0 else None
            if j == 0:
                self.mm_items([(pbk[cc], b, slice(cc * 128, (cc + 1) * 128)) for cc in range(4)], xsrc, xkey, TT, True)
            for cc in range(4):
                c = 4 * j + cc
                if j == 0:
                    p = pbk[cc]
                else:
                    p = self.bank()
                    self.mm_items([(p, b, slice(cc * 128, (cc + 1) * 128))], xsrc, xkey, TT, False)
                if scale is not None:
                    self.actv(dst(c), self.ps[p][:], AF.Identity, [("ps", p)], dkey(c), scale=scale)
                else:
                    self.cp(dst(c), self.ps[p][:], [("ps", p)], dkey(c), eng=eng)

    def ffn(self, g_d, u_d, d_d, gname):
        self.rmsnorm(lambda c: self.hT[:, c, :], lambda c: ("hT", c),
                     lambda c: self.xn[:, c, :], lambda c: ("xn", c), gname, TT, stats="done")
        for j in range(NF // 4):
            bg = self.wload(g_d[:, 512 * j:512 * j + 512], wid=(g_d.name, j))
            bu = self.wload(u_d[:, 512 * j:512 * j + 512], wid=(u_d.name, j))
            if self.preconv and j % 2 == 0:
                self.preconvert_one()
            xsrc, xkey = (lambda dc: self.xn[:, dc, :]), (lambda dc: ("xn", dc))
            for half in range(2):
                ccs = (2 * half, 2 * half + 1)
                bk = {cc: (self.bank(), self.bank()) for cc in ccs}
                items = []
                for cc in ccs:
                    cs = slice(cc * 128, (cc + 1) * 128)
                    items += [(bk[cc][0], bg, cs), (bk[cc][1], bu, cs)]
                self.mm_items(items, xsrc, xkey, TT, dc_outer=(j == 0 and half == 0))
                for cc in ccs:
                    f = 4 * j + cc
                    pg, pu = bk[cc]
                    t = self.tmp[f % 3]
                    self.actv(t[:], self.ps[pg][:], AF.Silu, [("ps", pg)], [("tmp", f % 3)])
                    self.tt(self.act[:, f, :], t[:], self.ps[pu][:], ALU.mult, [("tmp", f % 3), ("ps", pu)],
                            [("act", f)])
        for q in range(4):
            banks = [self.bank() for _ in range(4)]
            for g in range(3):
                nk = 16 if g < 2 else NF - 32
                b = self.wload(d_d[g * 2048:g * 2048 + nk * 128, 512 * q:512 * q + 512], nk, wid=(d_d.name, g, q))
                if self.preconv and g == 0:
                    self.preconvert_one()
                for cc in range(4):
                    for fk in range(nk):
                        f = g * 16 + fk
                        self.mm(self.ps[banks[cc]][:], self.wbuf[b][:, fk, cc * 128:(cc + 1) * 128],
                                self.act[:, f, :], f == 0, f == NF - 1, [("w", b), ("act", f)],
                                [("ps", banks[cc])])
            if q > 0:
                for c in range(4 * (q - 1), 4 * q):
                    self.rms_stat(c, lambda c: self.hT[:, c, :], lambda c: ("hT", c))
            for cc in range(4):
                c = 4 * q + cc
                self.tt(self.hT[:, c, :], self.ps[banks[cc]][:], self.hT[:, c, :], ALU.add,
                        [("ps", banks[cc]), ("hT", c)], [("hT", c)])
        for c in range(12, 16):
            self.rms_stat(c, lambda c: self.hT[:, c, :], lambda c: ("hT", c))

    def phaseA(self):
        self.plan_preconvert()
        for i in range(NTILE):
            cols = slice(i * TT, (i + 1) * TT)
            if i == 0:
                self.issue_x_load(0)
            self.load_x_tile(i)
            self.rmsnorm(lambda c: self.hT[:, c, :], lambda c: ("hT", c),
                         lambda c: self.xn[:, c, :], lambda c: ("xn", c), "a_g", TT)
            self.w1_glu(lambda c: self.xn[:, c, :], lambda c: ("xn", c), TT,
                        lambda c: self.u[:, c, HALO:HALO + TT], lambda c: ("u", c), False)
            if i == 0:
                self.load_halo()
                self.rmsnorm(lambda c: self.hTh[:, c, :], lambda c: ("hTh", c),
                             lambda c: self.xnh[:, c, :], lambda c: ("xnh", c), "a_g", HALO)
                self.w1_glu(lambda c: self.xnh[:, c, :], lambda c: ("xnh", c), HALO,
                            lambda c: self.u[:, c, 0:HALO], lambda c: ("uh", c), True)
            self.conv_ln(i)
            self.proj_resid(self.w2_d, lambda c: self.xn[:, c, :], lambda c: ("xn", c), "b2")
            self.preconv = (i >= 1 and not NO_PRECONV)
            self.ffn(self.g0_d, self.u0_d, self.d0_d, "f_g0")
            self.preconv = False
            if i + 1 < NTILE:
                self.issue_x_load(i + 1)
            self.dma(self.h1T_d[:, :, cols].rearrange("c p t -> p c t"), self.hT[:, :, :],
                     [("hT", c) for c in range(16)], [("dram", "h1", i)], "h1o")
            self.rmsnorm(lambda c: self.hT[:, c, :], lambda c: ("hT", c),
                         lambda c: self.xn[:, c, :], lambda c: ("xn", c), "kv_g", TT, stats="done")
            skh = lambda c: [("stage", c // 8)]
            for w_d, dst_d, nm, key, eng in ((self.wk_d, self.k_own[i], "k", "kto", "act"),
                                             (self.wv_d, self.v_own[i], "v", "vo", "dve")):
                self.proj_out(w_d, lambda c: self.st16[:, c, :], skh, None, eng)
                dv = dst_d.rearrange("(h e) t -> e h t", e=128)
                for hh in range(2):
                    self.dma(dv[:, 8 * hh:8 * hh + 8, :], self.st16[:, 8 * hh:8 * hh + 8, :], [("stage", hh)],
                             [("dram", nm, i)], key)
            if i == NTILE - 1:
                self.exchange(i, ("k",))
            if "B" in self.phase:
                self.rmsnorm(lambda c: self.hT[:, c, :], lambda c: ("hT", c),
                             lambda c: self.xn[:, c, :], lambda c: ("xn", c), "b_g", TT, stats="reuse")
                self.proj_out(self.wq_d, lambda c: self.st16[:, c, :], skh, QSCALE, "act")
                qv = self.qT_d[:, :, cols].rearrange("h e t -> e h t")
                for hh in range(2):
                    self.dma(qv[:, 8 * hh:8 * hh + 8, :], self.st16[:, 8 * hh:8 * hh + 8, :], [("stage", hh)], [],
                             "qo")
        self.exchange(NTILE - 1, ("v",))

    def exchange(self, i, which=("k", "v")):
        if self.phase != "AB" or DEBUG_NOCC:
            return
        rg = [[0, 1], [2, 3], [4, 5], [6, 7]]
        for nm, own, allb in (("k", self.k_own, self.k_all), ("v", self.v_own, self.v_all)):
            if nm not in which:
                continue
            self.P.op("pool", lambda e, own=own, allb=allb: e.collective_compute(
                "AllGather", ALU.bypass, replica_groups=rg,
                ins=[own[i].rearrange("(a b) t -> a (b t)", b=4)],
                outs=[allb[i].rearrange("(a b) t -> a (b t)", b=4)]),
                reads=[("dram", nm, i)], writes=[("dram", nm + "all")], dma="cc", inc=1)

    def phaseB1(self):
        for i in range(NTILE):
            cols = slice(i * TT, (i + 1) * TT)
            self.dma(self.hT[:, :, :], self.h1T_d[:, :, cols].rearrange("c p t -> p c t"), [("dram", "h1", i)],
                     [("hT", c) for c in range(16)], "h1i")
            self.rmsnorm(lambda c: self.hT[:, c, :], lambda c: ("hT", c),
                         lambda c: self.xn[:, c, :], lambda c: ("xn", c), "b_g", TT)
            sk = [("stage", 0), ("stage", 1)]
            self.proj_out(self.wq_d, lambda c: self.st16[:, c, :], lambda c: sk, QSCALE, "act")
            self.dma(self.qT_d[:, :, cols].rearrange("h e t -> e h t"), self.st16[:, :, :], sk, [], "qo")

    def att_loads(self, h):
        s = h % 2
        hs = slice(h * 128, (h + 1) * 128)
        for dst, prev, own, key, dk in ((self.kTs[s], self.k_prev, self.k_own, ("kTs", s), "kall"),
                                        (self.vTs[s], self.v_prev, self.v_own, ("vTs", s), "vall")):
            self.dma(dst[:, 0:NTOK].rearrange("p (i t) -> p i t", i=NTILE),
                     prev[:, hs, :].rearrange("i e t -> e i t"), [("dram", dk)], [key], ("kl", s))
            self.dma(dst[:, NTOK:2 * NTOK].rearrange("p (i t) -> p i t", i=NTILE),
                     own[:, hs, :].rearrange("i e t -> e i t"), [], [key], ("kl", s))
        self.dma(self.qTs[s][:], self.qT_d[h], [], [("qTs", s)], ("ql", s))

    def att_vtiles(self, h):
        s = h % 2
        vT = self.vTs[s]
        jobs = []
        cols = [slice(NTOK + 128 * kb, NTOK + 128 * (kb + 1)) for kb in range(-1, 16)]
        jobs.append((self.V1[s].rearrange("p t e -> p (t e)"), cols))
        cols = []
        for r in range(4):
            for kb in range(-1, 4):
                st = NTOK + r + 512 * kb
                cols.append(slice(st, st + 509, 4))
        jobs.append((self.V4[s].rearrange("p r t e -> p (r t e)"), cols))
        cols = []
        for r in range(16):
            for kb in range(-1, 1):
                st = NTOK * (kb + 1) + r
                cols.append(slice(st, st + 2033, 16))
        jobs.append((self.V16[s].rearrange("p r t e -> p (r t e)"), cols))
        n = 0
        for dest, cols in jobs:
            for c0 in range(0, len(cols), 8):
                grp = cols[c0:c0 + 8]
                b = n % 4
                n += 1
                pb = self.ps[b][:].bitcast(BF16)
                for k, cs in enumerate(grp):
                    self.tr(pb[:, 128 * k:128 * (k + 1)], vT[:, cs], self.identb[:],
                            [("vTs", s), "identb"], [("ps", b)])
                self.cp(dest[:, 128 * c0:128 * (c0 + len(grp))], pb[:, 0:128 * len(grp)], [("ps", b)],
                        [("V", s)], eng=("act" if n % 2 else "dve"))

    def att_groups(self, s):
        kT, qT = self.kTs[s], self.qTs[s]
        groups = []
        for g in range(4):
            blocks = []
            for bb in range(4):
                nb = 4 * g + bb
                blocks.append((kT[:, NTOK + 128 * (nb - 1):NTOK + 128 * nb],
                               kT[:, NTOK + 128 * nb:NTOK + 128 * (nb + 1)],
                               qT[:, 128 * nb:128 * (nb + 1)],
                               self.V1[s][:, nb, :], self.V1[s][:, nb + 1, :]))
            groups.append((self.TA if g == 0 else self.TB, blocks,
                           (lambda a, g=g: a[:, 512 * g:512 * (g + 1)]), 1))
        for r in range(4):
            blocks = []
            for nb in range(4):
                kb0 = NTOK + r + 512 * (nb - 1)
                kb1 = NTOK + r + 512 * nb
                blocks.append((kT[:, kb0:kb0 + 509:4], kT[:, kb1:kb1 + 509:4],
                               qT[:, r + 512 * nb:r + 512 * nb + 509:4],
                               self.V4[s][:, r, nb, :], self.V4[s][:, r, nb + 1, :]))
            groups.append((self.TA, blocks, (lambda a, r=r: a[:, 512 * r:512 * (r + 1)]), 4))
        for g in range(4):
            blocks = []
            for bb in range(4):
                r = 4 * g + bb
                blocks.append((kT[:, r:NTOK:16], kT[:, NTOK + r:2 * NTOK:16], qT[:, r:NTOK:16],
                               self.V16[s][:, r, 0, :], self.V16[s][:, r, 1, :]))
            groups.append((self.TD, blocks, (lambda a, g=g: a[:, 512 * g:512 * (g + 1)]), 16))
        return groups

    def att_scores(self, h, gi, grp):
        s = h % 2
        Tx, blocks, accf, d = grp
        par = gi % 2
        X, Y = self.ps[par * 2], self.ps[par * 2 + 1]
        for bb, (kp, kc, q, vp, vc) in enumerate(blocks):
            self.mm(X[:, 128 * bb:128 * (bb + 1)], kp, q, True, True, [("kTs", s), ("qTs", s)], [("ps", par * 2)])
        for bb, (kp, kc, q, vp, vc) in enumerate(blocks):
            self.mm(Y[:, 128 * bb:128 * (bb + 1)], kc, q, True, True, [("kTs", s), ("qTs", s)],
                    [("ps", par * 2 + 1)])
        slope = 2.0 ** (-8.0 * (h + 1) / H)
        for w, (T, Pb) in enumerate(((Tx, X), (self.TC, Y))):
            scb, ptb = self.sc[par][w], self.pT[par][w]
            self.stt(scb[:], T[:], -slope * d, Pb[:], ALU.mult, ALU.add, ["T", ("ps", par * 2 + w)],
                     [("sc", par, w)])
            self.actv(ptb[:], scb[:], AF.Exp, [("sc", par, w)], [("pT", par, w)])

    def att_pv(self, h, gi, grp):
        s = h % 2
        Tx, blocks, accf, d = grp
        par = gi % 2
        N_, Dn = self.ps[4 + par * 2], self.ps[5 + par * 2]
        pX, pY = self.pT[par][0], self.pT[par][1]
        for bb, (kp, kc, q, vp, vc) in enumerate(blocks):
            cs = slice(128 * bb, 128 * (bb + 1))
            self.mm(N_[:, cs], vp, pX[:, cs], True, False, [("V", s), ("pT", par, 0)], [("ps", 4 + par * 2)])
            self.mm(N_[:, cs], vc, pY[:, cs], False, True, [("V", s), ("pT", par, 1)], [("ps", 4 + par * 2)])
        self.mm(Dn[:], self.onesb[:], pX[:], True, False, ["onesb", ("pT", par, 0)], [("ps", 5 + par * 2)])
        self.mm(Dn[:], self.onesb[:], pY[:], False, True, ["onesb", ("pT", par, 1)], [("ps", 5 + par * 2)])
        br = gi // 4
        an, ad = accf(self.accn[br]), accf(self.accd[br])
        pn, pd = N_[:], Dn[:]
        self.cp(an, pn, [("ps", 4 + par * 2)], [("accn", br, gi % 4)], eng="act")
        self.cp(ad, pd, [("ps", 5 + par * 2)], [("accd", br, gi % 4)], eng=("act" if gi % 2 else "dve"))

    def att_finalize(self, h):
        s = h % 2
        nk = lambda br: [("accn", br, g) for g in range(4)]
        dk = lambda br: [("accd", br, g) for g in range(4)]
        v4 = lambda a: a[:].rearrange("p (r i) -> p i r", r=4)
        n4 = lambda a: a[:].rearrange("p (i r) -> p i r", r=4)
        v16 = lambda a: a[:].rearrange("p (r q) -> p q r", r=16)
        n16 = lambda a: a[:].rearrange("p (q r) -> p q r", r=16)
        for g in range(4):
            qs = slice(128 * g, 128 * (g + 1))
            self.tt(n4(self.nsum)[:, qs, :], n4(self.accn[0])[:, qs, :], v4(self.accn[1])[:, qs, :], ALU.add,
                    [("accn", 0, g)] + nk(1), [("nsum", g)], eng="pool")
            self.tt(n4(self.dsum)[:, qs, :], n4(self.accd[0])[:, qs, :], v4(self.accd[1])[:, qs, :], ALU.add,
                    [("accd", 0, g)] + dk(1), [("dsum", g)], eng="pool")
        sg = lambda nm: [(nm, g) for g in range(4)]
        self.tt(n16(self.dsum), n16(self.dsum), v16(self.accd[2]), ALU.add, sg("dsum") + dk(2), sg("dsum"),
                eng="pool")
        self.tt(n16(self.nsum), n16(self.nsum), v16(self.accn[2]), ALU.add, sg("nsum") + nk(2), sg("nsum"),
                eng="pool")

    def att_finalize_b(self, h, part):
        s = h % 2
        qs = slice(512 * part, 512 * (part + 1))
        self.actv(self.rden[:, qs], self.dsum[:, qs], AF.Ln, [("dsum", part)], [("rden", part)])
        self.actv(self.rden[:, qs], self.rden[:, qs], AF.Exp, [("rden", part)], [("rden", part)], scale=-1.0)
        if part == 3:
            sg = lambda nm: [(nm, g) for g in range(4)]
            ao = self.atto[s]
            self.tt(ao[:], self.nsum[:], self.rden[:], ALU.mult, sg("nsum") + sg("rden"), [("atto", s)],
                    eng="pool")
            self.dma(self.attT_d[h], ao[:], [("atto", s)], [], ("ao", s))

    def phaseB2(self):
        P = self.P
        P.fence()
        self.dma(self.TB[:], self.tc_d[:, 0, :], [], ["T"], "c2")
        self.dma(self.TC[:], self.tc_d[:, 1, :], [], ["T"], "c2")
        self.cp(self.TA[:], self.TB[:], ["T"], ["TA"])
        self.ts(self.TD[:], self.TB[:], self.vcol("pm"), None, ALU.add, None, ["T", "vecs"], ["T"])
        self.ts(self.TA[:, 0:128], self.TB[:, 0:128], self.vcol("pm"), None, ALU.add, None,
                ["T", "TA", "vecs"], ["T"])
        self.att_loads(0)
        for h in range(H):
            s = h % 2
            if h + 1 < H:
                self.att_loads(h + 1)
            groups = self.att_groups(s)
            self.att_vtiles(h)
            self.att_scores(h, 0, groups[0])
            for gi in range(len(groups)):
                if gi + 1 < len(groups):
                    self.att_scores(h, gi + 1, groups[gi + 1])
                self.att_pv(h, gi, groups[gi])
                if 4 <= gi < 8 and h > 0:
                    self.att_finalize_b(h - 1, gi - 4)
            self.att_finalize(h)
        for part in range(4):
            self.att_finalize_b(H - 1, part)
        P.fence()

    def load_att(self, i):
        cols = slice(i * TT, (i + 1) * TT)
        self.dma(self.act[:, 0:16, :], self.attT_d[:, :, cols].rearrange("h e t -> e h t"), [],
                 [("act", c) for c in range(16)], "ati")

    def phaseB3(self):
        self.load_att(0)
        for i in range(NTILE):
            cols = slice(i * TT, (i + 1) * TT)
            self.dma(self.hT[:, :, :], self.h1T_d[:, :, cols].rearrange("c p t -> p c t"), [("dram", "h1", i)],
                     [("hT", c) for c in range(16)], "h1i")
            self.proj_resid(self.wo_d, lambda c: self.act[:, c, :], lambda c: ("act", c), None)
            self.ffn(self.g1_d, self.u1_d, self.d1_d, "f_g1")
            if i + 1 < NTILE:
                self.load_att(i + 1)
            self.rmsnorm(lambda c: self.hT[:, c, :], lambda c: ("hT", c),
                         lambda c: self.v[:, c, :], lambda c: ("v", c), "fin_g", TT, stats="done")
            for tb in range(4):
                xs = self.xs[tb % 2]
                for cg in range(4):
                    b = self.bank()
                    for cc in range(4):
                        c = cg * 4 + cc
                        self.tr(self.ps[b][:, cc * 128:(cc + 1) * 128], self.v[:, c, tb * 128:(tb + 1) * 128],
                                self.ident[:], [("v", c), "ident"], [("ps", b)])
                    self.cp(xs[:, cg * 512:(cg + 1) * 512], self.ps[b][:], [("ps", b)], [("stage", tb % 2)],
                            eng=("act" if cg % 2 else "dve"))
                r0 = i * TT + tb * 128
                self.dma(self.out_d[r0:r0 + 128, :], xs, [("stage", tb % 2)], [], ("oo", tb % 2))

    def build(self):
        self.setup()
        if "A" in self.phase:
            self.phaseA()
        if "B" in self.phase:
            if "A" not in self.phase:
                self.phaseB1()
            self.phaseB2()
            self.phaseB3()
        self.P.emit()
        return self.nc


_NC_CACHE = {}


def get_nc(phase):
    if phase not in _NC_CACHE:
        _NC_CACHE[phase] = Builder(phase).build()
    return _NC_CACHE[phase]


def kernel(**inp):
    inp = {k: np.asarray(v) for k, v in inp.items()}
    x = inp["x"]
    ident, tconst = make_consts()
    ncores = 8
    c32 = lambda a: np.ascontiguousarray(a, dtype=np.float32)
    w = {"conv_w1": c32(inp["conv_w1"][0]), "conv_w2": c32(inp["conv_w2"][0]), "w_k": c32(inp["w_k"]),
         "w_v": c32(inp["w_v"]), "gate0": c32(inp["ffn_w_gate"][0]), "up0": c32(inp["ffn_w_up"][0]),
         "down0": c32(inp["ffn_w_down"][0]),
         "w_q": c32(inp["w_q"][0]), "w_o": c32(inp["w_o"][0]), "gate1": c32(inp["ffn_w_gate"][1]),
         "up1": c32(inp["ffn_w_up"][1]), "down1": c32(inp["ffn_w_down"][1]), "tconst": tconst,
         "ident": ident}
    maps = []
    for core in range(ncores):
        b, half = core // 2, core % 2
        t0 = half * NTOK
        xo = c32(x[b, t0:t0 + NTOK])
        xh = c32(x[b, t0 - HALO:t0]) if half == 1 else np.zeros((HALO, D), np.float32)
        m = {"x": xo, "xh": xh, "vecs": make_vecs(inp, half)}
        m.update(w)
        maps.append(m)
    res = run_bass_kernel_spmd(get_nc("AB"), maps, core_ids=list(range(ncores)))
    out = np.empty((4, SEQ, D), np.float32)
    for core in range(ncores):
        b, half = core // 2, core % 2
        out[b, half * NTOK:(half + 1) * NTOK] = res.results[core]["out"]
    return out
```

```python
import contextlib
import numpy as np
import ml_dtypes
import concourse.bass as bass
import concourse.mybir as mybir
from concourse.bass_utils import run_bass_kernel_spmd

F32 = mybir.dt.float32
BF16 = mybir.dt.bfloat16
AF = mybir.ActivationFunctionType
ALU = mybir.AluOpType

D = 2048
FF = 5632
NF = FF // 128
SEQ = 4096
NTOK = 2048
TT = 512
NTILE = NTOK // TT
H = 16
CW = 31
HALO = 32
RMS_EPS = 1e-6
LN_EPS = 1e-5
BIG = 30000.0
QSCALE = 128.0 ** -0.5

ENGS = ("pe", "act", "dve", "pool", "sp")
import os
DEBUG_NOCC = bool(os.environ.get("KDEBUG_NOCC"))
NO_PRECONV = False


class Op:
    __slots__ = ("eng", "idx", "fn", "deps", "signal", "dma_key", "dma_cnt", "cum", "inc")

    def __init__(self, eng, idx, fn):
        self.eng = eng
        self.idx = idx
        self.fn = fn
        self.deps = []
        self.signal = False
        self.dma_key = None
        self.dma_cnt = 0
        self.cum = 0
        self.inc = 16


class Prog:
    def __init__(self, nc):
        self.nc = nc
        self.ops = {e: [] for e in ENGS}
        self.last_w = {}
        self.readers = {}
        self.dma_cnt = {}
        self.dma_inc = {}

    def op(self, eng, fn, reads=(), writes=(), dma=None, inc=16):
        o = Op(eng, len(self.ops[eng]), fn)
        if dma is not None:
            o.dma_key = dma
            o.inc = inc
            self.dma_inc[dma] = inc
            self.dma_cnt[dma] = self.dma_cnt.get(dma, 0) + 1
            o.dma_cnt = self.dma_cnt[dma]
        deps = []
        for k in reads:
            w = self.last_w.get(k)
            if w is not None:
                deps.append(w)
        for k in writes:
            w = self.last_w.get(k)
            if w is not None:
                deps.append(w)
            for r in self.readers.get(k, {}).values():
                deps.append(r)
        seen = set()
        for d in deps:
            if d is o or id(d) in seen:
                continue
            seen.add(id(d))
            if d.dma_key is None and d.eng == eng:
                if eng in ("pe", "sp"):
                    continue
                if o.dma_key is None and o.idx - d.idx > 3:
                    continue
            o.deps.append(d)
            if d.dma_key is None:
                d.signal = True
        for k in reads:
            self.readers.setdefault(k, {})[(eng, o.dma_key)] = o
        for k in writes:
            self.last_w[k] = o
            self.readers[k] = {}
        self.ops[eng].append(o)
        return o

    def fence(self):
        lasts = []
        for e in ENGS:
            for o in reversed(self.ops[e]):
                if o.dma_key is None and o.fn is not None:
                    lasts.append(o)
                    break
        dmas = []
        for k, c in self.dma_cnt.items():
            p = Op("sp", -1, None)
            p.dma_key = k
            p.dma_cnt = c
            p.inc = self.dma_inc[k]
            dmas.append(p)
        for e in ENGS:
            o = Op(e, len(self.ops[e]), None)
            for d in lasts:
                if d.eng != e:
                    o.deps.append(d)
                    d.signal = True
            o.deps.extend(dmas)
            self.ops[e].append(o)
        self.last_w = {}
        self.readers = {}

    def emit(self):
        nc = self.nc
        with contextlib.ExitStack() as st:
            esem = {e: st.enter_context(nc.semaphore("s_" + e)) for e in ENGS}
            dsem = {}
            for n, k in enumerate(self.dma_cnt):
                dsem[k] = st.enter_context(nc.semaphore("d%d" % n))
            block = st.enter_context(nc.Block())
            for e in ENGS:
                c = 0
                for o in self.ops[e]:
                    if o.signal:
                        c += 1
                    o.cum = c

            def run(e, engine):
                waited = {}
                for o in self.ops[e]:
                    for d in o.deps:
                        if d.dma_key is not None:
                            s, v, key = dsem[d.dma_key], d.inc * d.dma_cnt, ("d", d.dma_key)
                        else:
                            s, v, key = esem[d.eng], d.cum, ("e", d.eng)
                        if waited.get(key, 0) >= v:
                            continue
                        waited[key] = v
                        engine.wait_ge(s, v)
                    if o.fn is None:
                        continue
                    ins = o.fn(engine)
                    if o.dma_key is not None:
                        ins.then_inc(dsem[o.dma_key], o.inc)
                    elif o.signal:
                        ins.then_inc(esem[e], 1)
                if e == "sp":
                    for k, c in self.dma_cnt.items():
                        if waited.get(("d", k), 0) < self.dma_inc[k] * c:
                            engine.wait_ge(dsem[k], self.dma_inc[k] * c)

            @block.tensor
            def _(eng):
                run("pe", eng)

            @block.scalar
            def _(eng):
                run("act", eng)

            @block.vector
            def _(eng):
                run("dve", eng)

            @block.gpsimd
            def _(eng):
                run("pool", eng)

            @block.sync
            def _(eng):
                run("sp", eng)


VEC_COLS = {}
_c = 0
for _n, _w in (("a_g", 16), ("b1", 32), ("dwb", 16), ("lng", 16), ("lnb", 16), ("b2", 16),
               ("kv_g", 16), ("b_g", 16), ("f_g0", 16), ("f_g1", 16), ("fin_g", 16),
               ("dw", 16 * CW), ("hm", 1), ("pm", 1)):
    VEC_COLS[_n] = (_c, _w)
    _c += _w
NV = _c


def _pack(v):
    return np.ascontiguousarray(np.asarray(v, np.float32).reshape(-1, 128).T)


def make_vecs(inp, half):
    vecs = np.zeros((128, NV), np.float32)

    def put(name, arr):
        s, w = VEC_COLS[name]
        vecs[:, s:s + w] = arr

    put("a_g", _pack(inp["a_norm_g"][0]))
    put("b1", _pack(inp["conv_b1"][0]))
    put("dwb", _pack(inp["conv_dw_b"][0]))
    put("lng", _pack(inp["conv_ln_g"][0]))
    put("lnb", _pack(inp["conv_ln_b"][0]))
    put("b2", _pack(inp["conv_b2"][0]))
    put("kv_g", _pack(inp["kv_norm_g"]))
    put("b_g", _pack(inp["b_norm_g"][0]))
    put("f_g0", _pack(inp["ffn_norm_g"][0]))
    put("f_g1", _pack(inp["ffn_norm_g"][1]))
    put("fin_g", _pack(inp["final_norm_g"]))
    dw = np.asarray(inp["conv_dw"][0], np.float32)
    dwp = dw.T.reshape(16, 128, CW).transpose(1, 0, 2).reshape(128, 16 * CW)
    put("dw", dwp)
    put("hm", np.full((128, 1), 1.0 if half == 1 else 0.0, np.float32))
    put("pm", np.full((128, 1), 0.0 if half == 1 else BIG, np.float32))
    return vecs


def make_consts():
    k = np.arange(128)[:, None]
    q = np.arange(128)[None, :]
    t0 = np.where(k >= q, q - k + 128, BIG).astype(np.float32)
    t1 = np.where(k <= q, q - k, BIG).astype(np.float32)
    tc = np.zeros((128, 2, 512), np.float32)
    tc[:, 0, :] = np.tile(t0, (1, 4))
    tc[:, 1, :] = np.tile(t1, (1, 4))
    return np.eye(128, dtype=np.float32), tc


class Builder:
    def __init__(self, phase):
        self.phase = phase
        nc = self.nc = bass.Bass("TRN2", target_bir_lowering=False)
        self.P = Prog(nc)
        self.nbank = 0
        self.nw = 0
        din = lambda n, s, dt=F32: nc.dram_tensor(n, s, dt, kind="ExternalInput").ap()
        dout = lambda n, s, dt=F32: nc.dram_tensor(n, s, dt, kind="ExternalOutput").ap()
        dint = lambda n, s, dt=F32: nc.dram_tensor(n, s, dt, kind="Internal").ap()
        self.vecs_d = din("vecs", [128, NV])
        self.ident_d = din("ident", [128, 128])
        hasA = "A" in phase
        hasB = "B" in phase
        if hasA:
            self.x_d = din("x", [NTOK, D])
            self.xh_d = din("xh", [HALO, D])
            self.w1_d = din("conv_w1", [D, 2 * D])
            self.w2_d = din("conv_w2", [D, D])
            self.wk_d = din("w_k", [D, D])
            self.wv_d = din("w_v", [D, D])
            self.g0_d = din("gate0", [D, FF])
            self.u0_d = din("up0", [D, FF])
            self.d0_d = din("down0", [FF, D])
        if hasB:
            self.tc_d = din("tconst", [128, 2, 512])
            self.wq_d = din("w_q", [D, D])
            self.wo_d = din("w_o", [D, D])
            self.g1_d = din("gate1", [D, FF])
            self.u1_d = din("up1", [D, FF])
            self.d1_d = din("down1", [FF, D])
            self.out_d = dout("out", [NTOK, D])
            self.qT_d = dint("qT", [H, 128, NTOK], BF16)
            self.attT_d = dint("attT", [H, 128, NTOK], BF16)
        KSH = [NTILE, H * 128, TT]
        assert phase == "AB", "only the fused program is supported"
        self.h1T_d = dint("h1T", [16, 128, NTOK])
        self.wscr = dint("wscr", [96, 128, 16 * 512], BF16)
        self.wmap = {}
        self.preconv = False
        self.preq = []
        self.k_own = dint("k_own", KSH, BF16)
        self.v_own = dint("v_own", KSH, BF16)
        self.k_all = dint("k_all", [NTILE, 2 * H * 128, TT], BF16)
        self.v_all = dint("v_all", [NTILE, 2 * H * 128, TT], BF16)
        self.k_prev = self.k_all[:, 0:H * 128, :]
        self.v_prev = self.v_all[:, 0:H * 128, :]
        self.alloc()

    def alloc(self):
        nc = self.nc
        self.ps = [nc.alloc_psum_tensor("ps%d" % i, [128, 512], F32) for i in range(8)]
        off = [0]
        NB = 101 * 1024
        arena = self.arena = nc.alloc_sbuf_tensor("arena", [128, NB], BF16)

        def carve(nbytes, dt=BF16):
            n = nbytes // 2
            a = arena[:, off[0]:off[0] + n]
            off[0] += n
            assert off[0] <= NB, off[0]
            return a.bitcast(F32) if dt == F32 else a

        self.carve = carve
        self.vecs = nc.alloc_sbuf_tensor("vecs_sb", [128, NV], F32)
        self.ident = nc.alloc_sbuf_tensor("ident_sb", [128, 128], F32)
        self.identb = nc.alloc_sbuf_tensor("identb", [128, 128], BF16)
        self.onesb = nc.alloc_sbuf_tensor("onesb", [128, 128], BF16)
        self.onesf = nc.alloc_sbuf_tensor("onesf", [128, 128], F32)
        self.eps_rms = nc.alloc_sbuf_tensor("eps_rms", [128, 1], F32)
        self.eps_ln = nc.alloc_sbuf_tensor("eps_ln", [128, 1], F32)
        self.m0 = off[0]
        self.hT = carve(16 * 512 * 4, F32).rearrange("p (c t) -> p c t", c=16)
        self.xn = carve(16 * 512 * 2).rearrange("p (c t) -> p c t", c=16)
        u0 = off[0]
        self.u = carve(16 * 544 * 2).rearrange("p (c t) -> p c t", c=16)
        self.vflat = carve(16 * 512 * 4, F32)
        self.v = self.vflat.rearrange("p (c t) -> p c t", c=16)
        u1 = off[0]
        self.act = arena[:, u0:u0 + NF * 512].rearrange("p (c t) -> p c t", c=NF)
        assert u0 + NF * 512 <= u1
        s0 = off[0]
        stage = carve(16 * 512 * 2)
        self.stage = stage
        self.st16 = stage.rearrange("p (c t) -> p c t", c=16)
        self.vst = stage.rearrange("p (b f) -> p b f", b=4)
        self.xs = [arena[:, s0 + i * 4096:s0 + (i + 1) * 4096].bitcast(F32) for i in range(2)]
        self.wbuf = [carve(16 * 512 * 2).rearrange("p (k n) -> p k n", k=16) for _ in range(3)]
        self.dg = [carve(CW * 128 * 2).rearrange("p (j m) -> p j m", j=CW) for _ in range(2)]
        self.sq = [carve(512 * 2) for _ in range(4)]
        self.tmp = [carve(512 * 4, F32) for _ in range(3)]
        self.rstd = carve(512 * 4, F32)
        self.lnA = carve(512 * 4, F32)
        self.lnB = carve(512 * 4, F32)
        self.mean = carve(512 * 4, F32)
        self.hTh = carve(16 * HALO * 4, F32).rearrange("p (c t) -> p c t", c=16)
        self.xnh = carve(16 * HALO * 2).rearrange("p (c t) -> p c t", c=16)
        self.uhs = carve(16 * HALO * 2).rearrange("p (c t) -> p c t", c=16)
        self.m1 = off[0]
        if "B" in self.phase:
            off[0] = self.m0
            self.kTs = [carve(2 * NTOK * 2) for _ in range(2)]
            self.qTs = [carve(NTOK * 2) for _ in range(2)]
            self.V1 = [carve(17 * 128 * 2).rearrange("p (t e) -> p t e", e=128) for _ in range(2)]
            self.V4 = [carve(20 * 128 * 2).rearrange("p (r t e) -> p r t e", r=4, e=128) for _ in range(2)]
            self.V16 = [carve(32 * 128 * 2).rearrange("p (r t e) -> p r t e", r=16, e=128) for _ in range(2)]
            self.vTs = [carve(2 * NTOK * 2) for _ in range(2)]
            self.accn = [carve(NTOK * 4, F32) for _ in range(3)]
            self.accd = [carve(NTOK * 4, F32) for _ in range(3)]
            self.nsum = carve(NTOK * 4, F32)
            self.dsum = carve(NTOK * 4, F32)
            self.rden = carve(NTOK * 4, F32)
            self.atto = [carve(NTOK * 2) for _ in range(2)]
            self.sc = [[carve(512 * 4, F32) for _ in range(2)] for _ in range(2)]
            self.pT = [[carve(512 * 2) for _ in range(2)] for _ in range(2)]
            self.TA = carve(512 * 4, F32)
            self.TB = carve(512 * 4, F32)
            self.TD = carve(512 * 4, F32)
            self.TC = carve(512 * 4, F32)
            assert off[0] <= self.m1, (off[0], self.m1)
            off[0] = self.m1

    def vcol(self, name, c=0, w=1):
        s, _ = VEC_COLS[name]
        return self.vecs[:, s + c:s + c + w]

    def bank(self):
        b = self.nbank % 6
        self.nbank += 1
        return b

    def mm(self, out, lhsT, rhs, start, stop, reads, writes):
        self.P.op("pe", lambda e: e.matmul(out, lhsT=lhsT, rhs=rhs, start=start, stop=stop),
                  reads=reads, writes=writes)

    def mm_items(self, items, src, skey, n, dc_outer):
        if dc_outer:
            for dc in range(16):
                for p, b, cs in items:
                    self.mm(self.ps[p][:, 0:n], self.wbuf[b][:, dc, cs], src(dc), dc == 0, dc == 15,
                            [("w", b), skey(dc)], [("ps", p)])
        else:
            for p, b, cs in items:
                for dc in range(16):
                    self.mm(self.ps[p][:, 0:n], self.wbuf[b][:, dc, cs], src(dc), dc == 0, dc == 15,
                            [("w", b), skey(dc)], [("ps", p)])

    def tr(self, out, in_, ident, reads, writes):
        self.P.op("pe", lambda e: e.transpose(out=out, in_=in_, identity=ident), reads=reads, writes=writes)

    def actv(self, out, in_, func, reads, writes, bias=0.0, scale=1.0):
        self.P.op("act", lambda e: e.activation(out=out, in_=in_, func=func, bias=bias, scale=scale),
                  reads=reads, writes=writes)

    def stt(self, out, in0, scalar, in1, op0, op1, reads, writes, eng="dve"):
        self.P.op(eng, lambda e: e.scalar_tensor_tensor(out=out, in0=in0, scalar=scalar, in1=in1, op0=op0, op1=op1),
                  reads=reads, writes=writes)

    def tt(self, out, in0, in1, op, reads, writes, eng="dve"):
        self.P.op(eng, lambda e: e.tensor_tensor(out=out, in0=in0, in1=in1, op=op), reads=reads, writes=writes)

    def ts(self, out, in0, s1, s2, op0, op1, reads, writes, eng="dve"):
        if s2 is None:
            self.P.op(eng, lambda e: e.tensor_scalar(out=out, in0=in0, scalar1=s1, scalar2=None, op0=op0),
                      reads=reads, writes=writes)
        else:
            self.P.op(eng, lambda e: e.tensor_scalar(out=out, in0=in0, scalar1=s1, scalar2=s2, op0=op0, op1=op1),
                      reads=reads, writes=writes)

    def cp(self, out, in_, reads, writes, eng="dve"):
        if eng == "act":
            self.P.op("act", lambda e: e.copy(out=out, in_=in_), reads=reads, writes=writes)
        else:
            self.P.op(eng, lambda e: e.tensor_copy(out=out, in_=in_), reads=reads, writes=writes)

    def dma(self, out, in_, reads, writes, key, eng="sp"):
        return self.P.op(eng, lambda e: e.dma_start(out=out, in_=in_), reads=reads, writes=writes, dma=key)

    def wload(self, src, nk=16, wid=None):
        b = self.nw % 3
        self.nw += 1
        dst = self.wbuf[b][:, 0:nk, :]
        if wid not in self.wmap:
            idx = self.wmap[wid] = len(self.wmap)
            self.dma(dst, src.rearrange("(k p) n -> p k n", p=128), reads=[], writes=[("w", b)],
                     key=("w", b), eng="pool")
            self.dma(self.wscr[idx][:, 0:nk * 512].rearrange("p (k n) -> p k n", k=nk), dst,
                     reads=[("w", b)], writes=[("dram", "wscr", idx)], key=("wst", b))
        else:
            idx = self.wmap[wid]
            self.dma(dst, self.wscr[idx][:, 0:nk * 512].rearrange("p (k n) -> p k n", k=nk),
                     reads=[("dram", "wscr", idx)], writes=[("w", b)], key=("w", b), eng="pool")
        return b

    def plan_preconvert(self):
        q = []
        for j in range(4):
            q.append((self.wo_d[:, 512 * j:512 * j + 512], 16, (self.wo_d.name, j)))
        for j in range(NF // 4):
            q.append((self.g1_d[:, 512 * j:512 * j + 512], 16, (self.g1_d.name, j)))
            q.append((self.u1_d[:, 512 * j:512 * j + 512], 16, (self.u1_d.name, j)))
        for qq in range(4):
            for g in range(3):
                nk = 16 if g < 2 else NF - 32
                q.append((self.d1_d[g * 2048:g * 2048 + nk * 128, 512 * qq:512 * qq + 512], nk,
                          (self.d1_d.name, g, qq)))
        self.preq = q
        q0 = []
        for j in range(4):
            q0.append((self.g0_d[:, 512 * j:512 * j + 512], 16, (self.g0_d.name, j)))
            q0.append((self.u0_d[:, 512 * j:512 * j + 512], 16, (self.u0_d.name, j)))
        self.preq0 = q0

    def preconvert_one(self, queue=None):
        queue = self.preq if queue is None else queue
        if not queue or NO_PRECONV:
            return
        key = ("wpc0", len(queue)) if queue is self.preq0 else "wpc"
        src, nk, wid = queue.pop(0)
        idx = self.wmap[wid] = len(self.wmap)
        self.dma(self.wscr[idx][:, 0:nk * 512].rearrange("p (k n) -> p k n", k=nk),
                 src.rearrange("(k p) n -> p k n", p=128), reads=[], writes=[("dram", "wscr", idx)],
                 key=key, eng="pool")

    def setup(self):
        self.dma(self.vecs[:], self.vecs_d, [], ["vecs"], "c0")
        self.dma(self.ident[:], self.ident_d, [], ["ident"], "c1")
        self.cp(self.identb[:], self.ident[:], ["ident"], ["identb"])
        self.P.op("dve", lambda e: e.memset(self.onesb[:], 1.0), writes=["onesb"])
        self.P.op("dve", lambda e: e.memset(self.onesf[:], 1.0), writes=["onesf"])
        self.P.op("dve", lambda e: e.memset(self.eps_rms[:], RMS_EPS), writes=["eps"])
        self.P.op("dve", lambda e: e.memset(self.eps_ln[:], LN_EPS), writes=["eps"])

    def rms_stat(self, c, src, skey, n=TT):
        S = self.ps[6]
        sq = self.sq[c % 4]
        self.actv(sq[:, 0:n], src(c), AF.Square, [skey(c)], [("sq", c % 4)])
        self.mm(S[:, 0:n], self.onesb[:], sq[:, 0:n], c == 0, c == 15, [("sq", c % 4), "onesb"], [("ps", 6)])

    def rmsnorm(self, src, skey, dst, dkey, gname, n, stats="compute"):
        S = self.ps[6]
        r = self.rstd[:, 0:n]
        if stats == "compute":
            for c in range(16):
                self.rms_stat(c, src, skey, n)
        if stats != "reuse":
            self.actv(r, S[:, 0:n], AF.Sqrt, [("ps", 6)], ["rstd"], bias=self.eps_rms[:], scale=1.0 / D)
            self.P.op("dve", lambda e: e.reciprocal(out=r, in_=r), reads=["rstd"], writes=["rstd"])
        for c in range(16):
            self.stt(dst(c), src(c), self.vcol(gname, c), r, ALU.mult, ALU.mult,
                     [skey(c), "rstd", "vecs"], [dkey(c)])

    def issue_x_load(self, i):
        self.dma(self.vflat.rearrange("p (b d) -> p b d", b=4),
                 self.x_d[i * TT:(i + 1) * TT, :].rearrange("(b p) d -> p b d", p=128), [],
                 [("v", c) for c in range(16)] + [("act", f) for f in range(17, NF)], "xld")

    def load_x_tile(self, i):
        for tb in range(4):
            xs = self.vflat[:, tb * D:(tb + 1) * D]
            xk = [("v", 4 * tb + k) for k in range(4)]
            for cg in range(4):
                b = self.bank()
                for cc in range(4):
                    c = cg * 4 + cc
                    self.tr(self.ps[b][:, cc * 128:(cc + 1) * 128], xs[:, c * 128:(c + 1) * 128], self.ident[:],
                            xk + ["ident"], [("ps", b)])
                self.cp(self.hT[:, cg * 4:(cg + 1) * 4, tb * 128:(tb + 1) * 128],
                        self.ps[b][:].rearrange("p (c t) -> p c t", c=4), [("ps", b)],
                        [("hT", cg * 4 + k) for k in range(4)], eng=("act" if cg % 2 else "dve"))

    def load_halo(self):
        self.xsh = self.xs[1]
        self.dma(self.xsh[0:HALO, :], self.xh_d, [], [("stage", 1)], "xsh")
        for cg in range(4):
            b = self.bank()
            for cc in range(4):
                c = cg * 4 + cc
                self.tr(self.ps[b][:, cc * HALO:(cc + 1) * HALO], self.xsh[0:HALO, c * 128:(c + 1) * 128],
                        self.ident[0:HALO, 0:HALO], [("stage", 1), "ident"], [("ps", b)])
            self.cp(self.hTh[:, cg * 4:(cg + 1) * 4, :],
                    self.ps[b][:, 0:4 * HALO].rearrange("p (c t) -> p c t", c=4), [("ps", b)],
                    [("hTh", cg * 4 + k) for k in range(4)])

    def w1_glu(self, src, skey, n, dst, dkey, halo):
        for j in range(4):
            ba = self.wload(self.w1_d[:, 512 * j:512 * j + 512], wid=("w1a", j))
            bg = self.wload(self.w1_d[:, D + 512 * j:D + 512 * j + 512], wid=("w1g", j))
            for half in range(2):
                ccs = (2 * half, 2 * half + 1)
                bk = {cc: (self.bank(), self.bank()) for cc in ccs}
                items = []
                for cc in ccs:
                    cs = slice(cc * 128, (cc + 1) * 128)
                    items += [(bk[cc][0], ba, cs), (bk[cc][1], bg, cs)]
                self.mm_items(items, src, skey, n, dc_outer=(j == 0 and half == 0))
                for cc in ccs:
                    c = 4 * j + cc
                    pa, pg = bk[cc]
                    t = self.tmp[c % 3]
                    self.actv(t[:, 0:n], self.ps[pg][:, 0:n], AF.Sigmoid, [("ps", pg), "vecs"],
                              [("tmp", c % 3)], bias=self.vcol("b1", 16 + c))
                    self.stt(dst(c), self.ps[pa][:, 0:n], self.vcol("b1", c), t[:, 0:n], ALU.add, ALU.mult,
                             [("ps", pa), ("tmp", c % 3), "vecs"], [dkey(c)])
                    if halo:
                        self.ts(dst(c), dst(c), self.vcol("hm"), None, ALU.mult, None, [dkey(c), "vecs"],
                                [dkey(c)])

    def conv_ln(self, i):
        S1, S2 = self.ps[6], self.ps[7]
        if i > 0:
            self.exchange(i - 1)
            self.cp(self.u[:, :, 0:HALO], self.uhs[:, :, :], ["uhs"], [("uh", c) for c in range(16)], eng="dve")
        for c in range(16):
            if i == 0 and c % 2 == 1:
                self.preconvert_one(self.preq0)
            elif i >= 1 and c % 4 == 3:
                self.preconvert_one()
            dg = self.dg[c % 2]
            s, _ = VEC_COLS["dw"]
            dwc = self.vecs[:, s + c * CW:s + (c + 1) * CW]
            self.tt(dg[:], self.identb[:].unsqueeze(1).broadcast_to([128, CW, 128]),
                    dwc.unsqueeze(2).broadcast_to([128, CW, 128]), ALU.mult,
                    ["identb", "vecs"], [("dg", c % 2)])
            b = self.bank()
            for j in range(CW):
                self.mm(self.ps[b][:], dg[:, j, :], self.u[:, c, HALO - (CW - 1 - j):HALO - (CW - 1 - j) + TT], j == 0, j == CW - 1,
                        [("dg", c % 2), ("u", c), ("uh", c)], [("ps", b)])
            self.actv(self.v[:, c, :], self.ps[b][:], AF.Identity, [("ps", b), "vecs"], [("v", c)],
                      bias=self.vcol("dwb", c))
            sq = self.sq[c % 4]
            self.actv(sq[:], self.ps[b][:], AF.Square, [("ps", b), "vecs"], [("sq", c % 4)],
                      bias=self.vcol("dwb", c))
            self.mm(S1[:], self.onesf[:], self.v[:, c, :], c == 0, c == 15, [("v", c), "onesf"], [("ps", 6)])
            self.mm(S2[:], self.onesb[:], sq[:], c == 0, c == 15, [("sq", c % 4), "onesb"], [("ps", 7)])
        if i < NTILE - 1:
            self.cp(self.uhs[:, :, :], self.u[:, :, TT:TT + HALO], [("u", c) for c in range(16)],
                    ["uhs"], eng="dve")
        mean, A, B = self.mean, self.lnA, self.lnB
        self.ts(mean[:], S1[:], 1.0 / D, None, ALU.mult, None, [("ps", 6)], ["mean"])
        self.tt(B[:], mean[:], mean[:], ALU.mult, ["mean"], ["lnB"])
        self.stt(A[:], S2[:], 1.0 / D, B[:], ALU.mult, ALU.subtract, [("ps", 7), "lnB"], ["lnA"])
        self.actv(A[:], A[:], AF.Sqrt, ["lnA"], ["lnA"], bias=self.eps_ln[:], scale=1.0)
        self.P.op("dve", lambda e: e.reciprocal(out=A[:], in_=A[:]), reads=["lnA"], writes=["lnA"])
        self.stt(B[:], mean[:], -1.0, A[:], ALU.mult, ALU.mult, ["mean", "lnA"], ["lnB"])
        for c in range(16):
            self.tt(self.v[:, c, :], self.v[:, c, :], A[:], ALU.mult, [("v", c), "lnA"], [("v", c)])
            if c > 0:
                self.ln_tail(c - 1, B)
        self.ln_tail(15, B)

    def ln_tail(self, c, B):
        if True:
            self.tt(self.v[:, c, :], self.v[:, c, :], B[:], ALU.add, [("v", c), "lnB"], [("v", c)])
            self.actv(self.xn[:, c, :], self.v[:, c, :], AF.Silu, [("v", c), "vecs"], [("xn", c)],
                      bias=self.vcol("lnb", c), scale=self.vcol("lng", c))

    def proj_resid(self, w_d, src, skey, bias_name):
        for j in range(4):
            b = self.wload(w_d[:, 512 * j:512 * j + 512], wid=(w_d.name, j))
            pbk = [self.bank() for _ in range(4)] if j == 0 else None
            if j == 0:
                self.mm_items([(pbk[cc], b, slice(cc * 128, (cc + 1) * 128)) for cc in range(4)], src, skey, TT, True)
            for cc in range(4):
                c = 4 * j + cc
                if j == 0:
                    p = pbk[cc]
                else:
                    p = self.bank()
                    self.mm_items([(p, b, slice(cc * 128, (cc + 1) * 128))], src, skey, TT, False)
                if bias_name is not None:
                    self.stt(self.hT[:, c, :], self.ps[p][:], self.vcol(bias_name, c), self.hT[:, c, :],
                             ALU.add, ALU.add, [("ps", p), ("hT", c), "vecs"], [("hT", c)])
                else:
                    self.tt(self.hT[:, c, :], self.ps[p][:], self.hT[:, c, :], ALU.add,
                            [("ps", p), ("hT", c)], [("hT", c)])
            if j > 0:
                for c in range(4 * (j - 1), 4 * j):
                    self.rms_stat(c, lambda c: self.hT[:, c, :], lambda c: ("hT", c))
        for c in range(12, 16):
            self.rms_stat(c, lambda c: self.hT[:, c, :], lambda c: ("hT", c))

    def proj_out(self, w_d, dst, dkey, scale, eng):
        for j in range(4):
            b = self.wload(w_d[:, 512 * j:512 * j + 512], wid=(w_d.name, j))
            xsrc, xkey = (lambda dc: self.xn[:, dc, :]), (lambda dc: ("xn", dc))
            pbk = [self.bank() for _ in range(4)] if j == 0 else None
            if j == 0:
                self.mm_items([(pbk[cc], b, slice(cc * 128, (cc + 1) * 128)) for cc in range(4)], xsrc, xkey, TT, True)
            for cc in range(4):
                c = 4 * j + cc
                if j == 0:
                    p = pbk[cc]
                else:
                    p = self.bank()
                    self.mm_items([(p, b, slice(cc * 128, (cc + 1) * 128))], xsrc, xkey, TT, False)
                if scale is not None:
                    self.actv(dst(c), self.ps[p][:], AF.Identity, [("ps", p)], dkey(c), scale=scale)
                else:
                    self.cp(dst(c), self.ps[p][:], [("ps", p)], dkey(c), eng=eng)

    def ffn(self, g_d, u_d, d_d, gname):
        self.rmsnorm(lambda c: self.hT[:, c, :], lambda c: ("hT", c),
                     lambda c: self.xn[:, c, :], lambda c: ("xn", c), gname, TT, stats="done")
        for j in range(NF // 4):
            bg = self.wload(g_d[:, 512 * j:512 * j + 512], wid=(g_d.name, j))
            bu = self.wload(u_d[:, 512 * j:512 * j + 512], wid=(u_d.name, j))
            if self.preconv and j % 2 == 0:
                self.preconvert_one()
            xsrc, xkey = (lambda dc: self.xn[:, dc, :]), (lambda dc: ("xn", dc))
            for half in range(2):
                ccs = (2 * half, 2 * half + 1)
                bk = {cc: (self.bank(), self.bank()) for cc in ccs}
                items = []
                for cc in ccs:
                    cs = slice(cc * 128, (cc + 1) * 128)
                    items += [(bk[cc][0], bg, cs), (bk[cc][1], bu, cs)]
                self.mm_items(items, xsrc, xkey, TT, dc_outer=(j == 0 and half == 0))
                for cc in ccs:
                    f = 4 * j + cc
                    pg, pu = bk[cc]
                    t = self.tmp[f % 3]
                    self.actv(t[:], self.ps[pg][:], AF.Silu, [("ps", pg)], [("tmp", f % 3)])
                    self.tt(self.act[:, f, :], t[:], self.ps[pu][:], ALU.mult, [("tmp", f % 3), ("ps", pu)],
                            [("act", f)])
        for q in range(4):
            banks = [self.bank() for _ in range(4)]
            for g in range(3):
                nk = 16 if g < 2 else NF - 32
                b = self.wload(d_d[g * 2048:g * 2048 + nk * 128, 512 * q:512 * q + 512], nk, wid=(d_d.name, g, q))
                if self.preconv and g == 0:
                    self.preconvert_one()
                for cc in range(4):
                    for fk in range(nk):
                        f = g * 16 + fk
                        self.mm(self.ps[banks[cc]][:], self.wbuf[b][:, fk, cc * 128:(cc + 1) * 128],
                                self.act[:, f, :], f == 0, f == NF - 1, [("w", b), ("act", f)],
                                [("ps", banks[cc])])
            if q > 0:
                for c in range(4 * (q - 1), 4 * q):
                    self.rms_stat(c, lambda c: self.hT[:, c, :], lambda c: ("hT", c))
            for cc in range(4):
                c = 4 * q + cc
                self.tt(self.hT[:, c, :], self.ps[banks[cc]][:], self.hT[:, c, :], ALU.add,
                        [("ps", banks[cc]), ("hT", c)], [("hT", c)])
        for c in range(12, 16):
            self.rms_stat(c, lambda c: self.hT[:, c, :], lambda c: ("hT", c))

    def phaseA(self):
        self.plan_preconvert()
        for i in range(NTILE):
            cols = slice(i * TT, (i + 1) * TT)
            if i == 0:
                self.issue_x_load(0)
            self.load_x_tile(i)
            self.rmsnorm(lambda c: self.hT[:, c, :], lambda c: ("hT", c),
                         lambda c: self.xn[:, c, :], lambda c: ("xn", c), "a_g", TT)
            self.w1_glu(lambda c: self.xn[:, c, :], lambda c: ("xn", c), TT,
                        lambda c: self.u[:, c, HALO:HALO + TT], lambda c: ("u", c), False)
            if i == 0:
                self.load_halo()
                self.rmsnorm(lambda c: self.hTh[:, c, :], lambda c: ("hTh", c),
                             lambda c: self.xnh[:, c, :], lambda c: ("xnh", c), "a_g", HALO)
                self.w1_glu(lambda c: self.xnh[:, c, :], lambda c: ("xnh", c), HALO,
                            lambda c: self.u[:, c, 0:HALO], lambda c: ("uh", c), True)
            self.conv_ln(i)
            self.proj_resid(self.w2_d, lambda c: self.xn[:, c, :], lambda c: ("xn", c), "b2")
            self.preconv = (i >= 1 and not NO_PRECONV)
            self.ffn(self.g0_d, self.u0_d, self.d0_d, "f_g0")
            self.preconv = False
            if i + 1 < NTILE:
                self.issue_x_load(i + 1)
            self.dma(self.h1T_d[:, :, cols].rearrange("c p t -> p c t"), self.hT[:, :, :],
                     [("hT", c) for c in range(16)], [("dram", "h1", i)], "h1o")
            self.rmsnorm(lambda c: self.hT[:, c, :], lambda c: ("hT", c),
                         lambda c: self.xn[:, c, :], lambda c: ("xn", c), "kv_g", TT, stats="done")
            skh = lambda c: [("stage", c // 8)]
            for w_d, dst_d, nm, key, eng in ((self.wk_d, self.k_own[i], "k", "kto", "act"),
                                             (self.wv_d, self.v_own[i], "v", "vo", "dve")):
                self.proj_out(w_d, lambda c: self.st16[:, c, :], skh, None, eng)
                dv = dst_d.rearrange("(h e) t -> e h t", e=128)
                for hh in range(2):
                    self.dma(dv[:, 8 * hh:8 * hh + 8, :], self.st16[:, 8 * hh:8 * hh + 8, :], [("stage", hh)],
                             [("dram", nm, i, hh)], (key, hh))
            if i == NTILE - 1:
                self.exchange(i, ("k",))
            if "B" in self.phase:
                self.rmsnorm(lambda c: self.hT[:, c, :], lambda c: ("hT", c),
                             lambda c: self.xn[:, c, :], lambda c: ("xn", c), "b_g", TT, stats="reuse")
                self.proj_out(self.wq_d, lambda c: self.st16[:, c, :], skh, QSCALE, "act")
                qv = self.qT_d[:, :, cols].rearrange("h e t -> e h t")
                for hh in range(2):
                    self.dma(qv[:, 8 * hh:8 * hh + 8, :], self.st16[:, 8 * hh:8 * hh + 8, :], [("stage", hh)], [],
                             ("qo", hh))
        self.exchange(NTILE - 1, ("v",))

    def exchange(self, i, which=("k", "v")):
        if self.phase != "AB" or DEBUG_NOCC:
            return
        rg = [[0, 1], [2, 3], [4, 5], [6, 7]]
        for nm, own, allb in (("k", self.k_own, self.k_all), ("v", self.v_own, self.v_all)):
            if nm not in which:
                continue
            self.P.op("pool", lambda e, own=own, allb=allb: e.collective_compute(
                "AllGather", ALU.bypass, replica_groups=rg,
                ins=[own[i].rearrange("(a b) t -> a (b t)", b=4)],
                outs=[allb[i].rearrange("(a b) t -> a (b t)", b=4)]),
                reads=[("dram", nm, i, 0), ("dram", nm, i, 1)], writes=[("dram", nm + "all")], dma="cc", inc=1)

    def phaseB1(self):
        for i in range(NTILE):
            cols = slice(i * TT, (i + 1) * TT)
            self.dma(self.hT[:, :, :], self.h1T_d[:, :, cols].rearrange("c p t -> p c t"), [("dram", "h1", i)],
                     [("hT", c) for c in range(16)], "h1i")
            self.rmsnorm(lambda c: self.hT[:, c, :], lambda c: ("hT", c),
                         lambda c: self.xn[:, c, :], lambda c: ("xn", c), "b_g", TT)
            sk = [("stage", 0), ("stage", 1)]
            self.proj_out(self.wq_d, lambda c: self.st16[:, c, :], lambda c: sk, QSCALE, "act")
            self.dma(self.qT_d[:, :, cols].rearrange("h e t -> e h t"), self.st16[:, :, :], sk, [], "qo")

    def att_loads(self, h):
        s = h % 2
        hs = slice(h * 128, (h + 1) * 128)
        for dst, prev, own, key, dk in ((self.kTs[s], self.k_prev, self.k_own, ("kTs", s), "kall"),
                                        (self.vTs[s], self.v_prev, self.v_own, ("vTs", s), "vall")):
            self.dma(dst[:, 0:NTOK].rearrange("p (i t) -> p i t", i=NTILE),
                     prev[:, hs, :].rearrange("i e t -> e i t"), [("dram", dk)], [key], ("kl", s))
            self.dma(dst[:, NTOK:2 * NTOK].rearrange("p (i t) -> p i t", i=NTILE),
                     own[:, hs, :].rearrange("i e t -> e i t"), [], [key], ("kl", s))
        self.dma(self.qTs[s][:], self.qT_d[h], [], [("qTs", s)], ("ql", s))

    def att_vtiles(self, h):
        s = h % 2
        vT = self.vTs[s]
        jobs = []
        cols = [slice(NTOK + 128 * kb, NTOK + 128 * (kb + 1)) for kb in range(-1, 16)]
        jobs.append((self.V1[s].rearrange("p t e -> p (t e)"), cols))
        cols = []
        for r in range(4):
            for kb in range(-1, 4):
                st = NTOK + r + 512 * kb
                cols.append(slice(st, st + 509, 4))
        jobs.append((self.V4[s].rearrange("p r t e -> p (r t e)"), cols))
        cols = []
        for r in range(16):
            for kb in range(-1, 1):
                st = NTOK * (kb + 1) + r
                cols.append(slice(st, st + 2033, 16))
        jobs.append((self.V16[s].rearrange("p r t e -> p (r t e)"), cols))
        n = 0
        for dest, cols in jobs:
            for c0 in range(0, len(cols), 8):
                grp = cols[c0:c0 + 8]
                b = n % 4
                n += 1
                pb = self.ps[b][:].bitcast(BF16)
                for k, cs in enumerate(grp):
                    self.tr(pb[:, 128 * k:128 * (k + 1)], vT[:, cs], self.identb[:],
                            [("vTs", s), "identb"], [("ps", b)])
                self.cp(dest[:, 128 * c0:128 * (c0 + len(grp))], pb[:, 0:128 * len(grp)], [("ps", b)],
                        [("V", s)], eng=("act" if n % 2 else "dve"))

    def att_groups(self, s):
        kT, qT = self.kTs[s], self.qTs[s]
        groups = []
        for g in range(4):
            blocks = []
            for bb in range(4):
                nb = 4 * g + bb
                blocks.append((kT[:, NTOK + 128 * (nb - 1):NTOK + 128 * nb],
                               kT[:, NTOK + 128 * nb:NTOK + 128 * (nb + 1)],
                               qT[:, 128 * nb:128 * (nb + 1)],
                               self.V1[s][:, nb, :], self.V1[s][:, nb + 1, :]))
            groups.append((self.TA if g == 0 else self.TB, blocks,
                           (lambda a, g=g: a[:, 512 * g:512 * (g + 1)]), 1))
        for r in range(4):
            blocks = []
            for nb in range(4):
                kb0 = NTOK + r + 512 * (nb - 1)
                kb1 = NTOK + r + 512 * nb
                blocks.append((kT[:, kb0:kb0 + 509:4], kT[:, kb1:kb1 + 509:4],
                               qT[:, r + 512 * nb:r + 512 * nb + 509:4],
                               self.V4[s][:, r, nb, :], self.V4[s][:, r, nb + 1, :]))
            groups.append((self.TA, blocks, (lambda a, r=r: a[:, 512 * r:512 * (r + 1)]), 4))
        for g in range(4):
            blocks = []
            for bb in range(4):
                r = 4 * g + bb
                blocks.append((kT[:, r:NTOK:16], kT[:, NTOK + r:2 * NTOK:16], qT[:, r:NTOK:16],
                               self.V16[s][:, r, 0, :], self.V16[s][:, r, 1, :]))
            groups.append((self.TD, blocks, (lambda a, g=g: a[:, 512 * g:512 * (g + 1)]), 16))
        return groups

    def att_scores(self, h, gi, grp):
        s = h % 2
        Tx, blocks, accf, d = grp
        par = gi % 2
        X, Y = self.ps[par * 2], self.ps[par * 2 + 1]
        for bb, (kp, kc, q, vp, vc) in enumerate(blocks):
            self.mm(X[:, 128 * bb:128 * (bb + 1)], kp, q, True, True, [("kTs", s), ("qTs", s)], [("ps", par * 2)])
        for bb, (kp, kc, q, vp, vc) in enumerate(blocks):
            self.mm(Y[:, 128 * bb:128 * (bb + 1)], kc, q, True, True, [("kTs", s), ("qTs", s)],
                    [("ps", par * 2 + 1)])
        slope = 2.0 ** (-8.0 * (h + 1) / H)
        for w, (T, Pb) in enumerate(((Tx, X), (self.TC, Y))):
            scb, ptb = self.sc[par][w], self.pT[par][w]
            self.stt(scb[:], T[:], -slope * d, Pb[:], ALU.mult, ALU.add, ["T", ("ps", par * 2 + w)],
                     [("sc", par, w)])
            self.actv(ptb[:], scb[:], AF.Exp, [("sc", par, w)], [("pT", par, w)])

    def att_pv(self, h, gi, grp):
        s = h % 2
        Tx, blocks, accf, d = grp
        par = gi % 2
        N_, Dn = self.ps[4 + par * 2], self.ps[5 + par * 2]
        pX, pY = self.pT[par][0], self.pT[par][1]
        for bb, (kp, kc, q, vp, vc) in enumerate(blocks):
            cs = slice(128 * bb, 128 * (bb + 1))
            self.mm(N_[:, cs], vp, pX[:, cs], True, False, [("V", s), ("pT", par, 0)], [("ps", 4 + par * 2)])
            self.mm(N_[:, cs], vc, pY[:, cs], False, True, [("V", s), ("pT", par, 1)], [("ps", 4 + par * 2)])
        self.mm(Dn[:], self.onesb[:], pX[:], True, False, ["onesb", ("pT", par, 0)], [("ps", 5 + par * 2)])
        self.mm(Dn[:], self.onesb[:], pY[:], False, True, ["onesb", ("pT", par, 1)], [("ps", 5 + par * 2)])
        br = gi // 4
        an, ad = accf(self.accn[br]), accf(self.accd[br])
        pn, pd = N_[:], Dn[:]
        self.cp(an, pn, [("ps", 4 + par * 2)], [("accn", br, gi % 4)], eng="act")
        self.cp(ad, pd, [("ps", 5 + par * 2)], [("accd", br, gi % 4)], eng=("act" if gi % 2 else "dve"))

    def att_finalize(self, h):
        s = h % 2
        nk = lambda br: [("accn", br, g) for g in range(4)]
        dk = lambda br: [("accd", br, g) for g in range(4)]
        v4 = lambda a: a[:].rearrange("p (r i) -> p i r", r=4)
        n4 = lambda a: a[:].rearrange("p (i r) -> p i r", r=4)
        v16 = lambda a: a[:].rearrange("p (r q) -> p q r", r=16)
        n16 = lambda a: a[:].rearrange("p (q r) -> p q r", r=16)
        for g in range(4):
            qs = slice(128 * g, 128 * (g + 1))
            self.tt(n4(self.nsum)[:, qs, :], n4(self.accn[0])[:, qs, :], v4(self.accn[1])[:, qs, :], ALU.add,
                    [("accn", 0, g)] + nk(1), [("nsum", g)], eng="pool")
            self.tt(n4(self.dsum)[:, qs, :], n4(self.accd[0])[:, qs, :], v4(self.accd[1])[:, qs, :], ALU.add,
                    [("accd", 0, g)] + dk(1), [("dsum", g)], eng="pool")
        sg = lambda nm: [(nm, g) for g in range(4)]
        self.tt(n16(self.dsum), n16(self.dsum), v16(self.accd[2]), ALU.add, sg("dsum") + dk(2), sg("dsum"),
                eng="pool")
        self.tt(n16(self.nsum), n16(self.nsum), v16(self.accn[2]), ALU.add, sg("nsum") + nk(2), sg("nsum"),
                eng="pool")

    def att_finalize_b(self, h, part):
        s = h % 2
        qs = slice(512 * part, 512 * (part + 1))
        self.actv(self.rden[:, qs], self.dsum[:, qs], AF.Ln, [("dsum", part)], [("rden", part)])
        self.actv(self.rden[:, qs], self.rden[:, qs], AF.Exp, [("rden", part)], [("rden", part)], scale=-1.0)
        if part == 3:
            sg = lambda nm: [(nm, g) for g in range(4)]
            ao = self.atto[s]
            self.tt(ao[:], self.nsum[:], self.rden[:], ALU.mult, sg("nsum") + sg("rden"), [("atto", s)],
                    eng="pool")
            self.dma(self.attT_d[h], ao[:], [("atto", s)], [], ("ao", s))

    def phaseB2(self):
        P = self.P
        P.fence()
        self.dma(self.TB[:], self.tc_d[:, 0, :], [], ["T"], "c2")
        self.dma(self.TC[:], self.tc_d[:, 1, :], [], ["T"], "c2")
        self.cp(self.TA[:], self.TB[:], ["T"], ["TA"])
        self.ts(self.TD[:], self.TB[:], self.vcol("pm"), None, ALU.add, None, ["T", "vecs"], ["T"])
        self.ts(self.TA[:, 0:128], self.TB[:, 0:128], self.vcol("pm"), None, ALU.add, None,
                ["T", "TA", "vecs"], ["T"])
        self.att_loads(0)
        for h in range(H):
            s = h % 2
            if h + 1 < H:
                self.att_loads(h + 1)
            groups = self.att_groups(s)
            self.att_vtiles(h)
            self.att_scores(h, 0, groups[0])
            for gi in range(len(groups)):
                if gi + 1 < len(groups):
                    self.att_scores(h, gi + 1, groups[gi + 1])
                self.att_pv(h, gi, groups[gi])
                if 4 <= gi < 8 and h > 0:
                    self.att_finalize_b(h - 1, gi - 4)
            self.att_finalize(h)
        for part in range(4):
            self.att_finalize_b(H - 1, part)
        P.fence()

    def load_att(self, i):
        cols = slice(i * TT, (i + 1) * TT)
        self.dma(self.act[:, 0:16, :], self.attT_d[:, :, cols].rearrange("h e t -> e h t"), [],
                 [("act", c) for c in range(16)], "ati")

    def phaseB3(self):
        self.load_att(0)
        for i in range(NTILE):
            cols = slice(i * TT, (i + 1) * TT)
            self.dma(self.hT[:, :, :], self.h1T_d[:, :, cols].rearrange("c p t -> p c t"), [("dram", "h1", i)],
                     [("hT", c) for c in range(16)], "h1i")
            self.proj_resid(self.wo_d, lambda c: self.act[:, c, :], lambda c: ("act", c), None)
            self.ffn(self.g1_d, self.u1_d, self.d1_d, "f_g1")
            if i + 1 < NTILE:
                self.load_att(i + 1)
            self.rmsnorm(lambda c: self.hT[:, c, :], lambda c: ("hT", c),
                         lambda c: self.v[:, c, :], lambda c: ("v", c), "fin_g", TT, stats="done")
            for tb in range(4):
                xs = self.xs[tb % 2]
                for cg in range(4):
                    b = self.bank()
                    for cc in range(4):
                        c = cg * 4 + cc
                        self.tr(self.ps[b][:, cc * 128:(cc + 1) * 128], self.v[:, c, tb * 128:(tb + 1) * 128],
                                self.ident[:], [("v", c), "ident"], [("ps", b)])
                    self.cp(xs[:, cg * 512:(cg + 1) * 512], self.ps[b][:], [("ps", b)], [("stage", tb % 2)],
                            eng=("act" if cg % 2 else "dve"))
                r0 = i * TT + tb * 128
                self.dma(self.out_d[r0:r0 + 128, :], xs, [("stage", tb % 2)], [], ("oo", tb % 2))

    def build(self):
        self.setup()
        if "A" in self.phase:
            self.phaseA()
        if "B" in self.phase:
            if "A" not in self.phase:
                self.phaseB1()
            self.phaseB2()
            self.phaseB3()
        self.P.emit()
        return self.nc


_NC_CACHE = {}


def get_nc(phase):
    if phase not in _NC_CACHE:
        _NC_CACHE[phase] = Builder(phase).build()
    return _NC_CACHE[phase]


def kernel(**inp):
    inp = {k: np.asarray(v) for k, v in inp.items()}
    x = inp["x"]
    ident, tconst = make_consts()
    ncores = 8
    c32 = lambda a: np.ascontiguousarray(a, dtype=np.float32)
    w = {"conv_w1": c32(inp["conv_w1"][0]), "conv_w2": c32(inp["conv_w2"][0]), "w_k": c32(inp["w_k"]),
         "w_v": c32(inp["w_v"]), "gate0": c32(inp["ffn_w_gate"][0]), "up0": c32(inp["ffn_w_up"][0]),
         "down0": c32(inp["ffn_w_down"][0]),
         "w_q": c32(inp["w_q"][0]), "w_o": c32(inp["w_o"][0]), "gate1": c32(inp["ffn_w_gate"][1]),
         "up1": c32(inp["ffn_w_up"][1]), "down1": c32(inp["ffn_w_down"][1]), "tconst": tconst,
         "ident": ident}
    maps = []
    for core in range(ncores):
        b, half = core // 2, core % 2
        t0 = half * NTOK
        xo = c32(x[b, t0:t0 + NTOK])
        xh = c32(x[b, t0 - HALO:t0]) if half == 1 else np.zeros((HALO, D), np.float32)
        m = {"x": xo, "xh": xh, "vecs": make_vecs(inp, half)}
        m.update(w)
        maps.append(m)
    res = run_bass_kernel_spmd(get_nc("AB"), maps, core_ids=list(range(ncores)))
    out = np.empty((4, SEQ, D), np.float32)
    for core in range(ncores):
        b, half = core // 2, core % 2
        out[b, half * NTOK:(half + 1) * NTOK] = res.results[core]["out"]
    return out
```
